# Optimizing a Trainium2 kernel written in Bass

```python
import math
import jax
import jax.numpy as jnp
from jax import lax
import numpy as np

D_MODEL = 4096
BATCH = 4
SEQ = 4096
DEPTH = 2

GRID_W = 64
CTX_LEN = 256
MIX_W = D_MODEL
HALF_W = MIX_W // 2
Q_BLOCK = 128
ROPE_THETA = 10000.0
NEG_INF = -1e30
A_HEAD = 64
A_HEADS = HALF_W // A_HEAD
A_LORA_W = 96
A_LORA_A = 96
A_LORA_G = 256
RWKV_GN_EPS = 64e-5
COLS_A = 3 * HALF_W + 2 * A_LORA_W + 2 * A_LORA_A + A_LORA_G
B_DK = 64
B_DV = 2 * B_DK
B_HEADS = HALF_W // B_DV
COLS_B = 3 * HALF_W
C_HEAD = 64
C_HEADS = HALF_W // C_HEAD
C_KV_HEADS = C_HEADS // 8
C_GROUP = C_HEADS // C_KV_HEADS
WINDOW = 128
COLS_C = HALF_W + 2 * C_KV_HEADS * C_HEAD
D_DK = 128
D_DV = 256
D_HEADS = HALF_W // D_DV
RET_CHUNK = 128
COLS_D = 2 * D_HEADS * D_DK + 2 * HALF_W
COLS_EVEN = COLS_A + COLS_B
COLS_ODD = COLS_C + COLS_D
D_FF = ((8 * D_MODEL // 3 + 255) // 256) * 256
N_EVEN = (DEPTH + 1) // 2
N_ODD = DEPTH // 2

kernel_name = 'hybrid_rwkv7_diffattn_swa_retention_dit'


def _split(t, sizes):
    return jnp.split(t, [int(s) for s in np.cumsum(sizes)[:-1]], axis=-1)


def rms_norm(x, eps=1e-6):
    xf = x.astype(jnp.float32)
    return (xf * lax.rsqrt(jnp.mean(xf * xf, axis=-1, keepdims=True) + eps)).astype(x.dtype)


def centred_shift(z):
    zp = jnp.pad(z, ((0, 0), (1, 1), (0, 0)))
    return 0.5 * (zp[:, :-2] + zp[:, 2:])


def dwconv3(z, w, b):
    zp = jnp.pad(z, ((0, 0), (1, 1), (0, 0)))
    return zp[:, :-2] * w[0] + zp[:, 1:-1] * w[1] + zp[:, 2:] * w[2] + b


def axial_rope(n, dim):
    rows = n // GRID_W
    row = jnp.repeat(jnp.arange(rows, dtype=jnp.float32), GRID_W)
    col = jnp.tile(jnp.arange(GRID_W, dtype=jnp.float32), rows)
    quarter = dim // 4
    inv = ROPE_THETA ** (-jnp.arange(quarter, dtype=jnp.float32) / quarter)
    ang = jnp.concatenate([row[:, None] * inv, col[:, None] * inv], axis=-1)
    return jnp.cos(ang), jnp.sin(ang)


def seq_rope(pos, dim):
    inv = ROPE_THETA ** (-jnp.linspace(0.0, 1.0, dim // 2, dtype=jnp.float32))
    ang = pos[:, None] * inv
    return jnp.cos(ang), jnp.sin(ang)


def apply_rope(x, cos, sin):
    x1, x2 = jnp.split(x, 2, axis=-1)
    return jnp.concatenate([x1 * cos - x2 * sin, x2 * cos + x1 * sin], axis=-1).astype(x.dtype)


def rwkv7_prep(pa, ap):
    B, n, _ = pa.shape
    pa = pa.astype(jnp.float32)
    pa = pa + ap['mu'] * (centred_shift(pa) - pa)
    r, k, v, w1f, w1b, a1f, a1b, g1 = _split(
        pa, [HALF_W] * 3 + [A_LORA_W] * 2 + [A_LORA_A] * 2 + [A_LORA_G])
    heads = lambda t: t.reshape(B, n, A_HEADS, A_HEAD)
    kk = heads(k * ap['k_k'])
    kk = kk * lax.rsqrt(jnp.sum(kk * kk, axis=-1, keepdims=True) + 1e-12)
    dirs = []
    for d, (w1, a1) in enumerate(((w1f, a1f), (w1b, a1b))):
        w = ap['w0'][d] + jnp.tanh(w1) @ ap['w2'][d]
        decay = jnp.exp(-jnp.exp(-jax.nn.softplus(-w) - 0.5))
        a = jax.nn.sigmoid(ap['a0'][d] + a1 @ ap['a2'][d])
        kd = k * (1.0 + (a - 1.0) * ap['k_a'])
        dirs.append((heads(decay), heads(kd), heads(a)))
    return {'r': heads(r), 'v': heads(v), 'kk': kk, 'g1': g1, 'dirs': dirs}


def rwkv7_scan(S0, decay, k, v, kk, a, r, reverse):
    emit = r is not None
    tm = lambda t: jnp.moveaxis(t, 1, 0)
    xs = (tm(decay), tm(k), tm(v), tm(kk), tm(a)) + ((tm(r),) if emit else ())

    def step(S, inp):
        w_t, k_t, v_t, kk_t, a_t = inp[:5]
        S = (S * w_t[:, :, None, :]
             - jnp.einsum('bhvk,bhk->bhv', S, kk_t)[..., None] * (kk_t * a_t)[:, :, None, :]
             + v_t[..., None] * k_t[:, :, None, :])
        y = jnp.einsum('bhvk,bhk->bhv', S, inp[5]) if emit else None
        return S, y

    S, ys = lax.scan(step, S0, xs, reverse=reverse)
    return S, (jnp.moveaxis(ys, 0, 1) if emit else None)


def rwkv7_out(y, st, ap):
    B, n = y.shape[:2]
    mu = jnp.mean(y, axis=-1, keepdims=True)
    var = jnp.mean(jnp.square(y - mu), axis=-1, keepdims=True)
    y = ((y - mu) * lax.rsqrt(var + RWKV_GN_EPS)).reshape(B, n, HALF_W) * ap['ln_w'] + ap['ln_b']
    k_sum = st['dirs'][0][1] + st['dirs'][1][1]
    bonus = jnp.sum(st['r'] * k_sum * ap['r_k'], axis=-1, keepdims=True) * st['v']
    g = jax.nn.sigmoid(st['g1']) @ ap['g2']
    return (y + bonus.reshape(B, n, HALF_W)) * g


def rwkv7_mix(pa_lat, pa_ctx, ap, need_ctx):
    lat, ctx = rwkv7_prep(pa_lat, ap), rwkv7_prep(pa_ctx, ap)
    S0 = jnp.zeros((pa_lat.shape[0], A_HEADS, A_HEAD, A_HEAD), jnp.float32)
    ys_lat, ys_ctx = [], []
    for d, rev in enumerate((False, True)):
        dec_c, k_c, a_c = ctx['dirs'][d]
        S_c, y_c = rwkv7_scan(S0, dec_c, k_c, ctx['v'], ctx['kk'], a_c,
                              ctx['r'] if need_ctx else None, rev)
        dec_l, k_l, a_l = lat['dirs'][d]
        _, y_l = rwkv7_scan(S_c, dec_l, k_l, lat['v'], lat['kk'], a_l, lat['r'], rev)
        ys_lat.append(y_l)
        ys_ctx.append(y_c)
    out_lat = rwkv7_out(ys_lat[0] + ys_lat[1], lat, ap)
    out_ctx = rwkv7_out(ys_ctx[0] + ys_ctx[1], ctx, ap) if need_ctx else None
    return out_lat, out_ctx


def diff_attend(q, k, v, lam):
    s = jnp.einsum('bqhcd,bkhcd->bhcqk', q, k).astype(jnp.float32) * (B_DK ** -0.5)
    p = jax.nn.softmax(s, axis=-1)
    a = p[:, :, 0] - lam * p[:, :, 1]
    return jnp.einsum('bhqk,bkhd->bqhd', a.astype(v.dtype), v)


def diff_attn_mix(pb_lat, pb_ctx, lam_vecs, subln, layer_idx, cos, sin, need_ctx):
    B, S, _ = pb_lat.shape

    def qkv(p):
        n = p.shape[1]
        q, k, v = _split(p, [HALF_W] * 3)
        return (q.reshape(B, n, B_HEADS, 2, B_DK), k.reshape(B, n, B_HEADS, 2, B_DK),
                v.reshape(B, n, B_HEADS, B_DV))

    ql, kl, vl = qkv(pb_lat)
    qc, kc, vc = qkv(pb_ctx)
    cs, sn = cos[:, None, None, :], sin[:, None, None, :]
    ql, kl = apply_rope(ql, cs, sn), apply_rope(kl, cs, sn)
    lam_init = 0.8 - 0.6 * math.exp(-0.3 * layer_idx)
    lv = lam_vecs.astype(jnp.float32)
    lam = jnp.exp(jnp.sum(lv[0] * lv[1])) - jnp.exp(jnp.sum(lv[2] * lv[3])) + lam_init
    k_all = jnp.concatenate([kl, kc], axis=1)
    v_all = jnp.concatenate([vl, vc], axis=1)
    nb = S // Q_BLOCK
    qb = jnp.moveaxis(ql.reshape(B, nb, Q_BLOCK, B_HEADS, 2, B_DK), 1, 0)
    yl = lax.map(lambda q: diff_attend(q, k_all, v_all, lam), qb)
    yl = jnp.moveaxis(yl, 0, 1).reshape(B, S, B_HEADS, B_DV)
    post = lambda y: (rms_norm(y, 1e-5) * subln * (1.0 - lam_init)).reshape(B, y.shape[1], HALF_W)
    return post(yl), (post(diff_attend(qc, kc, vc, lam)) if need_ctx else None)


def window_mix(pc_lat, pc_ctx, sink, cos, sin, need_ctx):
    B, S, _ = pc_lat.shape

    def qkv(p):
        n = p.shape[1]
        q, k, v = _split(p, [HALF_W, C_KV_HEADS * C_HEAD, C_KV_HEADS * C_HEAD])
        return (q.reshape(B, n, C_KV_HEADS, C_GROUP, C_HEAD), k.reshape(B, n, C_KV_HEADS, C_HEAD),
                v.reshape(B, n, C_KV_HEADS, C_HEAD))

    ql, kl, vl = qkv(pc_lat)
    qc, kc, vc = qkv(pc_ctx)
    ql = apply_rope(ql, cos[:, None, None, :], sin[:, None, None, :])
    kl = apply_rope(kl, cos[:, None, :], sin[:, None, :])
    scale = C_HEAD ** -0.5
    sink_f = sink.astype(jnp.float32).reshape(C_KV_HEADS, C_GROUP)[None, :, :, None, None]

    def sink_softmax_mix(scores, values):
        s = jnp.concatenate(scores + [jnp.broadcast_to(sink_f, scores[0].shape[:-1] + (1,))], axis=-1)
        p = jax.nn.softmax(s, axis=-1)
        out, start = 0.0, 0
        for sc, val in zip(scores, values):
            m = sc.shape[-1]
            out = out + jnp.einsum('bgrqk,bkgd->bqgrd', p[..., start:start + m].astype(val.dtype), val)
            start += m
        return out

    nb = S // WINDOW

    def band(t):
        tp = jnp.pad(t, ((0, 0), (WINDOW, WINDOW), (0, 0), (0, 0))).reshape(B, nb + 2, WINDOW, C_KV_HEADS, C_HEAD)
        return jnp.moveaxis(jnp.concatenate([tp[:, :-2], tp[:, 1:-1], tp[:, 2:]], axis=2), 1, 0)

    qi = jnp.arange(WINDOW)[:, None]
    kj = jnp.arange(3 * WINDOW)[None, :] - WINDOW
    key_pos = (jnp.arange(nb) * WINDOW)[:, None, None] + kj[None]
    valid = (jnp.abs(kj - qi) <= WINDOW)[None] & (key_pos >= 0) & (key_pos < S)
    qb = jnp.moveaxis(ql.reshape(B, nb, WINDOW, C_KV_HEADS, C_GROUP, C_HEAD), 1, 0)

    def block(args):
        q, kb, vb, ok = args
        s_win = jnp.einsum('bqgrd,bkgd->bgrqk', q, kb).astype(jnp.float32) * scale
        s_win = jnp.where(ok, s_win, NEG_INF)
        s_ctx = jnp.einsum('bqgrd,bkgd->bgrqk', q, kc).astype(jnp.float32) * scale
        return sink_softmax_mix([s_win, s_ctx], [vb, vc])

    y_lat = jnp.moveaxis(lax.map(block, (qb, band(kl), band(vl), valid)), 0, 1).reshape(B, S, HALF_W)
    y_ctx = None
    if need_ctx:
        s = jnp.einsum('bqgrd,bkgd->bgrqk', qc, kc).astype(jnp.float32) * scale
        y_ctx = sink_softmax_mix([s], [vc]).reshape(B, qc.shape[1], HALF_W)
    return y_lat, y_ctx


def retention_scan(q, k, v, log_gamma, S0, reverse):
    emit = q is not None
    B, n = k.shape[:2]
    nc = n // RET_CHUNK
    flip = (lambda t: jnp.flip(t, 1)) if reverse else (lambda t: t)
    chunks = lambda t: jnp.moveaxis(flip(t).reshape(B, nc, RET_CHUNK, *t.shape[2:]), 1, 0)
    idx = jnp.arange(RET_CHUNK, dtype=jnp.float32)
    lg = log_gamma[:, None]
    rel = idx[:, None] - idx[None, :]
    decay_mat = jnp.where(rel >= 0, jnp.exp(lg[:, :, None] * jnp.maximum(rel, 0.0)), 0.0)
    q_dec = jnp.exp(lg * (idx + 1.0)).T
    k_dec = jnp.exp(lg * (RET_CHUNK - 1.0 - idx)).T
    chunk_dec = jnp.exp(log_gamma * RET_CHUNK)
    xs = (chunks(k), chunks(v)) + ((chunks(q),) if emit else ())

    def step(S, inp):
        kc, vc = inp[0], inp[1]
        S_new = S * chunk_dec[None, :, None, None] + jnp.einsum('bchd,bche->bhde', kc * k_dec[None, :, :, None], vc)
        if not emit:
            return S_new, None
        qc = inp[2]
        att = jnp.einsum('bqhd,bkhd->bhqk', qc, kc) * decay_mat[None]
        y = (jnp.einsum('bhqk,bkhe->bqhe', att, vc)
             + jnp.einsum('bqhd,bhde->bqhe', qc * q_dec[None, :, :, None], S))
        return S_new, y

    S, ys = lax.scan(step, S0, xs)
    if not emit:
        return S, None
    ys = jnp.moveaxis(ys, 0, 1).reshape(B, n, D_HEADS, D_DV)
    return S, flip(ys)


def retention_mix(pd_lat, pd_ctx, decay_e, rope_lat, rope_ctx, need_ctx):
    B = pd_lat.shape[0]

    def prep(p, rope):
        n = p.shape[1]
        cs, sn = rope[0][:, None, :], rope[1][:, None, :]
        q, k, v, g = _split(p.astype(jnp.float32), [D_HEADS * D_DK] * 2 + [HALF_W] * 2)
        q = apply_rope(q.reshape(B, n, D_HEADS, D_DK), cs, sn) * (D_DK ** -0.5)
        k = apply_rope(k.reshape(B, n, D_HEADS, D_DK), cs, sn)
        return q, k, v.reshape(B, n, D_HEADS, D_DV), g

    ql, kl, vl, gl = prep(pd_lat, rope_lat)
    qc, kc, vc, gc = prep(pd_ctx, rope_ctx)
    S0 = jnp.zeros((B, D_HEADS, D_DK, D_DV), jnp.float32)
    ys_lat, ys_ctx = [], []
    for d, rev in enumerate((False, True)):
        log_gamma = jnp.log1p(-jnp.exp2(-decay_e[d].astype(jnp.float32)))
        S_c, y_c = retention_scan(qc if need_ctx else None, kc, vc, log_gamma, S0, rev)
        _, y_l = retention_scan(ql, kl, vl, log_gamma, S_c, rev)
        ys_lat.append(y_l)
        ys_ctx.append(y_c)
    post = lambda y, g: rms_norm(y).reshape(B, y.shape[1], HALF_W) * jax.nn.silu(g)
    y_ctx = post(ys_ctx[0] + ys_ctx[1], gc) if need_ctx else None
    return post(ys_lat[0] + ys_lat[1], gl), y_ctx


def conv_glu(h, w_in, conv_w, conv_b, w_out):
    gate, up = jnp.split(h @ w_in, 2, axis=-1)
    return (jax.nn.silu(dwconv3(gate, conv_w, conv_b)) * up) @ w_out


def setup_inputs(seed: int = 0):
    key = jax.random.key(seed)
    keys = iter(jax.random.split(key, 32))

    def nrm(shape, scale):
        return jax.random.normal(next(keys), shape, jnp.float32) * scale

    def unif(shape, lo, hi):
        return jax.random.uniform(next(keys), shape, jnp.float32, lo, hi)

    D = D_MODEL
    return {
        'x': nrm((BATCH, SEQ, D), 1.0),
        'c': nrm((BATCH, D), 1.0),
        'ctx': nrm((BATCH, CTX_LEN, D), 1.0),
        'c_ctx': nrm((D,), 1.0),
        'w_mod': nrm((DEPTH, D, 6 * D), D ** -0.5),
        'b_mod': nrm((DEPTH, 6 * D), 0.02),
        'w_in_even': nrm((N_EVEN, D, COLS_EVEN), D ** -0.5),
        'rwkv_mu': unif((N_EVEN, COLS_A), 0.0, 1.0),
        'rwkv_w0': unif((N_EVEN, 2, HALF_W), -6.0, -1.0),
        'rwkv_w2': nrm((N_EVEN, 2, A_LORA_W, HALF_W), 0.5 * A_LORA_W ** -0.5),
        'rwkv_a0': nrm((N_EVEN, 2, HALF_W), 0.1),
        'rwkv_a2': nrm((N_EVEN, 2, A_LORA_A, HALF_W), 0.5 * A_LORA_A ** -0.5),
        'rwkv_g2': nrm((N_EVEN, A_LORA_G, HALF_W), A_LORA_G ** -0.5),
        'rwkv_k_k': 0.85 + nrm((N_EVEN, HALF_W), 0.05),
        'rwkv_k_a': 1.0 + nrm((N_EVEN, HALF_W), 0.05),
        'rwkv_r_k': nrm((N_EVEN, A_HEADS, A_HEAD), 0.1),
        'rwkv_ln_w': 1.0 + nrm((N_EVEN, HALF_W), 0.02),
        'rwkv_ln_b': nrm((N_EVEN, HALF_W), 0.02),
        'diff_lambda': nrm((N_EVEN, 4, B_DK), 0.1),
        'diff_subln': 1.0 + nrm((N_EVEN, B_DV), 0.02),
        'w_in_odd': nrm((N_ODD, D, COLS_ODD), D ** -0.5),
        'swa_sink': nrm((N_ODD, C_HEADS), 0.5),
        'ret_decay': 5.0 + jnp.arange(D_HEADS, dtype=jnp.float32) + nrm((N_ODD, 2, D_HEADS), 0.1),
        'w_mix_out': nrm((DEPTH, MIX_W, D), MIX_W ** -0.5),
        'w_ffn_in': nrm((DEPTH, D, 2 * D_FF), D ** -0.5),
        'ffn_conv_w': nrm((DEPTH, 3, D_FF), 3 ** -0.5),
        'ffn_conv_b': nrm((DEPTH, D_FF), 0.02),
        'w_ffn_out': nrm((DEPTH, D_FF, D), D_FF ** -0.5),
        'final_norm': 1.0 + nrm((D,), 0.02),
    }


def reference(x, c, ctx, c_ctx, w_mod, b_mod, w_in_even, rwkv_mu, rwkv_w0, rwkv_w2, rwkv_a0, rwkv_a2,
              rwkv_g2, rwkv_k_k, rwkv_k_a, rwkv_r_k, rwkv_ln_w, rwkv_ln_b, diff_lambda, diff_subln,
              w_in_odd, swa_sink, ret_decay, w_mix_out, w_ffn_in, ffn_conv_w, ffn_conv_b, w_ffn_out,
              final_norm):
    S = x.shape[1]
    L = ctx.shape[1]
    cos_b, sin_b = axial_rope(S, B_DK)
    cos_c, sin_c = axial_rope(S, C_HEAD)
    rope_ret_ctx = seq_rope(jnp.arange(L, dtype=jnp.float32), D_DK)
    rope_ret_lat = seq_rope(L + jnp.arange(S, dtype=jnp.float32), D_DK)
    silu_c = jax.nn.silu(c)
    silu_cc = jax.nn.silu(c_ctx)[None]
    h_lat, h_ctx = x, ctx
    for l in range(DEPTH):
        need_ctx = l < DEPTH - 1
        m_lat = jnp.split((silu_c @ w_mod[l] + b_mod[l])[:, None, :], 6, axis=-1)
        m_ctx = jnp.split((silu_cc @ w_mod[l] + b_mod[l])[:, None, :], 6, axis=-1)
        a_lat = rms_norm(h_lat) * (1.0 + m_lat[1]) + m_lat[0]
        a_ctx = rms_norm(h_ctx) * (1.0 + m_ctx[1]) + m_ctx[0]
        if l % 2 == 0:
            e = l // 2
            p_lat, p_ctx = a_lat @ w_in_even[e], a_ctx @ w_in_even[e]
            ap = dict(mu=rwkv_mu[e], w0=rwkv_w0[e], w2=rwkv_w2[e], a0=rwkv_a0[e], a2=rwkv_a2[e],
                      g2=rwkv_g2[e], k_k=rwkv_k_k[e], k_a=rwkv_k_a[e], r_k=rwkv_r_k[e],
                      ln_w=rwkv_ln_w[e], ln_b=rwkv_ln_b[e])
            y1_l, y1_c = rwkv7_mix(p_lat[..., :COLS_A], p_ctx[..., :COLS_A], ap, need_ctx)
            y2_l, y2_c = diff_attn_mix(p_lat[..., COLS_A:], p_ctx[..., COLS_A:], diff_lambda[e],
                                       diff_subln[e], l, cos_b, sin_b, need_ctx)
        else:
            o = l // 2
            p_lat, p_ctx = a_lat @ w_in_odd[o], a_ctx @ w_in_odd[o]
            y1_l, y1_c = window_mix(p_lat[..., :COLS_C], p_ctx[..., :COLS_C], swa_sink[o],
                                    cos_c, sin_c, need_ctx)
            y2_l, y2_c = retention_mix(p_lat[..., COLS_C:], p_ctx[..., COLS_C:], ret_decay[o],
                                       rope_ret_lat, rope_ret_ctx, need_ctx)
        h_lat = h_lat + m_lat[2] * (jnp.concatenate([y1_l, y2_l], axis=-1) @ w_mix_out[l])
        f_lat = rms_norm(h_lat) * (1.0 + m_lat[4]) + m_lat[3]
        h_lat = h_lat + m_lat[5] * conv_glu(f_lat, w_ffn_in[l], ffn_conv_w[l], ffn_conv_b[l], w_ffn_out[l])
        if need_ctx:
            h_ctx = h_ctx + m_ctx[2] * (jnp.concatenate([y1_c, y2_c], axis=-1) @ w_mix_out[l])
            f_ctx = rms_norm(h_ctx) * (1.0 + m_ctx[4]) + m_ctx[3]
            h_ctx = h_ctx + m_ctx[5] * conv_glu(f_ctx, w_ffn_in[l], ffn_conv_w[l], ffn_conv_b[l], w_ffn_out[l])
    return rms_norm(h_lat) * final_norm
```

```python
import math
from contextlib import ExitStack
import numpy as np
import concourse.bass as bass
import concourse.mybir as mybir
from concourse.bass_utils import run_bass_kernel_spmd

F32 = mybir.dt.float32
BF16 = mybir.dt.bfloat16
AF = mybir.ActivationFunctionType
ALU = mybir.AluOpType
AX = mybir.AxisListType


class Cfg:
    def __init__(self, D=4096, S=4096, L=256, B=4, DEPTH=2):
        self.D, self.S, self.L, self.B, self.DEPTH = D, S, L, B, DEPTH
        self.HALF = D // 2
        self.NTOK = S + L
        self.NT = self.NTOK // 128
        self.KC = D // 128
        self.A_HEADS = self.HALF // 64
        self.COLS_A = 3 * self.HALF + 4 * 96 + 256
        self.B_HEADS = self.HALF // 128
        self.COLS_B = 3 * self.HALF
        self.C_HEADS = self.HALF // 64
        self.C_KV = self.C_HEADS // 8
        self.COLS_C = self.HALF + 2 * self.C_KV * 64
        self.D_HEADS = self.HALF // 256
        self.COLS_D = 2 * self.D_HEADS * 128 + 2 * self.HALF
        self.COLS_EVEN = self.COLS_A + self.COLS_B
        self.COLS_ODD = self.COLS_C + self.COLS_D
        self.DFF = ((8 * D // 3 + 255) // 256) * 256
        self.FC = self.DFF // 128


class Prog:
    CENG = ("pe", "dve", "act", "pool")
    NDS = 8

    def __init__(self, nc, es):
        self.nc = nc
        self.es = es
        self.q = {e: [] for e in ("pe", "dve", "act", "pool", "sp")}
        self.csem = {e: es.enter_context(nc.semaphore(f"c_{e}")) for e in self.CENG}
        self.dsem = {e: [es.enter_context(nc.semaphore(f"d_{e}{i}")) for i in range(self.NDS)]
                     for e in ("sp", "pool")}
        self.dcnt = {}
        self.drr = {"sp": 0, "pool": 0}
        self.known = {e: {} for e in self.q}
        self.bufs = {}
        self.waited = {e: set() for e in self.CENG}
        self.nins = {e: 0 for e in self.CENG}
        self.cbase = {e: 0 for e in self.CENG}

    def _st(self, b):
        s = self.bufs.get(b)
        if s is None:
            s = self.bufs[b] = {"w": None, "r": {}}
        return s

    def op(self, eng, fn, reads=(), writes=(), dma=False):
        deps = {}

        def add(tok):
            if tok is None:
                return
            k, v = tok
            if eng == "pe" and k == ("c", "pe"):
                return
            if deps.get(k, 0) < v:
                deps[k] = v

        for b in reads:
            add(self._st(b)["w"])
        for b in writes:
            s = self._st(b)
            add(s["w"])
            for k, v in s["r"].items():
                add((k, v))
        if dma:
            i = self.drr[eng]
            self.drr[eng] = (i + 1) % self.NDS
            key = ("d", eng, i)
            prev = self.dcnt.get(key, 0)
            if prev:
                add((key, prev))
            self.dcnt[key] = prev + 1
            tok = (key, prev + 1)
        else:
            self.nins[eng] += 1
            tok = (("c", eng), self.nins[eng])
        waits = []
        kn = self.known[eng]
        for k, v in deps.items():
            if kn.get(k, 0) < v:
                kn[k] = v
                waits.append((k, v))
                if k[0] == "c":
                    self.waited[k[1]].add(v)
        self.q[eng].append((waits, fn, tok))
        for b in writes:
            s = self._st(b)
            s["w"] = tok
            s["r"] = {}
        for b in reads:
            if b in writes:
                continue
            s = self._st(b)
            if s["r"].get(tok[0], 0) < tok[1]:
                s["r"][tok[0]] = tok[1]
        return tok

    def wait_bufs(self, eng, bufs):
        deps = {}
        for b in bufs:
            t = self._st(b)["w"]
            if t is not None and deps.get(t[0], 0) < t[1]:
                deps[t[0]] = t[1]
        waits = []
        for k, v in deps.items():
            if self.known[eng].get(k, 0) < v:
                self.known[eng][k] = v
                waits.append((k, v))
                if k[0] == "c":
                    self.waited[k[1]].add(v)
        self.q[eng].append((waits, None, None))

    def emit(self):
        nc = self.nc
        rank = {}
        for e in self.CENG:
            rank[e] = {v: self.cbase[e] + i + 1 for i, v in enumerate(sorted(self.waited[e]))}

        def replay(ename, eobj):
            for waits, fn, tok in self.q[ename]:
                for k, v in waits:
                    if k[0] == "c":
                        eobj.wait_ge(self.csem[k[1]], rank[k[1]][v])
                    else:
                        eobj.wait_ge(self.dsem[k[1]][k[2]], 16 * v)
                if fn is None:
                    continue
                ins = fn(eobj)
                if tok[0][0] == "c":
                    if tok[1] in rank[ename]:
                        ins.then_inc(self.csem[ename], 1)
                else:
                    ins.then_inc(self.dsem[tok[0][1]][tok[0][2]], 16)

        with nc.Block() as block:
            @block.tensor
            def _(e):
                replay("pe", e)

            @block.vector
            def _(e):
                replay("dve", e)

            @block.scalar
            def _(e):
                replay("act", e)

            @block.gpsimd
            def _(e):
                replay("pool", e)

            @block.sync
            def _(e):
                replay("sp", e)
        for e in self.CENG:
            self.cbase[e] += len(self.waited[e])
            self.waited[e] = set()
        for e in self.q:
            self.q[e] = []
        self.bufs = {}

    def flush(self):
        waits = []
        for key, cnt in self.dcnt.items():
            if self.known["sp"].get(key, 0) < cnt:
                self.known["sp"][key] = cnt
                waits.append((key, cnt))
        self.q["sp"].append((waits, None, None))

    def dma(self, out, in_, reads, writes, eng="sp", **kw):
        return self.op(eng, lambda e: e.dma_start(out=out, in_=in_, **kw), reads, writes, dma=True)

    def mm(self, out, lhsT, rhs, start, stop, reads, writes):
        return self.op("pe", lambda e: e.matmul(out, lhsT, rhs, start=start, stop=stop), reads, writes)

    def tr(self, out, in_, ident, reads, writes):
        return self.op("pe", lambda e: e.transpose(out, in_, ident), reads, writes)

    _uid = [0]

    def sb(self, name, shape, dt):
        self._uid[0] += 1
        return self.es.enter_context(self.nc.sbuf_tensor(f"s{self._uid[0]}_{name}", list(shape), dt))

    def ps(self, name, shape, dt=F32):
        self._uid[0] += 1
        return self.es.enter_context(self.nc.psum_tensor(f"p{self._uid[0]}_{name}", list(shape), dt))


def _divisor_le(n, cap):
    for d in range(min(n, cap), 0, -1):
        if n % d == 0:
            return d
    return 1


class Builder:
    def __init__(self, cfg, nlayers=None):
        self.c = cfg
        self.nc = bass.Bass("TRN2", target_bir_lowering=False)
        self.es = ExitStack()
        self.P = Prog(self.nc, self.es)
        self.dram = {}
        self.evac_rr = 0

    def din(self, name, shape, dt=F32):
        t = self.nc.dram_tensor(name, list(shape), dt, kind="ExternalInput").ap()
        self.dram[name] = t
        return t

    def dout(self, name, shape, dt=F32):
        t = self.nc.dram_tensor(name, list(shape), dt, kind="ExternalOutput").ap()
        self.dram[name] = t
        return t

    def dscr(self, name, shape, dt=F32):
        t = self.nc.dram_tensor(name, list(shape), dt, kind="Internal").ap()
        self.dram[name] = t
        return t

    def run_stage(self, fn, *a, **kw):
        outer = self.P.es
        with ExitStack() as st:
            self.P.es = st
            fn(*a, **kw)
            self.P.flush()
            self.P.emit()
        self.P.es = outer

    def evac_eng(self):
        self.evac_rr ^= 1
        return "dve" if self.evac_rr else "act"

    def copy(self, eng, out, in_, reads, writes):
        if eng == "act":
            return self.P.op("act", lambda e: e.copy(out=out, in_=in_), reads, writes)
        return self.P.op(eng, lambda e: e.tensor_copy(out=out, in_=in_), reads, writes)

    def st_init_h(self):
        c, P, d = self.c, self.P, self.dram
        P.dma(d["h"][0:c.L, :], d["ctx"][:, :], ["ctx"], ["h"])
        step = min(1024, c.S)
        for r in range(0, c.S, step):
            P.dma(d["h"][c.L + r:c.L + r + step, :], d["x"][r:r + step, :], ["x"], [("h", r)])

    def groups(self):
        c = self.c
        g = []
        nctx = c.L // 128
        for i in range(0, nctx, 4):
            g.append((1, list(range(i, min(nctx, i + 4)))))
        for i in range(nctx, c.NT, 4):
            g.append((0, list(range(i, min(c.NT, i + 4)))))
        return g

    def load_modT(self, l):
        c, P = self.c, self.P
        modT = P.sb("modT", [128, 2, 6, c.KC], F32)
        for cls in range(2):
            for j in range(6):
                P.dma(modT[:, cls, j, :], self.dram["mod"][l, cls, j, :].rearrange("(k p) -> p k", p=128),
                      ["mod"], ["modT"], allow_slow_non_contiguous=True)
        return modT

    def st_norm(self, l, which, final=False):
        c, P, d = self.c, self.P, self.dram
        ident = P.sb("ident", [128, 128], F32)
        P.dma(ident[:], d["ident"][:, :], ["identd"], ["ident"])
        modT = self.load_modT(l)
        onep = P.sb("onep", [128, 2, c.KC], F32)
        js, jb = (1, 0) if which == 0 else (4, 3)
        P.op("dve", lambda e: e.tensor_scalar(out=onep[:], in0=modT[:, :, js, :], scalar1=1.0, scalar2=None,
                                              op0=ALU.add), ["modT"], ["onep"])
        hs = [P.sb(f"hs{i}", [128, c.D], F32) for i in range(4)]
        junk = P.sb("junk", [128, c.D], BF16)
        ss = P.sb("ss", [128, 4], F32)
        rs = P.sb("rs", [128, 4], F32)
        aTg = [P.sb(f"aTg{i}", [128, c.KC, 512], BF16) for i in range(1)]
        pt = [P.ps(f"pt{i}", [128, 512]) for i in range(4)]
        pti = 0
        for gi, (cls, tiles) in enumerate(self.groups()):
            n = len(tiles)
            for j, ti in enumerate(tiles):
                P.dma(hs[j][:], d["h"][ti * 128:(ti + 1) * 128, :], [("h", ti)], [("hs", j)])
                P.op("dve", lambda e, j=j: e.memset(ss[:, j:j + 1], 0.0), [], [("ss", j)])
                P.op("act", lambda e, j=j: e.activation(out=junk[:], in_=hs[j][:], func=AF.Square,
                                                         accum_out=ss[:, j:j + 1]),
                     [("hs", j), ("ss", j)], ["junk", ("ss", j)])
                P.op("dve", lambda e, j=j: e.tensor_scalar(out=rs[:, j:j + 1], in0=ss[:, j:j + 1],
                                                           scalar1=1.0 / c.D, scalar2=1e-6,
                                                           op0=ALU.mult, op1=ALU.add),
                     [("ss", j)], [("rs", j)])
                P.op("act", lambda e, j=j: e.activation(out=rs[:, j:j + 1], in_=rs[:, j:j + 1], func=AF.Sqrt),
                     [("rs", j)], [("rs", j)])
                P.op("dve", lambda e, j=j: e.reciprocal(out=rs[:, j:j + 1], in_=rs[:, j:j + 1]),
                     [("rs", j)], [("rs", j)])
                P.op("act", lambda e, j=j: e.activation(out=hs[j][:], in_=hs[j][:], func=AF.Copy,
                                                         scale=rs[:, j:j + 1]),
                     [("hs", j), ("rs", j)], [("hs", j)])
            ab = aTg[0]
            for kc in range(c.KC):
                pb = pt[pti % 4]
                pbn = ("pt", pti % 4)
                pti += 1
                for j in range(n):
                    P.tr(pb[:, j * 128:(j + 1) * 128], hs[j][:, kc * 128:(kc + 1) * 128], ident[:],
                         [("hs", j), "ident"], [pbn])
                eng = self.evac_eng()
                if eng == "dve":
                    P.op("dve", lambda e, pb=pb, kc=kc, cls=cls, n=n: e.tensor_scalar(
                        out=ab[:, kc, 0:n * 128], in0=pb[:, 0:n * 128], scalar1=onep[:, cls, kc:kc + 1],
                        scalar2=modT[:, cls, jb, kc:kc + 1], op0=ALU.mult, op1=ALU.add),
                        [pbn, "onep", "modT"], [("aTg", kc)])
                else:
                    P.op("act", lambda e, pb=pb, kc=kc, cls=cls, n=n: e.activation(
                        out=ab[:, kc, 0:n * 128], in_=pb[:, 0:n * 128], func=AF.Identity,
                        scale=onep[:, cls, kc:kc + 1], bias=modT[:, cls, jb, kc:kc + 1]),
                        [pbn, "onep", "modT"], [("aTg", kc)])
            t0 = tiles[0] * 128
            P.dma(d["aT"][:, :, t0:t0 + n * 128].rearrange("k p t -> p k t"), ab[:, :, 0:n * 128],
                  [("aTg", kc) for kc in range(c.KC)], [("aT", ti) for ti in tiles])

    def dense_tok(self, srcT, kcn, W, col0, ncols, sink, wname, src_key, cw_max=512):
        c, P = self.c, self.P
        wsb = [P.sb(f"wsb{i}", [128, kcn, cw_max], BF16) for i in range(2 if kcn <= 40 else 1)]
        TG = 4 if kcn <= 40 else 1
        asb = [P.sb(f"asb{i}", [128, kcn, 128 * TG], BF16) for i in range(2)]
        pd = [P.ps(f"pd{i}", [128, 512]) for i in range(3)]
        it = 0
        ig = 0
        for ci, c0 in enumerate(range(col0, col0 + ncols, cw_max)):
            cw = min(cw_max, col0 + ncols - c0)
            wb = wsb[ci % len(wsb)]
            wk = ("wsb", ci % len(wsb))
            kstep = 8
            for k0 in range(0, kcn, kstep):
                k1 = min(kcn, k0 + kstep)
                P.dma(wb[:, k0:k1, 0:cw],
                      W[k0 * 128:k1 * 128, c0:c0 + cw].rearrange("(k p) n -> p k n", p=128),
                      [wname], [wk + (k0,)], eng="pool")
            wkeys = [wk + (k0,) for k0 in range(0, kcn, kstep)]
            for tg in range(0, c.NT, TG):
                tn = min(TG, c.NT - tg)
                ab = asb[ig % 2]
                ak = ("asb", ig % 2)
                ig += 1
                P.dma(ab[:, :, 0:tn * 128], srcT[:, :, tg * 128:(tg + tn) * 128].rearrange("k p t -> p k t"),
                      [(src_key, tg + j) for j in range(tn)], [ak])
                for j in range(tn):
                    ti = tg + j
                    pb = pd[it % 3]
                    pk = ("pd", it % 3)
                    it += 1
                    for kc in range(kcn):
                        P.mm(pb[:, 0:cw], ab[:, kc, j * 128:(j + 1) * 128], wb[:, kc, 0:cw], kc == 0, kc == kcn - 1,
                             [ak, wk + ((kc // kstep) * kstep,)], [pk])
                    sink(ti, c0, cw, pb, pk)

    def st_inproj(self, l):
        c, P, d = self.c, self.P, self.dram
        W = d["w_in_even"] if l % 2 == 0 else d["w_in_odd"]
        ncols = c.COLS_EVEN if l % 2 == 0 else c.COLS_ODD
        ot = [P.sb(f"ot{i}", [128, 512], F32) for i in range(3)]
        cnt = [0]

        def sink(ti, c0, cw, pb, pk):
            i = cnt[0] % 3
            cnt[0] += 1
            self.copy(self.evac_eng(), ot[i][:, 0:cw], pb[:, 0:cw], [pk], [("ot", i)])
            P.dma(d["p"][ti * 128:(ti + 1) * 128, c0:c0 + cw], ot[i][:, 0:cw], [("ot", i)], [("p", ti, c0)])

        self.dense_tok(d["aT"], c.KC, W, 0, ncols, sink, "w_in", "aT")

    def make_resid_sink(self, l, jgate):
        c, P, d = self.c, self.P, self.dram
        mb = [P.sb(f"mb{cls}", [128, c.D], F32) for cls in range(2)]
        for cls in range(2):
            P.dma(mb[cls][:], d["mod"][l, cls, jgate, :].partition_broadcast(128), ["mod"], [("mb", cls)])
        hb = [P.sb(f"hb{i}", [128, 512], F32) for i in range(3)]
        tb = [P.sb(f"tb{i}", [128, 512], F32) for i in range(3)]
        cnt = [0]
        nctx = c.L // 128

        def sink(ti, c0, cw, pb, pk):
            i = cnt[0] % 3
            cnt[0] += 1
            cls = 1 if ti < nctx else 0
            rows = slice(ti * 128, (ti + 1) * 128)
            P.dma(hb[i][:, 0:cw], d["h"][rows, c0:c0 + cw], [("h", ti, c0)], [("hb", i)])
            P.op("dve", lambda e: e.tensor_tensor(out=tb[i][:, 0:cw], in0=pb[:, 0:cw], in1=mb[cls][:, c0:c0 + cw],
                                                  op=ALU.mult), [pk, ("mb", cls)], [("tb", i)])
            P.op("pool", lambda e: e.tensor_tensor(out=hb[i][:, 0:cw], in0=hb[i][:, 0:cw], in1=tb[i][:, 0:cw],
                                                   op=ALU.add), [("tb", i), ("hb", i)], [("hb", i)])
            P.dma(d["h"][rows, c0:c0 + cw], hb[i][:, 0:cw], [("hb", i)], [("h", ti, c0)])

        return sink

    def st_mixout(self, l):
        c, d = self.c, self.dram
        sink = self.make_resid_sink(l, 2)
        self.dense_tok(d["yT"], c.KC, d["w_mix_out"][l], 0, c.D, sink, "w_mo", "yT")

    def st_ffn_out(self, l):
        c, d = self.c, self.dram
        sink = self.make_resid_sink(l, 5)
        self.dense_tok(d["hidT"], c.FC, d["w_ffn_out"][l], 0, c.D, sink, "w_fo", "hidT", cw_max=512)

    def st_ffn_in(self, l):
        c, P, d = self.c, self.P, self.dram
        W = d["w_ffn_in"][l]
        convT = P.sb("convT", [128, 3, c.FC], F32)
        cbT = P.sb("cbT", [128, c.FC], F32)
        for k in range(3):
            P.dma(convT[:, k, :], d["ffn_conv_w"][l, k, :].rearrange("(j p) -> p j", p=128), ["cw"], ["convT"],
                  allow_slow_non_contiguous=True)
        P.dma(cbT[:], d["ffn_conv_b"][l, :].rearrange("(j p) -> p j", p=128), ["cb"], ["cbT"],
              allow_slow_non_contiguous=True)
        segs = [(0, c.L), (c.L, c.S)]
        g = [P.sb(f"g{s}", [128, n + 2], F32) for s, (t0, n) in enumerate(segs)]
        u = [P.sb(f"u{s}", [128, n], F32) for s, (t0, n) in enumerate(segs)]
        t1 = P.sb("t1", [128, c.S], F32)
        hid = P.sb("hid", [128, c.NTOK], BF16)
        for s, (t0, n) in enumerate(segs):
            P.op("pool", lambda e, s=s: e.memset(g[s][:], 0.0), [], [("g", s)])
        wg = [P.sb(f"wg{i}", [128, c.KC, 128], BF16) for i in range(2)]
        wu = [P.sb(f"wu{i}", [128, c.KC, 128], BF16) for i in range(2)]
        fsb = [P.sb(f"fsb{i}", [128, c.KC, 512], BF16) for i in range(2)]
        pg = [P.ps(f"pg{i}", [128, 512]) for i in range(2)]
        pu = [P.ps(f"pu{i}", [128, 512]) for i in range(2)]
        it = 0
        for j in range(c.FC):
            wgb, wub = wg[j % 2], wu[j % 2]
            P.dma(wgb[:], W[:, j * 128:(j + 1) * 128].rearrange("(k p) n -> p k n", p=128), ["wfi"], [("wg", j % 2)],
                  eng="pool")
            P.dma(wub[:], W[:, c.DFF + j * 128:c.DFF + (j + 1) * 128].rearrange("(k p) n -> p k n", p=128),
                  ["wfi"], [("wu", j % 2)], eng="pool")
            for s, (t0, n) in enumerate(segs):
                for q0 in range(0, n, 512):
                    qn = min(512, n - q0)
                    fb = fsb[it % 2]
                    fk = ("fsb", it % 2)
                    pgb, pub = pg[it % 2], pu[it % 2]
                    pgk, puk = ("pg", it % 2), ("pu", it % 2)
                    it += 1
                    tiles = [(t0 + q0) // 128 + i for i in range(qn // 128)]
                    P.dma(fb[:, :, 0:qn], d["aT"][:, :, t0 + q0:t0 + q0 + qn].rearrange("k p t -> p k t"),
                          [("aT", ti) for ti in tiles], [fk])
                    for kc in range(c.KC):
                        P.mm(pgb[:, 0:qn], wgb[:, kc, :], fb[:, kc, 0:qn], kc == 0, kc == c.KC - 1,
                             [fk, ("wg", j % 2)], [pgk])
                    for kc in range(c.KC):
                        P.mm(pub[:, 0:qn], wub[:, kc, :], fb[:, kc, 0:qn], kc == 0, kc == c.KC - 1,
                             [fk, ("wu", j % 2)], [puk])
                    self.copy("act", g[s][:, 1 + q0:1 + q0 + qn], pgb[:, 0:qn], [pgk], [("g", s)])
                    self.copy("dve", u[s][:, q0:q0 + qn], pub[:, 0:qn], [puk], [("u", s)])
                tt = t1[:, 0:n]
                P.op("dve", lambda e, s=s, n=n, tt=tt, j=j: e.tensor_scalar(
                    out=tt, in0=g[s][:, 1:n + 1], scalar1=convT[:, 1, j:j + 1], scalar2=cbT[:, j:j + 1],
                    op0=ALU.mult, op1=ALU.add), [("g", s), "convT", "cbT"], ["t1"])
                P.op("dve", lambda e, s=s, n=n, tt=tt, j=j: e.scalar_tensor_tensor(
                    out=tt, in0=g[s][:, 0:n], scalar=convT[:, 0, j:j + 1], in1=tt, op0=ALU.mult, op1=ALU.add),
                    [("g", s), "convT", "t1"], ["t1"])
                P.op("dve", lambda e, s=s, n=n, tt=tt, j=j: e.scalar_tensor_tensor(
                    out=tt, in0=g[s][:, 2:n + 2], scalar=convT[:, 2, j:j + 1], in1=tt, op0=ALU.mult, op1=ALU.add),
                    [("g", s), "convT", "t1"], ["t1"])
                P.op("act", lambda e, tt=tt: e.activation(out=tt, in_=tt, func=AF.Silu), ["t1"], ["t1"])
                P.op("pool", lambda e, s=s, n=n, tt=tt, t0=t0: e.tensor_tensor(
                    out=hid[:, t0:t0 + n], in0=tt, in1=u[s][:, 0:n], op=ALU.mult),
                    ["t1", ("u", s)], ["hid"])
            P.dma(d["hidT"][j, :, :], hid[:], ["hid"], [("hidT", ti) for ti in range(c.NT)])

    def st_final(self):
        c, P, d = self.c, self.P, self.dram
        fnb = P.sb("fnb", [128, c.D], F32)
        P.dma(fnb[:], d["final_norm"][:].partition_broadcast(128), ["fn"], ["fnb"])
        hs = [P.sb(f"hs{i}", [128, c.D], F32) for i in range(2)]
        junk = P.sb("junk", [128, c.D], BF16)
        ss = P.sb("ss", [128, 2], F32)
        rs = P.sb("rs", [128, 2], F32)
        nctx = c.L // 128
        for i, ti in enumerate(range(nctx, c.NT)):
            j = i % 2
            P.dma(hs[j][:], d["h"][ti * 128:(ti + 1) * 128, :], [("h", ti)], [("hs", j)])
            P.op("dve", lambda e, j=j: e.memset(ss[:, j:j + 1], 0.0), [], [("ss", j)])
            P.op("act", lambda e, j=j: e.activation(out=junk[:], in_=hs[j][:], func=AF.Square,
                                                     accum_out=ss[:, j:j + 1]),
                 [("hs", j), ("ss", j)], ["junk", ("ss", j)])
            P.op("dve", lambda e, j=j: e.tensor_scalar(out=rs[:, j:j + 1], in0=ss[:, j:j + 1],
                                                       scalar1=1.0 / c.D, scalar2=1e-6,
                                                       op0=ALU.mult, op1=ALU.add), [("ss", j)], [("rs", j)])
            P.op("act", lambda e, j=j: e.activation(out=rs[:, j:j + 1], in_=rs[:, j:j + 1], func=AF.Sqrt),
                 [("rs", j)], [("rs", j)])
            P.op("dve", lambda e, j=j: e.reciprocal(out=rs[:, j:j + 1], in_=rs[:, j:j + 1]),
                 [("rs", j)], [("rs", j)])
            P.op("dve", lambda e, j=j: e.scalar_tensor_tensor(out=hs[j][:], in0=hs[j][:], scalar=rs[:, j:j + 1],
                                                              in1=fnb[:], op0=ALU.mult, op1=ALU.mult),
                 [("hs", j), ("rs", j), "fnb"], [("hs", j)])
            P.dma(d["out"][(ti - nctx) * 128:(ti - nctx + 1) * 128, :], hs[j][:], [("hs", j)], [("out", ti)])

    def prep_featmajor(self, tiles, c0, ncols, dstT, dkey, rope=None, hd=64, func=None, identb=None):
        c, P, d = self.c, self.P, self.dram
        nch = ncols // 128
        xt = [P.sb(f"xt{i}", [128, ncols], F32) for i in range(2)]
        xb = [P.sb(f"xb{i}", [128, ncols], BF16) for i in range(2)]
        half = ncols // 2
        if rope is not None:
            ct = [P.sb(f"ct{i}", [128, half], F32) for i in range(2)]
            st = [P.sb(f"st{i}", [128, half], F32) for i in range(2)]
            ta = P.sb("ropeA", [128, half], F32)
            tb = P.sb("ropeB", [128, half], F32)
        pt = [P.ps(f"ptb{i}", [128, 512], BF16) for i in range(2)]
        ev = [P.sb(f"ev{i}", [128, 512], BF16) for i in range(2)]
        n4 = 0
        for i, ti in enumerate(tiles):
            j = i % 2
            rows = slice(ti * 128, (ti + 1) * 128)
            P.dma(xt[j][:], d["p"][rows, c0:c0 + ncols], [("p", ti)], [("xt", j)])
            roped = rope is not None and rope[2](ti) is not None
            if roped:
                r0 = rope[2](ti)
                P.dma(ct[j][:], d[rope[0]][r0:r0 + 128, 0:half], ["ropetab"], [("ct", j)])
                P.dma(st[j][:], d[rope[1]][r0:r0 + 128, 0:half], ["ropetab"], [("st", j)])
                xv = xt[j][:].rearrange("p (g two e) -> p g two e", two=2, e=hd // 2)
                ov = xb[j][:].rearrange("p (g two e) -> p g two e", two=2, e=hd // 2)
                cv = ct[j][:].rearrange("p (g e) -> p g e", e=hd // 2)
                sv = st[j][:].rearrange("p (g e) -> p g e", e=hd // 2)
                tav = ta[:].rearrange("p (g e) -> p g e", e=hd // 2)
                tbv = tb[:].rearrange("p (g e) -> p g e", e=hd // 2)
                rk = [("xt", j), ("ct", j), ("st", j)]
                P.op("dve", lambda e, xv=xv, cv=cv: e.tensor_tensor(out=tav, in0=xv[:, :, 0, :], in1=cv, op=ALU.mult),
                     rk, ["ropeA"])
                P.op("pool", lambda e, xv=xv, sv=sv: e.tensor_tensor(out=tbv, in0=xv[:, :, 1, :], in1=sv, op=ALU.mult),
                     rk, ["ropeB"])
                P.op("dve", lambda e, ov=ov: e.tensor_tensor(out=ov[:, :, 0, :], in0=tav, in1=tbv, op=ALU.subtract),
                     ["ropeA", "ropeB"], [("xb", j)])
                P.op("dve", lambda e, xv=xv, cv=cv: e.tensor_tensor(out=tav, in0=xv[:, :, 1, :], in1=cv, op=ALU.mult),
                     rk + [("xb", j)], ["ropeA"])
                P.op("pool", lambda e, xv=xv, sv=sv: e.tensor_tensor(out=tbv, in0=xv[:, :, 0, :], in1=sv, op=ALU.mult),
                     rk + [("xb", j)], ["ropeB"])
                P.op("dve", lambda e, ov=ov: e.tensor_tensor(out=ov[:, :, 1, :], in0=tav, in1=tbv, op=ALU.add),
                     ["ropeA", "ropeB"], [("xb", j)])
            elif func is not None:
                P.op("act", lambda e, j=j: e.activation(out=xb[j][:], in_=xt[j][:], func=func), [("xt", j)], [("xb", j)])
            else:
                self.copy("act", xb[j][:], xt[j][:], [("xt", j)], [("xb", j)])
            for q0 in range(0, nch, 4):
                qn = min(4, nch - q0)
                pb, pk = pt[n4 % 2], ("ptb", n4 % 2)
                eb, ek = ev[n4 % 2], ("ev", n4 % 2)
                n4 += 1
                for q in range(qn):
                    P.tr(pb[:, q * 128:(q + 1) * 128], xb[j][:, (q0 + q) * 128:(q0 + q + 1) * 128], identb[:],
                         [("xb", j), "identb"], [pk])
                self.copy(self.evac_eng(), eb[:, 0:qn * 128], pb[:, 0:qn * 128], [pk], [ek])
                P.dma(d[dstT][q0:q0 + qn, :, ti * 128:(ti + 1) * 128].rearrange("k p t -> p k t"),
                      eb[:, 0:qn * 128].rearrange("p (k t) -> p k t", t=128), [ek], [(dkey, ti)])

    def prep_tokmajor(self, tiles, c0, ncols, dst, dkey):
        c, P, d = self.c, self.P, self.dram
        xt = [P.sb(f"vt{i}", [128, ncols], F32) for i in range(2)]
        xb = [P.sb(f"vb{i}", [128, ncols], BF16) for i in range(2)]
        for i, ti in enumerate(tiles):
            j = i % 2
            rows = slice(ti * 128, (ti + 1) * 128)
            P.dma(xt[j][:], d["p"][rows, c0:c0 + ncols], [("p", ti)], [("vt", j)])
            self.copy("pool", xb[j][:], xt[j][:], [("vt", j)], [("vb", j)])
            P.dma(d[dst][rows, :], xb[j][:], [("vb", j)], [(dkey, ti)])

    def load_identb(self):
        P = self.P
        identb = P.sb("identb", [128, 128], BF16)
        P.dma(identb[:], self.dram["identb"][:, :], ["identbd"], ["identb"])
        return identb

    def st_diff_prep(self):
        c = self.c
        identb = self.load_identb()
        nctx = c.L // 128
        cb = c.COLS_A
        rope = ("ropecos", "ropesin", lambda ti: None if ti < nctx else (ti - nctx) * 128)
        self.prep_featmajor(range(c.NT), cb, 2 * c.HALF, "dqkT", "dqkT", rope=rope, hd=64, identb=identb)

    def st_diff_prepv(self):
        c = self.c
        self.prep_tokmajor(range(c.NT), c.COLS_A + 2 * c.HALF, c.HALF, "dV", "dV")

    def st_diff_core(self, l, e):
        c, P, d = self.c, self.P, self.dram
        H = c.B_HEADS
        scale = 64 ** -0.5
        lam_init = 0.8 - 0.6 * math.exp(-0.3 * l)
        lv = P.sb("lv", [128, 4, 64], F32)
        P.dma(lv[:].rearrange("p a b -> p (a b)"), d["diff_lambda"][e].rearrange("a b -> (a b)").partition_broadcast(128),
              ["dl"], ["lv"])
        lt = P.sb("lt", [128, 2, 64], F32)
        ls = P.sb("ls", [128, 2], F32)
        nlam = P.sb("nlam", [128, 1], F32)
        P.op("dve", lambda e_: e_.tensor_tensor(out=lt[:, 0, :], in0=lv[:, 0, :], in1=lv[:, 1, :], op=ALU.mult), ["lv"], ["lt0"])
        P.op("dve", lambda e_: e_.tensor_tensor(out=lt[:, 1, :], in0=lv[:, 2, :], in1=lv[:, 3, :], op=ALU.mult), ["lv"], ["lt1"])
        P.op("dve", lambda e_: e_.tensor_reduce(out=ls[:], in_=lt[:], axis=AX.X, op=ALU.add), ["lt0", "lt1"], ["ls"])
        P.op("act", lambda e_: e_.activation(out=ls[:], in_=ls[:], func=AF.Exp), ["ls"], ["ls"])
        P.op("dve", lambda e_: e_.tensor_tensor(out=nlam[:], in0=ls[:, 1:2], in1=ls[:, 0:1], op=ALU.subtract), ["ls"], ["nlam"])
        P.op("dve", lambda e_: e_.tensor_scalar(out=nlam[:], in0=nlam[:], scalar1=-lam_init, scalar2=None, op0=ALU.add),
             ["nlam"], ["nlam"])
        sub = P.sb("subln", [128, 1], F32)
        P.dma(sub[:], d["diff_subln"][e].rearrange("(p o) -> p o", o=1), ["dsl"], ["subln"])
        P.op("dve", lambda e_: e_.tensor_scalar(out=sub[:], in0=sub[:], scalar1=1.0 - lam_init, scalar2=None, op0=ALU.mult),
             ["subln"], ["subln"])
        onesb = P.sb("onesb", [128, 128], BF16)
        onesf = P.sb("onesf", [128, 128], F32)
        P.op("pool", lambda e_: e_.memset(onesb[:], 1.0), [], ["onesb"])
        P.op("pool", lambda e_: e_.memset(onesf[:], 1.0 / 128.0), [], ["onesf"])
        qs = [P.sb(f"qs{i}", [64, c.NTOK], BF16) for i in range(2)]
        ks = [P.sb(f"ks{i}", [64, c.NTOK], BF16) for i in range(2)]
        vs = P.sb("vs", [128, c.NT, 128], BF16)
        pT = [P.sb(f"pT{i}", [128, 512], BF16) for i in range(4)]
        ps_s = [P.ps(f"pss{i}", [128, 512]) for i in range(2)]
        ps_o = [P.ps(f"pso{i}", [128, 512]) for i in range(2)]
        ps_z = [P.ps(f"psz{i}", [128, 512]) for i in range(2)]
        ps_n = P.ps("psn", [128, 512])
        r = [P.sb(f"r{i}", [128, 512], F32) for i in range(2)]
        A = [P.sb(f"A{i}", [128, 512], F32) for i in range(2)]
        Y = P.sb("Y", [128, 512], F32)
        Y2 = P.sb("Y2", [128, 512], F32)
        yo = P.sb("yo", [128, 512], BF16)
        nctx = c.L // 128
        QH = c.HALF // 128
        it = 0
        for h in range(H):
            for sm in range(2):
                col = h * 128 + sm * 64
                cc, p0 = col // 128, col % 128
                P.dma(qs[sm][:], d["dqkT"][cc, p0:p0 + 64, :], [("dqkT", ti) for ti in range(c.NT)], [("qs", sm)])
                P.dma(ks[sm][:], d["dqkT"][QH + cc, p0:p0 + 64, :], [("dqkT", ti) for ti in range(c.NT)], [("ks", sm)])
            P.dma(vs[:], d["dV"][:, h * 128:(h + 1) * 128].rearrange("(t p) v -> p t v", p=128),
                  [("dV", ti) for ti in range(c.NT)], ["vs"])
            qchunks = [(0, c.L, list(range(nctx)))]
            allk = list(range(nctx, c.NT)) + list(range(nctx))
            for q0 in range(c.L, c.NTOK, 512):
                qchunks.append((q0, min(512, c.NTOK - q0), allk))
            for (q0, qn, ktiles) in qchunks:
                for sm in range(2):
                    for ki, kt in enumerate(ktiles):
                        sb_, sk = ps_s[it % 2], ("pss", it % 2)
                        pb, pk = pT[it % 4], ("pT", it % 4)
                        it += 1
                        P.mm(sb_[:, 0:qn], ks[sm][:, kt * 128:(kt + 1) * 128], qs[sm][:, q0:q0 + qn], True, True,
                             [("ks", sm), ("qs", sm)], [sk])
                        P.op("act", lambda e_, pb=pb, sb_=sb_, qn=qn: e_.activation(out=pb[:, 0:qn], in_=sb_[:, 0:qn],
                                                                                 func=AF.Exp, scale=scale),
                             [sk], [pk])
                        first, last = ki == 0, ki == len(ktiles) - 1
                        P.mm(ps_o[sm][:, 0:qn], vs[:, kt, :], pb[:, 0:qn], first, last, ["vs", pk], [("pso", sm)])
                        P.mm(ps_z[sm][:, 0:qn], onesb[:], pb[:, 0:qn], first, last, ["onesb", pk], [("psz", sm)])
                    P.op("dve", lambda e_, sm=sm, qn=qn: e_.reciprocal(out=r[sm][:, 0:qn], in_=ps_z[sm][:, 0:qn]),
                         [("psz", sm)], [("r", sm)])
                    P.op("dve", lambda e_, sm=sm, qn=qn: e_.tensor_tensor(out=A[sm][:, 0:qn], in0=ps_o[sm][:, 0:qn],
                                                                       in1=r[sm][:, 0:qn], op=ALU.mult),
                         [("pso", sm), ("r", sm)], [("A", sm)])
                P.op("dve", lambda e_, qn=qn: e_.scalar_tensor_tensor(out=Y[:, 0:qn], in0=A[1][:, 0:qn], scalar=nlam[:, 0:1],
                                                                    in1=A[0][:, 0:qn], op0=ALU.mult, op1=ALU.add),
                     [("A", 0), ("A", 1), "nlam"], ["Y"])
                P.op("pool", lambda e_, qn=qn: e_.tensor_tensor(out=Y2[:, 0:qn], in0=Y[:, 0:qn], in1=Y[:, 0:qn], op=ALU.mult),
                     ["Y"], ["Y2"])
                P.mm(ps_n[:, 0:qn], onesf[:], Y2[:, 0:qn], True, True, ["onesf", "Y2"], ["psn"])
                P.op("dve", lambda e_, qn=qn: e_.tensor_scalar(out=Y2[:, 0:qn], in0=ps_n[:, 0:qn], scalar1=1e-5, scalar2=None,
                                                             op0=ALU.add), ["psn"], ["Y2"])
                P.op("act", lambda e_, qn=qn: e_.activation(out=Y2[:, 0:qn], in_=Y2[:, 0:qn], func=AF.Sqrt), ["Y2"], ["Y2"])
                P.op("dve", lambda e_, qn=qn: e_.reciprocal(out=Y2[:, 0:qn], in_=Y2[:, 0:qn]), ["Y2"], ["Y2"])
                P.op("dve", lambda e_, qn=qn: e_.scalar_tensor_tensor(out=yo[:, 0:qn], in0=Y[:, 0:qn], scalar=sub[:, 0:1],
                                                                    in1=Y2[:, 0:qn], op0=ALU.mult, op1=ALU.mult),
                     ["Y", "Y2", "subln"], ["yo"])
                P.dma(d["yT"][c.HALF // 128 + h, :, q0:q0 + qn], yo[:, 0:qn], ["yo"], [("yT", h, q0)])

    def st_swa_prep(self):
        c = self.c
        identb = self.load_identb()
        nctx = c.L // 128
        rope = ("ropecos", "ropesin", lambda ti: None if ti < nctx else (ti - nctx) * 128)
        self.prep_featmajor(range(c.NT), 0, c.HALF + c.C_KV * 64, "sqkT", "sqkT", rope=rope, hd=64, identb=identb)

    def st_swa_prepv(self):
        c = self.c
        self.prep_tokmajor(range(c.NT), c.HALF + c.C_KV * 64, c.C_KV * 64, "sV", "sV")

    def st_swa_core(self, o):
        c, P, d = self.c, self.P, self.dram
        scale = 64 ** -0.5
        nctx = c.L // 128
        nb = c.S // 128
        esink = P.sb("esink", [64, c.C_HEADS], F32)
        P.dma(esink[:], d["swa_sink"][o].partition_broadcast(64), ["sink"], ["esink"])
        P.op("act", lambda e_: e_.activation(out=esink[:], in_=esink[:], func=AF.Exp), ["esink"], ["esink"])
        maskW = P.sb("maskW", [128, 6, 512], BF16)
        P.dma(maskW[:], d["maskW"].rearrange("m p f -> p m f"), ["maskWd"], ["maskW"])
        onesb = P.sb("onesb", [128, 64], BF16)
        P.op("pool", lambda e_: e_.memset(onesb[:], 1.0), [], ["onesb"])
        qs = [P.sb(f"qs{i}", [64, c.S], BF16) for i in range(2)]
        ks = P.sb("ks", [64, c.NTOK], BF16)
        vs = P.sb("vs", [128, c.NT, 64], BF16)
        pT = [P.sb(f"pT{i}", [128, 512], BF16) for i in range(4)]
        ps_s = [P.ps(f"pss{i}", [128, 512]) for i in range(2)]
        ps_o = [P.ps(f"pso{i}", [64, 512]) for i in range(2)]
        ps_z = [P.ps(f"psz{i}", [64, 512]) for i in range(2)]
        r = [P.sb(f"r{i}", [64, 512], F32) for i in range(2)]
        yo = [P.sb(f"yo{i}", [64, 512], BF16) for i in range(2)]
        allt = [("sqkT", ti) for ti in range(c.NT)]
        it = 0
        ic = 0
        for hq in range(c.C_HEADS):
            g = hq // 8
            if hq % 8 == 0:
                colk = c.HALF + g * 64
                P.dma(ks[:], d["sqkT"][colk // 128, colk % 128:colk % 128 + 64, :], allt, ["ks"])
                P.dma(vs[:], d["sV"][:, g * 64:(g + 1) * 64].rearrange("(t p) v -> p t v", p=128),
                      [("sV", ti) for ti in range(c.NT)], ["vs"])
            qb = qs[hq % 2]
            qk = ("qs", hq % 2)
            colq = hq * 64
            P.dma(qb[:], d["sqkT"][colq // 128, colq % 128:colq % 128 + 64, c.L:], allt, [qk])
            for t0 in range(0, c.S, 512):
                qn = min(512, c.S - t0)
                qb0 = t0 // 128
                kts = []
                for m in range(6):
                    kb = qb0 + m - 1
                    if 0 <= kb < nb and kb * 128 - 128 <= t0 + qn - 1:
                        kts.append((nctx + kb, m))
                kts += [(kt, None) for kt in range(nctx)]
                po, pok = ps_o[ic % 2], ("pso", ic % 2)
                pz, pzk = ps_z[ic % 2], ("psz", ic % 2)
                rb, rk = r[ic % 2], ("r", ic % 2)
                yb, yk = yo[ic % 2], ("yo", ic % 2)
                ic += 1
                for ki, (kt, m) in enumerate(kts):
                    sb_, sk = ps_s[it % 2], ("pss", it % 2)
                    pb, pk = pT[it % 4], ("pT", it % 4)
                    it += 1
                    P.mm(sb_[:, 0:qn], ks[:, kt * 128:(kt + 1) * 128], qb[:, t0:t0 + qn], True, True, ["ks", qk], [sk])
                    P.op("act", lambda e_, pb=pb, sb_=sb_, qn=qn: e_.activation(out=pb[:, 0:qn], in_=sb_[:, 0:qn],
                                                                             func=AF.Exp, scale=scale), [sk], [pk])
                    if m is not None:
                        P.op("dve", lambda e_, pb=pb, m=m, qn=qn: e_.tensor_tensor(out=pb[:, 0:qn], in0=pb[:, 0:qn],
                                                                                in1=maskW[:, m, 0:qn], op=ALU.mult),
                             [pk, "maskW"], [pk])
                    first, last = ki == 0, ki == len(kts) - 1
                    P.mm(po[:, 0:qn], vs[:, kt, :], pb[:, 0:qn], first, last, ["vs", pk], [pok])
                    P.mm(pz[:, 0:qn], onesb[:], pb[:, 0:qn], first, last, ["onesb", pk], [pzk])
                P.op("dve", lambda e_, rb=rb, pz=pz, qn=qn, hq=hq: e_.tensor_scalar(
                    out=rb[:, 0:qn], in0=pz[:, 0:qn], scalar1=esink[:, hq:hq + 1], scalar2=None, op0=ALU.add),
                    [pzk, "esink"], [rk])
                P.op("dve", lambda e_, rb=rb, qn=qn: e_.reciprocal(out=rb[:, 0:qn], in_=rb[:, 0:qn]), [rk], [rk])
                P.op("dve", lambda e_, rb=rb, po=po, yb=yb, qn=qn: e_.tensor_tensor(
                    out=yb[:, 0:qn], in0=po[:, 0:qn], in1=rb[:, 0:qn], op=ALU.mult), [pok, rk], [yk])
                P.dma(d["yT"][colq // 128, colq % 128:colq % 128 + 64, c.L + t0:c.L + t0 + qn], yb[:, 0:qn],
                      [yk], [("yT", hq, t0)])

    def st_ret_prep(self):
        c = self.c
        identb = self.load_identb()
        rope = ("rrcos", "rrsin", lambda ti: ti * 128)
        self.prep_featmajor(range(c.NT), c.COLS_C, 2 * c.D_HEADS * 128, "rqkT", "rqkT", rope=rope, hd=128,
                            identb=identb)

    def st_ret_prepv(self):
        c = self.c
        self.prep_tokmajor(range(c.NT), c.COLS_C + 2 * c.D_HEADS * 128, c.HALF, "rV", "rV")

    def st_ret_prepg(self):
        c = self.c
        identb = self.load_identb()
        nctx = c.L // 128
        self.prep_featmajor(range(nctx, c.NT), c.COLS_C + 2 * c.D_HEADS * 128 + c.HALF, c.HALF, "rgT", "rgT",
                            func=AF.Silu, identb=identb)

    def st_ret_core(self, o):
        c, P, d = self.c, self.P, self.dram
        nctx = c.L // 128
        H = c.D_HEADS
        lnscale = math.log(128 ** -0.5)
        lg = P.sb("lg", [128, 2, H], F32)
        nlg = P.sb("nlg", [128, 2, H], F32)
        P.dma(lg[:].rearrange("p a b -> p (a b)"), d["ret_decay"][o].rearrange("a b -> (a b)").partition_broadcast(128),
              ["rd"], ["lg"])
        P.op("act", lambda e_: e_.activation(out=lg[:], in_=lg[:], func=AF.Exp, scale=-math.log(2.0)), ["lg"], ["lg"])
        P.op("dve", lambda e_: e_.tensor_scalar(out=lg[:], in0=lg[:], scalar1=-1.0, scalar2=1.0, op0=ALU.mult, op1=ALU.add),
             ["lg"], ["lg"])
        P.op("act", lambda e_: e_.activation(out=lg[:], in_=lg[:], func=AF.Ln), ["lg"], ["lg"])
        P.op("dve", lambda e_: e_.tensor_scalar(out=nlg[:], in0=lg[:], scalar1=-1.0, scalar2=None, op0=ALU.mult),
             ["lg"], ["nlg"])
        T0 = P.sb("T0", [128, 512], F32)
        P.dma(T0[:], d["retT0"][:, :], ["retT0"], ["T0"])
        Mge = P.sb("Mge", [128, 4, 512], F32)
        Mle = P.sb("Mle", [128, 4, 512], F32)
        P.dma(Mge[:], d["retMge"].rearrange("m p f -> p m f"), ["retM"], ["Mge"])
        P.dma(Mle[:], d["retMle"].rearrange("m p f -> p m f"), ["retM"], ["Mle"])
        onesf = P.sb("onesf", [128, 128], F32)
        P.op("pool", lambda e_: e_.memset(onesf[:], 1.0 / 256.0), [], ["onesf"])
        qs = P.sb("qs", [128, c.S], BF16)
        ks = P.sb("ks", [128, c.NTOK], BF16)
        vs = P.sb("vs", [128, c.NT, 256], BF16)
        gs = [P.sb(f"gs{i}", [128, 512], BF16) for i in range(2)]
        Wt = [P.sb(f"Wt{i}", [128, 512], F32) for i in range(3)]
        Wu = [P.sb(f"Wu{i}", [128, 512], F32) for i in range(2)]
        bc = [P.sb(f"bc{i}", [128, 2], F32) for i in range(4)]
        pT = [P.sb(f"pT{i}", [128, 512], BF16) for i in range(3)]
        ps_s = [P.ps(f"pss{i}", [128, 512]) for i in range(2)]
        ps_o = [P.ps(f"pso{i}", [128, 512]) for i in range(2)]
        ps_n = P.ps("psn", [128, 512])
        Ys = [P.sb(f"Ys{i}", [128, 512], F32) for i in range(2)]
        Y2 = [P.sb(f"Y2{i}", [128, 512], F32) for i in range(2)]
        rst = P.sb("rst", [128, 512], F32)
        yo = [P.sb(f"yo{i}", [128, 512], BF16) for i in range(2)]
        allt = [("rqkT", ti) for ti in range(c.NT)]
        it = 0
        for hd in range(H):
            P.dma(qs[:], d["rqkT"][hd, :, c.L:], allt, ["qs"])
            P.dma(ks[:], d["rqkT"][H + hd, :, :], allt, ["ks"])
            P.dma(vs[:], d["rV"][:, hd * 256:(hd + 1) * 256].rearrange("(t p) v -> p t v", p=128),
                  [("rV", ti) for ti in range(c.NT)], ["vs"])
            lgf, lgb = lg[:, 0, hd:hd + 1], lg[:, 1, hd:hd + 1]
            nlgb = nlg[:, 1, hd:hd + 1]
            for t0 in range(0, c.S, 512):
                qn = min(512, c.S - t0)
                for kt in range(c.NT):
                    wb, wk = Wt[it % 3], ("Wt", it % 3)
                    b_, bk = bc[it % 4], ("bc", it % 4)
                    sb_, sk = ps_s[it % 2], ("pss", it % 2)
                    pb, pk = pT[it % 3], ("pT", it % 3)
                    it += 1

                    def bias(col, lgap, off, b_=b_, bk=bk):
                        P.op("dve", lambda e_: e_.tensor_scalar(out=b_[:, col:col + 1], in0=lgap, scalar1=float(off),
                                                                scalar2=lnscale, op0=ALU.mult, op1=ALU.add),
                             ["lg", "nlg"], [bk + (col,)])

                    def expw(out, scale_ap, col, b_=b_, bk=bk, qn=qn):
                        P.op("act", lambda e_: e_.activation(out=out[:, 0:qn], in_=T0[:, 0:qn], func=AF.Exp,
                                                             scale=scale_ap, bias=b_[:, col:col + 1]),
                             ["T0", "lg", "nlg", bk + (col,)], [])

                    if kt < nctx:
                        j0 = kt * 128
                        bias(0, lgf, c.L + t0 - j0)
                        bias(1, lgb, c.S - t0 + j0)
                        u0, u0k = Wu[0], ("Wu", 0)
                        P.op("act", lambda e_, wb=wb, b_=b_, qn=qn, lgf=lgf, nlgb=nlgb: e_.activation(
                            out=wb[:, 0:qn], in_=T0[:, 0:qn], func=AF.Exp, scale=lgf, bias=b_[:, 0:1]),
                            ["T0", "lg", bk + (0,)], [wk])
                        P.op("act", lambda e_, u0=u0, b_=b_, qn=qn, lgf=lgf, nlgb=nlgb: e_.activation(
                            out=u0[:, 0:qn], in_=T0[:, 0:qn], func=AF.Exp, scale=nlgb, bias=b_[:, 1:2]),
                            ["T0", "nlg", bk + (1,)], [u0k])
                        P.op("pool", lambda e_, wb=wb, u0=u0, qn=qn: e_.tensor_tensor(
                            out=wb[:, 0:qn], in0=wb[:, 0:qn], in1=u0[:, 0:qn], op=ALU.add), [wk, u0k], [wk])
                    else:
                        k0 = (kt - nctx) * 128
                        off = t0 - k0
                        if off >= 128:
                            bias(0, lgf, off)
                            P.op("act", lambda e_, wb=wb, b_=b_, qn=qn, lgf=lgf, nlgb=nlgb: e_.activation(
                                out=wb[:, 0:qn], in_=T0[:, 0:qn], func=AF.Exp, scale=lgf, bias=b_[:, 0:1]),
                                ["T0", "lg", bk + (0,)], [wk])
                        elif off <= -512:
                            bias(0, nlgb, off)
                            P.op("act", lambda e_, wb=wb, b_=b_, qn=qn, lgf=lgf, nlgb=nlgb: e_.activation(
                                out=wb[:, 0:qn], in_=T0[:, 0:qn], func=AF.Exp, scale=nlgb, bias=b_[:, 0:1]),
                                ["T0", "nlg", bk + (0,)], [wk])
                        else:
                            di = (-off) // 128
                            bias(0, lgf, off)
                            bias(1, nlgb, off)
                            u0, u0k = Wu[0], ("Wu", 0)
                            u1, u1k = Wu[1], ("Wu", 1)
                            P.op("act", lambda e_, u0=u0, b_=b_, qn=qn, lgf=lgf, nlgb=nlgb: e_.activation(
                                out=u0[:, 0:qn], in_=T0[:, 0:qn], func=AF.Exp, scale=lgf, bias=b_[:, 0:1]),
                                ["T0", "lg", bk + (0,)], [u0k])
                            P.op("act", lambda e_, u1=u1, b_=b_, qn=qn, lgf=lgf, nlgb=nlgb: e_.activation(
                                out=u1[:, 0:qn], in_=T0[:, 0:qn], func=AF.Exp, scale=nlgb, bias=b_[:, 1:2]),
                                ["T0", "nlg", bk + (1,)], [u1k])
                            P.op("dve", lambda e_, u0=u0, di=di, qn=qn: e_.tensor_tensor(
                                out=u0[:, 0:qn], in0=u0[:, 0:qn], in1=Mge[:, di, 0:qn], op=ALU.mult), [u0k, "Mge"], [u0k])
                            P.op("pool", lambda e_, u1=u1, di=di, qn=qn: e_.tensor_tensor(
                                out=u1[:, 0:qn], in0=u1[:, 0:qn], in1=Mle[:, di, 0:qn], op=ALU.mult), [u1k, "Mle"], [u1k])
                            P.op("dve", lambda e_, wb=wb, u0=u0, u1=u1, qn=qn: e_.tensor_tensor(
                                out=wb[:, 0:qn], in0=u0[:, 0:qn], in1=u1[:, 0:qn], op=ALU.add), [u0k, u1k], [wk])
                    if "dbgW" in d and hd == 0 and t0 == 0:
                        P.dma(d["dbgW"][kt, :, 0:qn], wb[:, 0:qn], [wk], [("dbgW", kt)])
                    P.mm(sb_[:, 0:qn], ks[:, kt * 128:(kt + 1) * 128], qs[:, t0:t0 + qn], True, True, ["ks", "qs"], [sk])
                    P.op("dve", lambda e_, pb=pb, sb_=sb_, wb=wb, qn=qn: e_.tensor_tensor(
                        out=pb[:, 0:qn], in0=sb_[:, 0:qn], in1=wb[:, 0:qn], op=ALU.mult), [sk, wk], [pk])
                    first, last = kt == 0, kt == c.NT - 1
                    P.mm(ps_o[0][:, 0:qn], vs[:, kt, 0:128], pb[:, 0:qn], first, last, ["vs", pk], [("pso", 0)])
                    P.mm(ps_o[1][:, 0:qn], vs[:, kt, 128:256], pb[:, 0:qn], first, last, ["vs", pk], [("pso", 1)])
                for hf in range(2):
                    self.copy("act", Ys[hf][:, 0:qn], ps_o[hf][:, 0:qn], [("pso", hf)], [("Ys", hf)])
                    P.op("pool", lambda e_, hf=hf, qn=qn: e_.tensor_tensor(
                        out=Y2[hf][:, 0:qn], in0=Ys[hf][:, 0:qn], in1=Ys[hf][:, 0:qn], op=ALU.mult),
                        [("Ys", hf)], [("Y2", hf)])
                    P.mm(ps_n[:, 0:qn], onesf[:], Y2[hf][:, 0:qn], hf == 0, hf == 1, ["onesf", ("Y2", hf)], ["psn"])
                P.op("dve", lambda e_, qn=qn: e_.tensor_scalar(out=rst[:, 0:qn], in0=ps_n[:, 0:qn], scalar1=1e-6,
                                                             scalar2=None, op0=ALU.add), ["psn"], ["rst"])
                P.op("act", lambda e_, qn=qn: e_.activation(out=rst[:, 0:qn], in_=rst[:, 0:qn], func=AF.Sqrt),
                     ["rst"], ["rst"])
                P.op("dve", lambda e_, qn=qn: e_.reciprocal(out=rst[:, 0:qn], in_=rst[:, 0:qn]), ["rst"], ["rst"])
                for hf in range(2):
                    ch = hd * 2 + hf
                    P.dma(gs[hf][:, 0:qn], d["rgT"][ch, :, c.L + t0:c.L + t0 + qn],
                          [("rgT", ti) for ti in range(c.NT)], [("gs", hf)])
                    P.op("dve", lambda e_, hf=hf, qn=qn: e_.tensor_tensor(
                        out=Ys[hf][:, 0:qn], in0=Ys[hf][:, 0:qn], in1=rst[:, 0:qn], op=ALU.mult),
                        [("Ys", hf), "rst"], [("Ys", hf)])
                    P.op("dve", lambda e_, hf=hf, qn=qn: e_.tensor_tensor(
                        out=yo[hf][:, 0:qn], in0=Ys[hf][:, 0:qn], in1=gs[hf][:, 0:qn], op=ALU.mult),
                        [("Ys", hf), ("gs", hf)], [("yo", hf)])
                    P.dma(d["yT"][c.HALF // 128 + ch, :, c.L + t0:c.L + t0 + qn], yo[hf][:, 0:qn],
                          [("yo", hf)], [("yT", "r", ch, t0)])

    def st_zero_y1(self):
        c, P, d = self.c, self.P, self.dram
        z = P.sb("z", [128, c.NTOK], BF16)
        P.op("pool", lambda e_: e_.memset(z[:], 0.0), [], ["z"])
        for ch in range(c.HALF // 128):
            P.dma(d["yT"][ch, :, :], z[:], ["z"], [("yT", "z", ch)])


    def seg_of(self, ti):
        nctx = self.c.L // 128
        return (0, nctx) if ti < nctx else (nctx, self.c.NT)

    def st_rwkv_shift(self, e):
        c, P, d = self.c, self.P, self.dram
        CA = c.COLS_A
        mu = P.sb("mu", [128, CA], F32)
        P.dma(mu[:], d["rwkv_mu"][e].partition_broadcast(128), ["mud"], ["mu"])
        xt = P.sb("xt", [128, CA], F32)
        xp = P.sb("xp", [128, CA], F32)
        xn = P.sb("xn", [128, CA], F32)
        for ti in range(c.NT):
            s0, s1 = self.seg_of(ti)
            r0 = ti * 128
            P.dma(xt[:], d["p"][r0:r0 + 128, 0:CA], ["p"], ["xt"])
            if ti == s0:
                P.op("pool", lambda e_: e_.memset(xp[:], 0.0), [], ["xp"])
                P.dma(xp[1:128, :], d["p"][r0:r0 + 127, 0:CA], ["p"], ["xp"])
            else:
                P.dma(xp[:], d["p"][r0 - 1:r0 + 127, 0:CA], ["p"], ["xp"])
            if ti == s1 - 1:
                P.op("pool", lambda e_: e_.memset(xn[:], 0.0), [], ["xn"])
                P.dma(xn[0:127, :], d["p"][r0 + 1:r0 + 128, 0:CA], ["p"], ["xn"])
            else:
                P.dma(xn[:], d["p"][r0 + 1:r0 + 129, 0:CA], ["p"], ["xn"])
            P.op("pool", lambda e_: e_.tensor_tensor(out=xp[:], in0=xp[:], in1=xn[:], op=ALU.add), ["xp", "xn"], ["xp"])
            P.op("dve", lambda e_: e_.scalar_tensor_tensor(out=xp[:], in0=xp[:], scalar=0.5, in1=xt[:], op0=ALU.mult,
                                                          op1=ALU.subtract), ["xp", "xt"], ["xp"])
            P.op("pool", lambda e_: e_.tensor_tensor(out=xp[:], in0=xp[:], in1=mu[:], op=ALU.mult), ["xp", "mu"], ["xp"])
            P.op("dve", lambda e_: e_.tensor_tensor(out=xt[:], in0=xt[:], in1=xp[:], op=ALU.add), ["xp", "xt"], ["xt"])
            P.dma(d["pa"][r0:r0 + 128, :], xt[:], ["xt"], [("pa", ti)])

    def st_rwkv_prep(self, e):
        c, P, d = self.c, self.P, self.dram
        CA, H2, A = c.COLS_A, c.HALF, c.A_HEADS
        J = A // 2
        lo = 3 * H2
        ident = P.sb("ident", [128, 128], F32)
        P.dma(ident[:], d["ident"][:, :], ["identd"], ["ident"])
        w2s = P.sb("w2s", [97, 2, H2], F32)
        a2s = P.sb("a2s", [97, 2, H2], F32)
        g2s = P.sb("g2s", [128, 2, H2], F32)
        for dd in range(2):
            P.dma(w2s[0:96, dd, :], d["rwkv_w2"][e, dd], ["w2d"], ["w2s"])
            P.dma(w2s[96:97, dd, :], d["rwkv_w0"][e, dd:dd + 1, :], ["w0d"], ["w2s"])
            P.dma(a2s[0:96, dd, :], d["rwkv_a2"][e, dd], ["a2d"], ["a2s"])
            P.dma(a2s[96:97, dd, :], d["rwkv_a0"][e, dd:dd + 1, :], ["a0d"], ["a2s"])
            P.dma(g2s[:, dd, :], d["rwkv_g2"][e, dd * 128:(dd + 1) * 128, :], ["g2d"], ["g2s"])
        kkb = P.sb("kkb", [128, H2], F32)
        kab = P.sb("kab", [128, H2], F32)
        rkb = P.sb("rkb", [128, H2], F32)
        P.dma(kkb[:], d["rwkv_k_k"][e].partition_broadcast(128), ["kkd"], ["kkb"])
        P.dma(kab[:], d["rwkv_k_a"][e].partition_broadcast(128), ["kad"], ["kab"])
        P.dma(rkb[:], d["rwkv_r_k"][e].rearrange("h k -> (h k)").partition_broadcast(128), ["rkd"], ["rkb"])
        X = P.sb("X", [128, CA], F32)
        T = [P.sb(f"T{i}", [128, H2], F32) if i != 5 else None for i in range(8)]
        lz = P.sb("lz", [128, 640], F32)
        zT = P.sb("zT", [128, 6, 128], F32)
        P.op("pool", lambda e_: e_.memset(zT[96:97, 0:4, :], 1.0), [], ["zT"])
        vTt = P.sb("vTt", [128, J, 128], F32)
        ssq = P.sb("ssq", [128, A], F32)
        bon = P.sb("bon", [128, A], F32)
        pzA = P.ps("pzA", [128, 512])
        pzB = P.ps("pzB", [128, 256])
        pw = [P.ps(f"pw{i}", [128, 512]) for i in range(3)]
        pv = [P.ps(f"pv{i}", [128, 512]) for i in range(2)]
        npw = 0
        v3 = lambda t: t[:].rearrange("p (h k) -> p h k", k=64)
        chunked = getattr(self, "_chunked", False)

        def wstore(dd, arr, r0, ap, rkeys, wkey):
            if chunked:
                P.dma(d[f"Wt{dd}"][r0:r0 + 128, arr, :], ap, rkeys, [wkey])
                return
            for hh in range(2):
                P.dma(d[f"Wd{dd}"][r0:r0 + 128, hh, arr, :, :],
                      ap.rearrange("p (j hh k) -> p hh j k", hh=2, k=64)[:, hh], rkeys, [wkey + (hh,)])
        for ti in range(c.NT):
            r0 = ti * 128
            P.dma(X[:], d["pa"][r0:r0 + 128, :], [("pa", ti)], ["X"])
            rr, kk_, vv = X[:, 0:H2], X[:, H2:2 * H2], X[:, 2 * H2:3 * H2]
            P.op("act", lambda e_: e_.activation(out=lz[:, 0:192], in_=X[:, lo:lo + 192], func=AF.Tanh), ["X"], ["lz"])
            P.op("act", lambda e_: e_.copy(out=lz[:, 192:384], in_=X[:, lo + 192:lo + 384]), ["X"], ["lz"])
            P.op("act", lambda e_: e_.activation(out=lz[:, 384:640], in_=X[:, lo + 384:lo + 640], func=AF.Sigmoid),
                 ["X"], ["lz"])
            for i in range(4):
                P.tr(pzA[0:96, i * 128:(i + 1) * 128], lz[:, i * 96:(i + 1) * 96], ident[:], ["lz", "ident"], ["pzA"])
            for i in range(2):
                P.tr(pzB[:, i * 128:(i + 1) * 128], lz[:, 384 + i * 128:384 + (i + 1) * 128], ident[:], ["lz", "ident"], ["pzB"])
            P.op("dve", lambda e_: e_.tensor_copy(out=zT[0:96, 0:4, :], in_=pzA[0:96, :].rearrange("p (a t) -> p a t", t=128)),
                 ["pzA"], ["zT"])
            P.op("dve", lambda e_: e_.tensor_copy(out=zT[:, 4:6, :], in_=pzB[:, :].rearrange("p (a t) -> p a t", t=128)),
                 ["pzB"], ["zT"])
            for cc in range(0, H2, 512):
                pb, pk = pw[npw % 3], ("pw", npw % 3)
                npw += 1
                P.mm(pb[:, :], zT[:, 4, :], g2s[:, 0, cc:cc + 512], True, False, ["zT", "g2s"], [pk])
                P.mm(pb[:, :], zT[:, 5, :], g2s[:, 1, cc:cc + 512], False, True, ["zT", "g2s"], [pk])
                self.copy("act", T[6][:, cc:cc + 512], pb[:, :], [pk], ["T6"])
            P.dma(d["gate"][r0:r0 + 128, :], T[6][:], ["T6"], [("gate", ti)])
            P.op("pool", lambda e_: e_.tensor_tensor(out=T[3][:], in0=kk_, in1=kkb[:], op=ALU.mult), ["X", "kkb"], ["T3"])
            P.op("dve", lambda e_: e_.tensor_tensor(out=T[7][:], in0=T[3][:], in1=T[3][:], op=ALU.mult), ["T3"], ["T7"])
            P.op("dve", lambda e_: e_.tensor_reduce(out=ssq[:], in_=v3(T[7]), axis=AX.X, op=ALU.add), ["T7"], ["ssq"])
            P.op("dve", lambda e_: e_.tensor_scalar(out=ssq[:], in0=ssq[:], scalar1=1e-12, scalar2=None, op0=ALU.add),
                 ["ssq"], ["ssq"])
            P.op("act", lambda e_: e_.activation(out=ssq[:], in_=ssq[:], func=AF.Sqrt), ["ssq"], ["ssq"])
            P.op("dve", lambda e_: e_.reciprocal(out=ssq[:], in_=ssq[:]), ["ssq"], ["ssq"])
            P.op("dve", lambda e_: e_.tensor_tensor(out=v3(T[2]), in0=v3(T[3]),
                                                    in1=ssq[:].unsqueeze(2).broadcast_to([128, A, 64]), op=ALU.mult),
                 ["T3", "ssq"], ["T2"])
            for dd in range(2):
                wstore(dd, 1, r0, T[2][:], ["T2"], ("Wd", dd, ti, 1))
                wstore(dd, 4, r0, rr, ["X"], ("Wd", dd, ti, 4))
            for dd in range(2):
                for cc in range(0, H2, 512):
                    pb, pk = pw[npw % 3], ("pw", npw % 3)
                    npw += 1
                    P.mm(pb[:, :], zT[0:97, dd, :], w2s[0:97, dd, cc:cc + 512], True, True, ["zT", "w2s"], [pk])
                    P.op("act", lambda e_, pb=pb, cc=cc: e_.activation(out=T[0][:, cc:cc + 512], in_=pb[:, :],
                                                                     func=AF.Sigmoid), [pk], ["T0"])
                P.op("act", lambda e_: e_.activation(out=T[0][:], in_=T[0][:],
                                                     func=(AF.Copy if chunked else AF.Exp), scale=-math.exp(-0.5)),
                     ["T0"], ["T0"])
                wstore(dd, 0, r0, T[0][:], ["T0"], ("Wd", dd, ti, 0))
                for cc in range(0, H2, 512):
                    pb, pk = pw[npw % 3], ("pw", npw % 3)
                    npw += 1
                    P.mm(pb[:, :], zT[0:97, 2 + dd, :], a2s[0:97, dd, cc:cc + 512], True, True, ["zT", "a2s"], [pk])
                    P.op("act", lambda e_, pb=pb, cc=cc: e_.activation(out=T[1][:, cc:cc + 512], in_=pb[:, :],
                                                                     func=AF.Sigmoid), [pk], ["T1"])
                P.op("pool", lambda e_: e_.tensor_tensor(out=T[3][:], in0=T[2][:], in1=T[1][:], op=ALU.mult),
                     ["T2", "T1"], ["T3"])
                wstore(dd, 2, r0, T[3][:], ["T3"], ("Wd", dd, ti, 2))
                kdst, kdk = (T[4], "T4") if dd == 0 else (T[6], "T6")
                P.op("dve", lambda e_: e_.scalar_tensor_tensor(out=T[7][:], in0=T[1][:], scalar=-1.0, in1=kab[:],
                                                              op0=ALU.add, op1=ALU.mult), ["T1", "kab"], ["T7"])
                P.op("dve", lambda e_, kdst=kdst: e_.scalar_tensor_tensor(out=kdst[:], in0=T[7][:], scalar=1.0, in1=kk_,
                                                                         op0=ALU.add, op1=ALU.mult),
                     ["T7", "X"], [kdk])
                wstore(dd, 3, r0, kdst[:], [kdk], ("Wd", dd, ti, 3))
            P.op("pool", lambda e_: e_.tensor_tensor(out=T[7][:], in0=T[4][:], in1=T[6][:], op=ALU.add), ["T4", "T6"], ["T7"])
            P.op("dve", lambda e_: e_.tensor_tensor(out=T[7][:], in0=T[7][:], in1=rr, op=ALU.mult), ["T7", "X"], ["T7"])
            P.op("pool", lambda e_: e_.tensor_tensor(out=T[7][:], in0=T[7][:], in1=rkb[:], op=ALU.mult), ["T7", "rkb"], ["T7"])
            P.op("dve", lambda e_: e_.tensor_reduce(out=bon[:], in_=v3(T[7]), axis=AX.X, op=ALU.add), ["T7"], ["bon"])
            P.dma(d["bon"][r0:r0 + 128, :], bon[:], ["bon"], [("bond", ti)])
            P.dma(d["vtok"][r0:r0 + 128, :], vv, ["X"], [("vtok", ti)])
            for j0 in range(0, 0 if chunked else J, 4):
                jn = min(4, J - j0)
                pb, pk = pv[(j0 // 4) % 2], ("pv", (j0 // 4) % 2)
                for jj in range(jn):
                    P.tr(pb[:, jj * 128:(jj + 1) * 128], X[:, 2 * H2 + (j0 + jj) * 128:2 * H2 + (j0 + jj + 1) * 128], ident[:],
                         ["X", "ident"], [pk])
                P.op("dve", lambda e_, pb=pb, j0=j0, jn=jn: e_.tensor_copy(
                    out=vTt[:, j0:j0 + jn, :], in_=pb[:, 0:jn * 128].rearrange("p (a t) -> p a t", t=128)), [pk], ["vTt"])
            if not chunked:
                P.dma(d["vTs"][:, :, r0:r0 + 128], vTt[:], ["vTt"], [("vTs", ti)])

    def st_rwkv_scan(self, dd):
        c, P, d = self.c, self.P, self.dram
        A = c.A_HEADS
        J = A // 2
        nctx = c.L // 128
        S = P.sb("S", [128, J, 64], F32)
        P.op("pool", lambda e_: e_.memset(S[:], 0.0), [], ["S"])
        NB = 3
        bcb = [P.sb(f"bcb{i}", [128, 5, J, 64], F32) for i in range(NB)]
        vT = [P.sb(f"vT{i}", [128, J, 128], F32) for i in range(2)]
        yT = [P.sb(f"yTt{i}", [128, J, 128], F32) for i in range(2)]
        tA = [P.sb(f"tA{i}", [128, J, 64], F32) for i in range(2)]
        t2 = [P.sb(f"t2{i}", [128, J, 64], F32) for i in range(2)]
        t3 = [P.sb(f"t3{i}", [128, J, 64], F32) for i in range(2)]
        sa = [P.sb(f"sa{i}", [128, J], F32) for i in range(2)]
        Wd = d[f"Wd{dd}"]
        if dd == 0:
            tiles = list(range(c.NT))
        else:
            tiles = list(range(nctx - 1, -1, -1)) + list(range(c.NT - 1, nctx - 1, -1))
        step = 0
        for xi, ti in enumerate(tiles):
            vb, vk = vT[xi % 2], ("vT", xi % 2)
            yb, yk = yT[xi % 2], ("yT", xi % 2)
            r0 = ti * 128
            P.dma(vb[:], d["vTs"][:, :, r0:r0 + 128], ["vTs"], [vk])
            order = range(128) if dd == 0 else range(127, -1, -1)
            for tt in order:
                u = r0 + tt
                bb, bk = bcb[step % NB], ("bcb", step % NB)
                i2 = step % 2
                step += 1
                for hh in range(2):
                    P.dma(bb[hh * 64:(hh + 1) * 64].rearrange("p a j k -> p (a j k)"),
                          Wd[u, hh].rearrange("a j k -> (a j k)").partition_broadcast(64), ["Wd"], [bk + (hh,)])
                bks = [bk + (0,), bk + (1,)]
                P.op("dve", lambda e_, bb=bb, i2=i2: e_.tensor_tensor(out=tA[i2][:], in0=S[:], in1=bb[:, 1], op=ALU.mult),
                     ["S"] + bks, [("tA", i2)])
                P.op("dve", lambda e_, i2=i2: e_.tensor_reduce(out=sa[i2][:], in_=tA[i2][:], axis=AX.X, op=ALU.add),
                     [("tA", i2)], [("sa", i2)])
                P.op("pool", lambda e_, bb=bb, vb=vb, tt=tt, i2=i2: e_.tensor_tensor(
                    out=t3[i2][:], in0=bb[:, 3], in1=vb[:, :, tt].unsqueeze(2).broadcast_to([128, J, 64]), op=ALU.mult),
                    bks + [vk], [("t3", i2)])
                P.op("pool", lambda e_, bb=bb: e_.tensor_tensor(out=S[:], in0=S[:], in1=bb[:, 0], op=ALU.mult),
                     ["S"] + bks, ["S"])
                P.op("dve", lambda e_, bb=bb, i2=i2: e_.tensor_tensor(
                    out=t2[i2][:], in0=bb[:, 2], in1=sa[i2][:].unsqueeze(2).broadcast_to([128, J, 64]), op=ALU.mult),
                    bks + [("sa", i2)], [("t2", i2)])
                P.op("dve", lambda e_, i2=i2: e_.tensor_tensor(out=t3[i2][:], in0=t3[i2][:], in1=t2[i2][:], op=ALU.subtract),
                     [("t3", i2), ("t2", i2)], [("t3", i2)])
                P.op("pool", lambda e_, i2=i2: e_.tensor_tensor(out=S[:], in0=S[:], in1=t3[i2][:], op=ALU.add),
                     ["S", ("t3", i2)], ["S"])
                P.op("dve", lambda e_, bb=bb, i2=i2: e_.tensor_tensor(out=tA[i2][:], in0=S[:], in1=bb[:, 4], op=ALU.mult),
                     ["S"] + bks, [("tA", i2)])
                P.op("dve", lambda e_, yb=yb, tt=tt, i2=i2: e_.tensor_reduce(out=yb[:, :, tt], in_=tA[i2][:], axis=AX.X,
                                                                            op=ALU.add), [("tA", i2)], [yk])
            P.dma(d[f"ysc{dd}"][:, :, r0:r0 + 128], yb[:], [yk], [("ysc", ti)])

    def st_rwkv_prep2(self, e):
        self._chunked = True
        self.st_rwkv_prep(e)

    def st_rwkv_chunk(self, dd):
        c, P, d = self.c, self.P, self.dram
        H2, A = c.HALF, c.A_HEADS
        C = 64
        G = min(16, A)
        GW = G * 64
        NG = A // G
        nch_ctx = c.L // C
        nch = c.NTOK // C
        ident = P.sb("ident", [128, 128], F32)
        P.dma(ident[:], d["ident"][:, :], ["identd"], ["ident"])
        tri = P.sb("tri", [64, 64], F32)
        P.dma(tri[:], d["ctri"][dd], ["ctri"], ["tri"])
        onec = P.sb("onec", [64, 1], F32)
        P.op("pool", lambda e_: e_.memset(onec[:], 1.0), [], ["onec"])
        onesr = P.sb("onesr", [1, 64], F32)
        P.op("pool", lambda e_: e_.memset(onesr[:], 1.0), [], ["onesr"])
        mP = P.sb("mP", [64, 128], F32)
        mL = P.sb("mL", [64, 64], F32)
        P.dma(mP[:], d["cmaskP"][dd], ["cmp"], ["mP"])
        P.dma(mL[:], d["cmaskL"][dd], ["cml"], ["mL"])
        Tst = P.sb("Tst", [64, A, 64], F32)
        P.op("pool", lambda e_: e_.memset(Tst[:], 0.0), [], [("T", g) for g in range(NG)])
        X5 = P.sb("X5", [64, 5, GW], F32)
        V = P.sb("V", [64, GW], F32)
        Lc = P.sb("Lc", [64, GW], F32)
        LtR = P.sb("LtR", [1, GW], F32)
        E = [P.sb(f"E{i}", [64, GW], F32) for i in range(8)]
        ARt = P.sb("ARt", [64, G, 128], F32)
        BtT = P.sb("BtT", [64, G, 64], F32)
        KtT = P.sb("KtT", [64, G, 64], F32)
        GamT = P.sb("GamT", [64, G], F32)
        Pb = P.sb("Pb", [64, G, 128], F32)
        Pk = P.sb("Pk", [64, G, 128], F32)
        Lm = [P.sb(f"Lm{i}", [64, G, 64], F32) for i in range(2)]
        Nm = [P.sb(f"Nm{i}", [64, G, 64], F32) for i in range(2)]
        MT = [P.sb(f"MT{i}", [64, G, 64], F32) for i in range(2)]
        Zs = P.sb("Zs", [64, G, 64], F32)
        Us = P.sb("Us", [64, G, 64], F32)
        Yt = P.sb("Yt", [64, GW], F32)
        ps = [P.ps(f"pc{i}", [64, 512]) for i in range(8)]
        psi = [0]

        def bank():
            i = psi[0] % 8
            psi[0] += 1
            return ps[i], ("pc", i)

        def hv(t):
            return t[:].rearrange("p (h k) -> p h k", k=64)

        Wt = d[f"Wt{dd}"]
        if dd == 0:
            chunks = list(range(nch))
        else:
            chunks = list(range(nch_ctx - 1, -1, -1)) + list(range(nch - 1, nch_ctx - 1, -1))
        HB = 8
        for ch in chunks:
            u0 = ch * C
            for g in range(NG):
                c0 = g * GW
                P.dma(X5[:], Wt[u0:u0 + C, :, c0:c0 + GW], ["Wt"], ["X5"])
                P.dma(V[:], d["vtok"][u0:u0 + C, c0:c0 + GW], ["vtok"], ["V"])
                lw, kk_, b_, kd_, r_ = (X5[:, i, :] for i in range(5))
                for cc in range(0, GW, 512):
                    pb, pk = bank()
                    P.mm(pb[:, :], tri[:], X5[:, 0, cc:cc + 512], True, True, ["tri", "X5"], [pk])
                    self.copy("act", Lc[:, cc:cc + 512], pb[:, :], [pk], ["Lc"])
                    pb, pk = bank()
                    P.mm(pb[0:1, :], onec[:], X5[:, 0, cc:cc + 512], True, True, ["onec", "X5"], [pk])
                    self.copy("dve", LtR[:, cc:cc + 512], pb[0:1, :], [pk], ["LtR"])
                P.op("act", lambda e_: e_.activation(out=E[0][:], in_=Lc[:], func=AF.Exp), ["Lc"], ["E0"])
                P.op("act", lambda e_: e_.activation(out=E[1][:], in_=Lc[:], func=AF.Exp, scale=-1.0), ["Lc"], ["E1"])
                P.op("dve", lambda e_, lw=lw: e_.tensor_tensor(out=E[2][:], in0=Lc[:], in1=lw, op=ALU.subtract),
                     ["Lc", "X5"], ["E2"])
                P.op("act", lambda e_: e_.activation(out=E[2][:], in_=E[2][:], func=AF.Exp), ["E2"], ["E2"])
                P.op("act", lambda e_: e_.activation(out=LtR[:], in_=LtR[:], func=AF.Exp), ["LtR"], ["LtR"])
                for cc in range(0, GW, 512):
                    pb, pk = bank()
                    P.mm(pb[:, :], onesr[:], LtR[0:1, cc:cc + 512], True, True, ["onesr", "LtR"], [pk])
                    P.op("dve", lambda e_, pb=pb, cc=cc: e_.tensor_tensor(out=E[3][:, cc:cc + 512], in0=pb[:, :],
                                                                       in1=E[1][:, cc:cc + 512], op=ALU.mult),
                         [pk, "E1"], ["E3"])
                P.op("dve", lambda e_, kk_=kk_: e_.scalar_tensor_tensor(out=E[2][:], in0=kk_, scalar=-1.0, in1=E[2][:],
                                                                      op0=ALU.mult, op1=ALU.mult), ["X5", "E2"], ["E2"])
                P.op("pool", lambda e_, b_=b_: e_.tensor_tensor(out=E[4][:], in0=b_, in1=E[1][:], op=ALU.mult), ["X5", "E1"], ["E4"])
                P.op("dve", lambda e_, kd_=kd_: e_.tensor_tensor(out=E[5][:], in0=kd_, in1=E[1][:], op=ALU.mult), ["X5", "E1"], ["E5"])
                P.op("pool", lambda e_, r_=r_: e_.tensor_tensor(out=E[0][:], in0=r_, in1=E[0][:], op=ALU.mult), ["X5", "E0"], ["E0"])
                P.op("dve", lambda e_, b_=b_: e_.tensor_tensor(out=E[6][:], in0=b_, in1=E[3][:], op=ALU.mult), ["X5", "E3"], ["E6"])
                P.op("pool", lambda e_, kd_=kd_: e_.tensor_tensor(out=E[7][:], in0=kd_, in1=E[3][:], op=ALU.mult), ["X5", "E3"], ["E7"])
                for src_t, skey, dst, dkey, doff in ((E[2], "E2", ARt, "ARt", 0), (E[0], "E0", ARt, "ARt", 64),
                                                     (E[4], "E4", BtT, "BtT", 0), (E[5], "E5", KtT, "KtT", 0)):
                    for h0 in range(0, G, HB):
                        pb, pk = bank()
                        for hh in range(HB):
                            P.tr(pb[:, hh * 64:(hh + 1) * 64], src_t[:, (h0 + hh) * 64:(h0 + hh + 1) * 64], ident[0:64, 0:64],
                                 [skey, "ident"], [pk])
                        dv = dst[:, h0:h0 + HB, doff:doff + 64]
                        self.copy(self.evac_eng(), dv, pb[:, :].rearrange("p (h t) -> p h t", t=64), [pk], [dkey])
                pb, pk = bank()
                for h in range(G):
                    P.tr(pb[:, h:h + 1], LtR[0:1, h * 64:(h + 1) * 64], ident[0:1, 0:1], ["LtR", "ident"], [pk])
                self.copy("act", GamT[:], pb[:, 0:G], [pk], ["GamT"])
                for lhs, lkey, dst, dkey in ((BtT, "BtT", Pb, "Pb"), (KtT, "KtT", Pk, "Pk")):
                    for h0 in range(0, G, 4):
                        pb, pk = bank()
                        for hh in range(4):
                            P.mm(pb[:, hh * 128:(hh + 1) * 128], lhs[:, h0 + hh, :], ARt[:, h0 + hh, :], True, True,
                                 [lkey, "ARt"], [pk])
                        P.op("dve", lambda e_, pb=pb, dst=dst, h0=h0: e_.tensor_tensor(
                            out=dst[:, h0:h0 + 4, :], in0=pb[:, :].rearrange("p (h t) -> p h t", t=128),
                            in1=mP[:].unsqueeze(1).broadcast_to([64, 4, 128]), op=ALU.mult), [pk, "mP"], [dkey])
                for h0 in range(0, G, HB):
                    pb, pk = bank()
                    for hh in range(HB):
                        P.mm(pb[:, hh * 64:(hh + 1) * 64], ARt[:, h0 + hh, 0:64], BtT[:, h0 + hh, :], True, True,
                             ["ARt", "BtT"], [pk])
                    P.op("dve", lambda e_, pb=pb, h0=h0: e_.tensor_tensor(
                        out=Lm[0][:, h0:h0 + HB, :], in0=pb[:, :].rearrange("p (h t) -> p h t", t=64),
                        in1=mL[:].unsqueeze(1).broadcast_to([64, HB, 64]), op=ALU.mult), [pk, "mL"], [("Lm", 0)])
                P.op("pool", lambda e_: e_.tensor_copy(out=Nm[0][:], in_=Pb[:, :, 0:64]), ["Pb"], [("Nm", 0)])
                P.op("dve", lambda e_: e_.tensor_tensor(out=MT[0][:], in0=Nm[0][:],
                                                        in1=ident[0:64, 0:64].unsqueeze(1).broadcast_to([64, G, 64]),
                                                        op=ALU.add), [("Nm", 0), "ident"], [("MT", 0)])
                cur = 0
                for lvl in range(1, 6):
                    nxt = cur ^ 1
                    for lhsA, lkA, rhsA, rkA, dstA, dkA in ((Lm[cur], ("Lm", cur), Nm[cur], ("Nm", cur), Nm[nxt], ("Nm", nxt)),
                                                            (Nm[cur], ("Nm", cur), Lm[cur], ("Lm", cur), Lm[nxt], ("Lm", nxt))):
                        for h0 in range(0, G, HB):
                            pb, pk = bank()
                            for hh in range(HB):
                                P.mm(pb[:, hh * 64:(hh + 1) * 64], lhsA[:, h0 + hh, :], rhsA[:, h0 + hh, :], True, True,
                                     [lkA, rkA], [pk])
                            self.copy(self.evac_eng(), dstA[:, h0:h0 + HB, :], pb[:, :].rearrange("p (h t) -> p h t", t=64),
                                      [pk], [dkA])
                    for h0 in range(0, G, HB):
                        pb, pk = bank()
                        for hh in range(HB):
                            P.mm(pb[:, hh * 64:(hh + 1) * 64], Lm[nxt][:, h0 + hh, :], MT[cur][:, h0 + hh, :], True, True,
                                 [("Lm", nxt), ("MT", cur)], [pk])
                        P.op("dve", lambda e_, pb=pb, h0=h0, cur=cur, nxt=nxt: e_.tensor_tensor(
                            out=MT[nxt][:, h0:h0 + HB, :], in0=pb[:, :].rearrange("p (h t) -> p h t", t=64),
                            in1=MT[cur][:, h0:h0 + HB, :], op=ALU.add), [pk, ("MT", cur)], [("MT", nxt)])
                    cur = nxt
                MTf, MTk = MT[cur], ("MT", cur)
                Tg = Tst[:, g * G:(g + 1) * G, :]
                tk = ("T", g)
                Vh = hv(V)
                for h0 in range(0, G, HB):
                    pb, pk = bank()
                    for hh in range(HB):
                        h = h0 + hh
                        P.mm(pb[:, hh * 64:(hh + 1) * 64], ARt[:, h, 0:64], Tg[:, h, :], True, False, ["ARt", tk], [pk])
                        P.mm(pb[:, hh * 64:(hh + 1) * 64], Pk[:, h, 0:64], Vh[:, h, :], False, True, ["Pk", "V"], [pk])
                    self.copy(self.evac_eng(), Zs[:, h0:h0 + HB, :], pb[:, :].rearrange("p (h t) -> p h t", t=64), [pk], ["Zs"])
                for h0 in range(0, G, HB):
                    pb, pk = bank()
                    for hh in range(HB):
                        h = h0 + hh
                        P.mm(pb[:, hh * 64:(hh + 1) * 64], MTf[:, h, :], Zs[:, h, :], True, True, [MTk, "Zs"], [pk])
                    self.copy(self.evac_eng(), Us[:, h0:h0 + HB, :], pb[:, :].rearrange("p (h t) -> p h t", t=64), [pk], ["Us"])
                for h0 in range(0, G, HB):
                    pb, pk = bank()
                    for hh in range(HB):
                        h = h0 + hh
                        P.mm(pb[:, hh * 64:(hh + 1) * 64], ARt[:, h, 64:128], Tg[:, h, :], True, False, ["ARt", tk], [pk])
                        P.mm(pb[:, hh * 64:(hh + 1) * 64], Pb[:, h, 64:128], Us[:, h, :], False, False, ["Pb", "Us"], [pk])
                        P.mm(pb[:, hh * 64:(hh + 1) * 64], Pk[:, h, 64:128], Vh[:, h, :], False, True, ["Pk", "V"], [pk])
                    self.copy(self.evac_eng(), Yt[:, h0 * 64:(h0 + HB) * 64], pb[:, :], [pk], ["Yt"])
                P.dma(d[f"ytk{dd}"][u0:u0 + C, c0:c0 + GW], Yt[:], ["Yt"], [("ytk", ch, g)])
                Bh, Kh = hv(E[6]), hv(E[7])
                P.op("pool", lambda e_, Tg=Tg: e_.tensor_tensor(out=Tg, in0=Tg, in1=GamT[:].unsqueeze(2).broadcast_to([64, G, 64]),
                                                               op=ALU.mult), [tk, "GamT"], [tk])
                for h0 in range(0, G, HB):
                    pb, pk = bank()
                    for hh in range(HB):
                        h = h0 + hh
                        P.mm(pb[:, hh * 64:(hh + 1) * 64], Bh[:, h, :], Us[:, h, :], True, False, ["E6", "Us"], [pk])
                        P.mm(pb[:, hh * 64:(hh + 1) * 64], Kh[:, h, :], Vh[:, h, :], False, True, ["E7", "V"], [pk])
                    P.op("dve", lambda e_, pb=pb, h0=h0, Tg=Tg: e_.tensor_tensor(
                        out=Tg[:, h0:h0 + HB, :], in0=Tg[:, h0:h0 + HB, :], in1=pb[:, :].rearrange("p (h t) -> p h t", t=64),
                        op=ALU.add), [pk, tk], [tk])

    def st_rwkv_post(self, e):
        c, P, d = self.c, self.P, self.dram
        H2, A = c.HALF, c.A_HEADS
        J = A // 2
        ident = P.sb("ident", [128, 128], F32)
        P.dma(ident[:], d["ident"][:, :], ["identd"], ["ident"])
        identb = self.load_identb()
        lwb = P.sb("lwb", [128, H2], F32)
        lbb = P.sb("lbb", [128, H2], F32)
        P.dma(lwb[:], d["rwkv_ln_w"][e].partition_broadcast(128), ["lwd"], ["lwb"])
        P.dma(lbb[:], d["rwkv_ln_b"][e].partition_broadcast(128), ["lbd"], ["lbb"])
        ya = P.sb("ya", [128, J, 128], F32)
        yb_ = P.sb("yb", [128, J, 128], F32)
        Y = P.sb("Y", [128, H2], F32)
        Q = P.sb("Q", [128, H2], F32)
        G = P.sb("G", [128, H2], F32)
        V = P.sb("V", [128, H2], F32)
        Yb = P.sb("Yb", [128, H2], BF16)
        st = P.sb("st", [128, A], F32)
        bon = P.sb("bon", [128, A], F32)
        ev = [P.sb(f"ev{i}", [128, 512], BF16) for i in range(2)]
        pp = [P.ps(f"pp{i}", [128, 512]) for i in range(2)]
        pq = [P.ps(f"pq{i}", [128, 512], BF16) for i in range(2)]
        v3 = lambda t: t[:].rearrange("p (h k) -> p h k", k=64)
        bc = lambda s: s[:].unsqueeze(2).broadcast_to([128, A, 64])
        n4 = 0
        for ti in range(c.NT):
            r0 = ti * 128
            chunked = getattr(self, "_chunked", False)
            if chunked:
                P.dma(Y[:], d["ytk0"][r0:r0 + 128, :], ["ytk0"], ["Y"])
                P.dma(Q[:], d["ytk1"][r0:r0 + 128, :], ["ytk1"], ["Q"])
                P.op("pool", lambda e_: e_.tensor_tensor(out=Y[:], in0=Y[:], in1=Q[:], op=ALU.add), ["Y", "Q"], ["Y"])
            else:
                P.dma(ya[:], d["ysc0"][:, :, r0:r0 + 128], ["ysc0"], ["ya"])
                P.dma(yb_[:], d["ysc1"][:, :, r0:r0 + 128], ["ysc1"], ["yb"])
            P.dma(G[:], d["gate"][r0:r0 + 128, :], ["gate"], ["G"])
            P.dma(V[:], d["vtok"][r0:r0 + 128, :], ["vtok"], ["V"])
            P.dma(bon[:], d["bon"][r0:r0 + 128, :], ["bond"], ["bon"])
            if not chunked:
                P.op("pool", lambda e_: e_.tensor_tensor(out=ya[:], in0=ya[:], in1=yb_[:], op=ALU.add), ["ya", "yb"], ["ya"])
            for j0 in range(0, 0 if chunked else J, 4):
                jn = min(4, J - j0)
                pb, pk = pp[(j0 // 4) % 2], ("pp", (j0 // 4) % 2)
                for jj in range(jn):
                    P.tr(pb[:, jj * 128:(jj + 1) * 128], ya[:, j0 + jj, :], ident[:], ["ya", "ident"], [pk])
                self.copy("act", Y[:, j0 * 128:(j0 + jn) * 128], pb[:, 0:jn * 128], [pk], ["Y"])
            P.op("dve", lambda e_: e_.tensor_reduce(out=st[:], in_=v3(Y), axis=AX.X, op=ALU.add), ["Y"], ["st"])
            P.op("dve", lambda e_: e_.tensor_scalar(out=st[:], in0=st[:], scalar1=1.0 / 64.0, scalar2=None, op0=ALU.mult),
                 ["st"], ["st"])
            P.op("dve", lambda e_: e_.tensor_tensor(out=v3(Y), in0=v3(Y), in1=bc(st), op=ALU.subtract), ["Y", "st"], ["Y"])
            P.op("pool", lambda e_: e_.tensor_tensor(out=Q[:], in0=Y[:], in1=Y[:], op=ALU.mult), ["Y"], ["Q"])
            P.op("dve", lambda e_: e_.tensor_reduce(out=st[:], in_=v3(Q), axis=AX.X, op=ALU.add), ["Q", "Y"], ["st"])
            P.op("dve", lambda e_: e_.tensor_scalar(out=st[:], in0=st[:], scalar1=1.0 / 64.0, scalar2=64e-5,
                                                    op0=ALU.mult, op1=ALU.add), ["st"], ["st"])
            P.op("act", lambda e_: e_.activation(out=st[:], in_=st[:], func=AF.Sqrt), ["st"], ["st"])
            P.op("dve", lambda e_: e_.reciprocal(out=st[:], in_=st[:]), ["st"], ["st"])
            P.op("dve", lambda e_: e_.tensor_tensor(out=v3(Y), in0=v3(Y), in1=bc(st), op=ALU.mult), ["Y", "st"], ["Y"])
            P.op("pool", lambda e_: e_.tensor_tensor(out=Y[:], in0=Y[:], in1=lwb[:], op=ALU.mult), ["Y", "lwb"], ["Y"])
            P.op("dve", lambda e_: e_.tensor_tensor(out=Y[:], in0=Y[:], in1=lbb[:], op=ALU.add), ["Y", "lbb"], ["Y"])
            P.op("pool", lambda e_: e_.tensor_tensor(out=v3(Q), in0=v3(V), in1=bc(bon), op=ALU.mult), ["V", "bon", "Q"], ["Q"])
            P.op("dve", lambda e_: e_.tensor_tensor(out=Y[:], in0=Y[:], in1=Q[:], op=ALU.add), ["Y", "Q"], ["Y"])
            P.op("dve", lambda e_: e_.tensor_tensor(out=Yb[:], in0=Y[:], in1=G[:], op=ALU.mult), ["Y", "G"], ["Yb"])
            for q0 in range(0, H2 // 128, 4):
                qn = min(4, H2 // 128 - q0)
                pb, pk = pq[n4 % 2], ("pq", n4 % 2)
                eb, ek = ev[n4 % 2], ("ev", n4 % 2)
                n4 += 1
                for q in range(qn):
                    P.tr(pb[:, q * 128:(q + 1) * 128], Yb[:, (q0 + q) * 128:(q0 + q + 1) * 128], identb[:],
                         ["Yb", "identb"], [pk])
                self.copy(self.evac_eng(), eb[:, 0:qn * 128], pb[:, 0:qn * 128], [pk], [ek])
                P.dma(d["yT"][q0:q0 + qn, :, r0:r0 + 128].rearrange("k p t -> p k t"),
                      eb[:, 0:qn * 128].rearrange("p (k t) -> p k t", t=128), [ek], [("yT", "a", ti, q0)])

    def st_mod(self):
        c, P, d = self.c, self.P, self.dram
        NR = 2
        ncols = 6 * c.D
        ident = P.sb("ident", [128, 128], F32)
        P.dma(ident[:], d["ident"][:, :], ["identd"], ["ident"])
        cv = P.sb("cv", [NR, c.D], F32)
        P.dma(cv[:], d["cvec"][:, :], ["cvec"], ["cv"])
        P.op("act", lambda e: e.activation(out=cv[:], in_=cv[:], func=AF.Silu), ["cv"], ["cv"])
        cT = P.sb("cT", [128, c.KC, NR], F32)
        pt = P.ps("pt", [128, 512])
        for kc in range(c.KC):
            P.tr(pt[:, 0:NR], cv[:, kc * 128:(kc + 1) * 128], ident[0:NR, 0:NR], ["cv", "ident"], ["pt"])
            self.copy("dve", cT[:, kc, :], pt[:, 0:NR], ["pt"], ["cT"])
        ws = [P.sb(f"ws{i}", [128, c.KC, 512], F32) for i in range(2)]
        po = [P.ps(f"po{i}", [NR, 512]) for i in range(2)]
        bb = [P.sb(f"bb{i}", [NR, 512], F32) for i in range(2)]
        ob = [P.sb(f"ob{i}", [NR, 512], F32) for i in range(2)]
        it = 0
        for l in range(c.DEPTH):
            mo = d["mod"][l].rearrange("r j d -> r (j d)")
            for c0 in range(0, ncols, 512):
                cw = min(512, ncols - c0)
                i2 = it % 2
                w, wk = ws[i2], ("ws", i2)
                p_, pk = po[i2], ("po", i2)
                o_, ok = ob[i2], ("ob", i2)
                b_, bk = bb[i2], ("bb", i2)
                it += 1
                P.dma(b_[:, 0:cw], d["b_mod"][l, c0:c0 + cw].partition_broadcast(NR), ["bm"], [bk])
                for k0 in range(0, c.KC, 8):
                    k1 = min(c.KC, k0 + 8)
                    P.dma(w[:, k0:k1, 0:cw], d["w_mod"][l, k0 * 128:k1 * 128, c0:c0 + cw].rearrange(
                        "(k p) n -> p k n", p=128), ["wm"], [wk + (k0,)])
                for kc in range(c.KC):
                    P.mm(p_[:, 0:cw], cT[:, kc, :], w[:, kc, 0:cw], kc == 0, kc == c.KC - 1,
                         ["cT", wk + ((kc // 8) * 8,)], [pk])
                P.op("dve", lambda e, o_=o_, p_=p_, b_=b_, cw=cw: e.tensor_tensor(
                    out=o_[:, 0:cw], in0=p_[:, 0:cw], in1=b_[:, 0:cw], op=ALU.add), [pk, bk], [ok])
                P.dma(mo[:, c0:c0 + cw], o_[:, 0:cw], [ok], [("mod", l, c0)])

def _axial(n, dim):
    rows = n // 64
    row = np.repeat(np.arange(rows, dtype=np.float32), 64)
    col = np.tile(np.arange(64, dtype=np.float32), rows)
    quarter = dim // 4
    inv = (10000.0 ** (-np.arange(quarter, dtype=np.float32) / quarter)).astype(np.float32)
    ang = np.concatenate([row[:, None] * inv, col[:, None] * inv], -1)
    return np.cos(ang).astype(np.float32), np.sin(ang).astype(np.float32)


def _seqrope(n, dim):
    inv = (10000.0 ** (-np.linspace(0.0, 1.0, dim // 2, dtype=np.float32))).astype(np.float32)
    ang = np.arange(n, dtype=np.float32)[:, None] * inv
    return np.cos(ang).astype(np.float32), np.sin(ang).astype(np.float32)


def _consts(cfg):
    import ml_dtypes
    cs, sn = _axial(cfg.S, 64)
    rc, rs = _seqrope(cfg.NTOK, 128)
    p = np.arange(128)[:, None]
    f = np.arange(512)[None, :]
    maskW = np.stack([(np.abs((m - 1) * 128 + p - f) <= 128) for m in range(6)]).astype(ml_dtypes.bfloat16)
    T0 = (f - p).astype(np.float32)
    Mge = np.stack([((-128 * di + f - p) >= 0) for di in range(4)]).astype(np.float32)
    Mle = np.stack([((-128 * di + f - p) <= 0) for di in range(4)]).astype(np.float32)
    s_ = np.arange(64)[:, None]
    t_ = np.arange(64)[None, :]
    ctri = np.stack([s_ <= t_, s_ >= t_]).astype(np.float32)
    cmaskP = np.stack([np.concatenate([s_ < t_, s_ <= t_], 1), np.concatenate([s_ > t_, s_ >= t_], 1)]).astype(np.float32)
    cmaskL = np.stack([t_ < s_, t_ > s_]).astype(np.float32)
    return dict(ropecos=np.tile(cs, (1, cfg.HALF // 32)), ropesin=np.tile(sn, (1, cfg.HALF // 32)),
                rrcos=np.tile(rc, (1, 2 * cfg.D_HEADS)), rrsin=np.tile(rs, (1, 2 * cfg.D_HEADS)),
                maskW=maskW, retT0=T0, retMge=Mge, retMle=Mle, ctri=ctri, cmaskP=cmaskP, cmaskL=cmaskL,
                ident=np.eye(128, dtype=np.float32), identb=np.eye(128).astype(ml_dtypes.bfloat16))


def build_mod(cfg, ncols):
    b = Builder(cfg)
    P, c = b.P, cfg
    NR = cfg.B + 1
    b.din("cvec", [NR, c.D]); b.din("wm", [c.DEPTH, c.D, ncols]); b.din("bm", [c.DEPTH, ncols]); b.din("ident", [128, 128])
    b.dout("mo", [c.DEPTH, NR, ncols])
    d = b.dram

    def stage():
        ident = P.sb("ident", [128, 128], F32)
        P.dma(ident[:], d["ident"][:, :], ["identd"], ["ident"])
        cv = P.sb("cv", [NR, c.D], F32)
        P.dma(cv[:], d["cvec"][:, :], ["cvec"], ["cv"])
        P.op("act", lambda e: e.activation(out=cv[:], in_=cv[:], func=AF.Silu), ["cv"], ["cv"])
        cT = P.sb("cT", [128, c.KC, NR], F32)
        pt = P.ps("pt", [128, 512])
        for kc in range(c.KC):
            P.tr(pt[:, 0:NR], cv[:, kc * 128:(kc + 1) * 128], ident[0:NR, 0:NR], ["cv", "ident"], ["pt"])
            b.copy("dve", cT[:, kc, :], pt[:, 0:NR], ["pt"], ["cT"])
        ws = [P.sb(f"ws{i}", [128, c.KC, 512], F32) for i in range(2)]
        po = [P.ps(f"po{i}", [NR, 512]) for i in range(2)]
        bb = P.sb("bb", [NR, c.DEPTH, ncols], F32)
        for l in range(c.DEPTH):
            P.dma(bb[:, l, :], d["bm"][l, :].partition_broadcast(NR), ["bm"], ["bb"])
        ob = [P.sb(f"ob{i}", [NR, 512], F32) for i in range(2)]
        it = 0
        for l in range(c.DEPTH):
            for c0 in range(0, ncols, 512):
                cw = min(512, ncols - c0)
                w, wk = ws[it % 2], ("ws", it % 2)
                p_, pk = po[it % 2], ("po", it % 2)
                o_, ok = ob[it % 2], ("ob", it % 2)
                it += 1
                for k0 in range(0, c.KC, 8):
                    P.dma(w[:, k0:k0 + 8, 0:cw], d["wm"][l, k0 * 128:(k0 + 8) * 128, c0:c0 + cw].rearrange(
                        "(k p) n -> p k n", p=128), ["wm"], [wk + (k0,)])
                for kc in range(c.KC):
                    P.mm(p_[:, 0:cw], cT[:, kc, :], w[:, kc, 0:cw], kc == 0, kc == c.KC - 1,
                         ["cT", wk + ((kc // 8) * 8,)], [pk])
                P.op("dve", lambda e, o_=o_, p_=p_, l=l, c0=c0, cw=cw: e.tensor_tensor(
                    out=o_[:, 0:cw], in0=p_[:, 0:cw], in1=bb[:, l, c0:c0 + cw], op=ALU.add), [pk, "bb"], [ok])
                P.dma(d["mo"][l, :, c0:c0 + cw], o_[:, 0:cw], [ok], [("mo", l, c0)])

    b.run_stage(stage)
    return b


def rwkv_decl(b, dbg=False):
    c = b.c
    scr = b.dout if dbg else b.dscr
    H2, A = c.HALF, c.A_HEADS
    b.din("rwkv_mu", [1, c.COLS_A]); b.din("rwkv_w0", [1, 2, H2]); b.din("rwkv_w2", [1, 2, 96, H2])
    b.din("rwkv_a0", [1, 2, H2]); b.din("rwkv_a2", [1, 2, 96, H2]); b.din("rwkv_g2", [1, 256, H2])
    b.din("rwkv_k_k", [1, H2]); b.din("rwkv_k_a", [1, H2]); b.din("rwkv_r_k", [1, A, 64])
    b.din("rwkv_ln_w", [1, H2]); b.din("rwkv_ln_b", [1, H2])
    scr("pa", [c.NTOK, c.COLS_A]); scr("Wd0", [c.NTOK, 2, 5, A // 2, 64]); scr("Wd1", [c.NTOK, 2, 5, A // 2, 64])
    scr("vTs", [128, A // 2, c.NTOK]); scr("ysc0", [128, A // 2, c.NTOK]); scr("ysc1", [128, A // 2, c.NTOK])
    scr("gate", [c.NTOK, H2]); scr("vtok", [c.NTOK, H2]); scr("bon", [c.NTOK, A])
    scr("Wt0", [c.NTOK, 5, H2]); scr("Wt1", [c.NTOK, 5, H2]); scr("ytk0", [c.NTOK, H2]); scr("ytk1", [c.NTOK, H2])
    b.din("ctri", [2, 64, 64]); b.din("cmaskP", [2, 64, 128]); b.din("cmaskL", [2, 64, 64])


RWKV_KEYS = ("rwkv_mu", "rwkv_w0", "rwkv_w2", "rwkv_a0", "rwkv_a2", "rwkv_g2", "rwkv_k_k", "rwkv_k_a", "rwkv_r_k",
             "rwkv_ln_w", "rwkv_ln_b")


def build_main(cfg):
    b = Builder(cfg)
    c = cfg
    b.din("x", [c.S, c.D]); b.din("ctx", [c.L, c.D]); b.dscr("mod", [c.DEPTH, 2, 6, c.D])
    b.din("cvec", [2, c.D]); b.din("w_mod", [c.DEPTH, c.D, 6 * c.D]); b.din("b_mod", [c.DEPTH, 6 * c.D])
    b.din("ident", [128, 128]); b.din("identb", [128, 128], BF16)
    b.din("ropecos", [c.S, c.HALF]); b.din("ropesin", [c.S, c.HALF])
    b.din("rrcos", [c.NTOK, 2 * c.D_HEADS * 64]); b.din("rrsin", [c.NTOK, 2 * c.D_HEADS * 64])
    b.din("maskW", [6, 128, 512], BF16); b.din("retT0", [128, 512])
    b.din("retMge", [4, 128, 512]); b.din("retMle", [4, 128, 512])
    b.din("w_in_even", [c.D, c.COLS_EVEN]); b.din("w_in_odd", [c.D, c.COLS_ODD])
    b.din("diff_lambda", [1, 4, 64]); b.din("diff_subln", [1, 128])
    b.din("swa_sink", [1, c.C_HEADS]); b.din("ret_decay", [1, 2, c.D_HEADS])
    rwkv_decl(b)
    b.din("w_mix_out", [c.DEPTH, c.D, c.D]); b.din("w_ffn_in", [c.DEPTH, c.D, 2 * c.DFF])
    b.din("ffn_conv_w", [c.DEPTH, 3, c.DFF]); b.din("ffn_conv_b", [c.DEPTH, c.DFF])
    b.din("w_ffn_out", [c.DEPTH, c.DFF, c.D]); b.din("final_norm", [c.D])
    b.dout("out", [c.S, c.D])
    b.dscr("h", [c.NTOK, c.D]); b.dscr("aT", [c.KC, 128, c.NTOK], BF16)
    b.dscr("p", [c.NTOK, max(c.COLS_EVEN, c.COLS_ODD)])
    b.dscr("yT", [c.KC, 128, c.NTOK], BF16); b.dscr("hidT", [c.FC, 128, c.NTOK], BF16)
    b.dscr("dqkT", [2 * c.HALF // 128, 128, c.NTOK], BF16); b.dscr("dV", [c.NTOK, c.HALF], BF16)
    b.dscr("sqkT", [(c.HALF + c.C_KV * 64) // 128, 128, c.NTOK], BF16); b.dscr("sV", [c.NTOK, c.C_KV * 64], BF16)
    b.dscr("rqkT", [2 * c.D_HEADS, 128, c.NTOK], BF16); b.dscr("rV", [c.NTOK, c.HALF], BF16)
    b.dscr("rgT", [c.HALF // 128, 128, c.NTOK], BF16)
    rs = b.run_stage
    rs(b.st_mod)
    rs(b.st_init_h)
    for l in range(c.DEPTH):
        rs(b.st_norm, l, 0)
        rs(b.st_inproj, l)
        if l % 2 == 0:
            e = l // 2
            rs(b.st_rwkv_shift, e); rs(b.st_rwkv_prep2, e)
            rs(b.st_rwkv_chunk, 0); rs(b.st_rwkv_chunk, 1)
            rs(b.st_rwkv_post, e)
            rs(b.st_diff_prep); rs(b.st_diff_prepv); rs(b.st_diff_core, l, l // 2)
        else:
            rs(b.st_swa_prep); rs(b.st_swa_prepv); rs(b.st_swa_core, l // 2)
            rs(b.st_ret_prep); rs(b.st_ret_prepv); rs(b.st_ret_prepg); rs(b.st_ret_core, l // 2)
        rs(b.st_mixout, l)
        rs(b.st_norm, l, 1)
        rs(b.st_ffn_in, l)
        rs(b.st_ffn_out, l)
    rs(b.st_final)
    return b


def kernel(x, c, ctx, c_ctx, w_mod, b_mod, w_in_even, rwkv_mu, rwkv_w0, rwkv_w2, rwkv_a0, rwkv_a2,
           rwkv_g2, rwkv_k_k, rwkv_k_a, rwkv_r_k, rwkv_ln_w, rwkv_ln_b, diff_lambda, diff_subln,
           w_in_odd, swa_sink, ret_decay, w_mix_out, w_ffn_in, ffn_conv_w, ffn_conv_b, w_ffn_out,
           final_norm):
    f32 = lambda a: np.ascontiguousarray(np.asarray(a, dtype=np.float32))
    cfg = Cfg(D=x.shape[2], S=x.shape[1], L=ctx.shape[1], B=x.shape[0], DEPTH=w_mod.shape[0])
    hc = _consts(cfg)
    main = build_main(cfg)
    x = np.asarray(x)
    ctx = np.asarray(ctx)
    c = f32(c)
    c_ctx = f32(c_ctx)
    shared = dict(w_mod=f32(w_mod), b_mod=f32(b_mod), w_in_even=f32(w_in_even)[0], w_in_odd=f32(w_in_odd)[0], diff_lambda=f32(diff_lambda),
                  diff_subln=f32(diff_subln), swa_sink=f32(swa_sink), ret_decay=f32(ret_decay),
                  w_mix_out=f32(w_mix_out), w_ffn_in=f32(w_ffn_in), ffn_conv_w=f32(ffn_conv_w),
                  ffn_conv_b=f32(ffn_conv_b), w_ffn_out=f32(w_ffn_out), final_norm=f32(final_norm),
                  rwkv_mu=f32(rwkv_mu), rwkv_w0=f32(rwkv_w0), rwkv_w2=f32(rwkv_w2), rwkv_a0=f32(rwkv_a0),
                  rwkv_a2=f32(rwkv_a2), rwkv_g2=f32(rwkv_g2), rwkv_k_k=f32(rwkv_k_k), rwkv_k_a=f32(rwkv_k_a),
                  rwkv_r_k=f32(rwkv_r_k), rwkv_ln_w=f32(rwkv_ln_w), rwkv_ln_b=f32(rwkv_ln_b))
    for k in ("ident", "identb", "ropecos", "ropesin", "rrcos", "rrsin", "maskW", "retT0", "retMge", "retMle",
              "ctri", "cmaskP", "cmaskL"):
        shared[k] = hc[k]
    in2 = []
    for bi in range(cfg.B):
        dd = dict(shared)
        dd.update(x=f32(x[bi]), ctx=f32(ctx[bi]), cvec=np.ascontiguousarray(np.stack([c[bi], c_ctx], 0)))
        in2.append(dd)
    r2 = run_bass_kernel_spmd(main.nc, in2, core_ids=list(range(cfg.B)))
    return np.stack([r2.results[bi]["out"] for bi in range(cfg.B)], axis=0).astype(np.float32)
```

```python
import math
from contextlib import ExitStack
import numpy as np
import concourse.bass as bass
import concourse.mybir as mybir
from concourse.bass_utils import run_bass_kernel_spmd

F32 = mybir.dt.float32
BF16 = mybir.dt.bfloat16
AF = mybir.ActivationFunctionType
ALU = mybir.AluOpType
AX = mybir.AxisListType


class Cfg:
    def __init__(self, D=4096, S=4096, L=256, B=4, DEPTH=2):
        self.D, self.S, self.L, self.B, self.DEPTH = D, S, L, B, DEPTH
        self.HALF = D // 2
        self.NTOK = S + L
        self.NT = self.NTOK // 128
        self.KC = D // 128
        self.A_HEADS = self.HALF // 64
        self.COLS_A = 3 * self.HALF + 4 * 96 + 256
        self.B_HEADS = self.HALF // 128
        self.COLS_B = 3 * self.HALF
        self.C_HEADS = self.HALF // 64
        self.C_KV = self.C_HEADS // 8
        self.COLS_C = self.HALF + 2 * self.C_KV * 64
        self.D_HEADS = self.HALF // 256
        self.COLS_D = 2 * self.D_HEADS * 128 + 2 * self.HALF
        self.COLS_EVEN = self.COLS_A + self.COLS_B
        self.COLS_ODD = self.COLS_C + self.COLS_D
        self.DFF = ((8 * D // 3 + 255) // 256) * 256
        self.FC = self.DFF // 128


class Prog:
    CENG = ("pe", "dve", "act", "pool")
    NDS = 8

    def __init__(self, nc, es):
        self.nc = nc
        self.es = es
        self.q = {e: [] for e in ("pe", "dve", "act", "pool", "sp")}
        self.csem = {e: es.enter_context(nc.semaphore(f"c_{e}")) for e in self.CENG}
        self.dsem = {e: [es.enter_context(nc.semaphore(f"d_{e}{i}")) for i in range(self.NDS)]
                     for e in ("sp", "pool")}
        self.dcnt = {}
        self.drr = {"sp": 0, "pool": 0}
        self.known = {e: {} for e in self.q}
        self.bufs = {}
        self.waited = {e: set() for e in self.CENG}
        self.nins = {e: 0 for e in self.CENG}
        self.cbase = {e: 0 for e in self.CENG}

    def _st(self, b):
        s = self.bufs.get(b)
        if s is None:
            s = self.bufs[b] = {"w": None, "r": {}}
        return s

    def op(self, eng, fn, reads=(), writes=(), dma=False):
        deps = {}

        def add(tok):
            if tok is None:
                return
            k, v = tok
            if eng == "pe" and k == ("c", "pe"):
                return
            if deps.get(k, 0) < v:
                deps[k] = v

        for b in reads:
            add(self._st(b)["w"])
        for b in writes:
            s = self._st(b)
            add(s["w"])
            for k, v in s["r"].items():
                add((k, v))
        if dma:
            i = self.drr[eng]
            self.drr[eng] = (i + 1) % self.NDS
            key = ("d", eng, i)
            prev = self.dcnt.get(key, 0)
            if prev:
                add((key, prev))
            self.dcnt[key] = prev + 1
            tok = (key, prev + 1)
        else:
            self.nins[eng] += 1
            tok = (("c", eng), self.nins[eng])
        waits = []
        kn = self.known[eng]
        for k, v in deps.items():
            if kn.get(k, 0) < v:
                kn[k] = v
                waits.append((k, v))
                if k[0] == "c":
                    self.waited[k[1]].add(v)
        self.q[eng].append((waits, fn, tok))
        for b in writes:
            s = self._st(b)
            s["w"] = tok
            s["r"] = {}
        for b in reads:
            if b in writes:
                continue
            s = self._st(b)
            if s["r"].get(tok[0], 0) < tok[1]:
                s["r"][tok[0]] = tok[1]
        return tok

    def wait_bufs(self, eng, bufs):
        deps = {}
        for b in bufs:
            t = self._st(b)["w"]
            if t is not None and deps.get(t[0], 0) < t[1]:
                deps[t[0]] = t[1]
        waits = []
        for k, v in deps.items():
            if self.known[eng].get(k, 0) < v:
                self.known[eng][k] = v
                waits.append((k, v))
                if k[0] == "c":
                    self.waited[k[1]].add(v)
        self.q[eng].append((waits, None, None))

    def emit(self):
        nc = self.nc
        rank = {}
        for e in self.CENG:
            rank[e] = {v: self.cbase[e] + i + 1 for i, v in enumerate(sorted(self.waited[e]))}

        def replay(ename, eobj):
            for waits, fn, tok in self.q[ename]:
                for k, v in waits:
                    if k[0] == "c":
                        eobj.wait_ge(self.csem[k[1]], rank[k[1]][v])
                    else:
                        eobj.wait_ge(self.dsem[k[1]][k[2]], 16 * v)
                if fn is None:
                    continue
                ins = fn(eobj)
                if tok[0][0] == "c":
                    if tok[1] in rank[ename]:
                        ins.then_inc(self.csem[ename], 1)
                else:
                    ins.then_inc(self.dsem[tok[0][1]][tok[0][2]], 16)

        with nc.Block() as block:
            @block.tensor
            def _(e):
                replay("pe", e)

            @block.vector
            def _(e):
                replay("dve", e)

            @block.scalar
            def _(e):
                replay("act", e)

            @block.gpsimd
            def _(e):
                replay("pool", e)

            @block.sync
            def _(e):
                replay("sp", e)
        for e in self.CENG:
            self.cbase[e] += len(self.waited[e])
            self.waited[e] = set()
        for e in self.q:
            self.q[e] = []
        self.bufs = {}

    def flush(self):
        waits = []
        for key, cnt in self.dcnt.items():
            if self.known["sp"].get(key, 0) < cnt:
                self.known["sp"][key] = cnt
                waits.append((key, cnt))
        self.q["sp"].append((waits, None, None))

    def dma(self, out, in_, reads, writes, eng="sp", **kw):
        return self.op(eng, lambda e: e.dma_start(out=out, in_=in_, **kw), reads, writes, dma=True)

    def mm(self, out, lhsT, rhs, start, stop, reads, writes):
        return self.op("pe", lambda e: e.matmul(out, lhsT, rhs, start=start, stop=stop), reads, writes)

    def tr(self, out, in_, ident, reads, writes):
        return self.op("pe", lambda e: e.transpose(out, in_, ident), reads, writes)

    _uid = [0]

    def sb(self, name, shape, dt):
        self._uid[0] += 1
        return self.es.enter_context(self.nc.sbuf_tensor(f"s{self._uid[0]}_{name}", list(shape), dt))

    def ps(self, name, shape, dt=F32):
        self._uid[0] += 1
        return self.es.enter_context(self.nc.psum_tensor(f"p{self._uid[0]}_{name}", list(shape), dt))


def _divisor_le(n, cap):
    for d in range(min(n, cap), 0, -1):
        if n % d == 0:
            return d
    return 1


class Builder:
    def __init__(self, cfg, nlayers=None):
        self.c = cfg
        self.nc = bass.Bass("TRN2", target_bir_lowering=False)
        self.es = ExitStack()
        self.P = Prog(self.nc, self.es)
        self.dram = {}
        self.evac_rr = 0

    def din(self, name, shape, dt=F32):
        t = self.nc.dram_tensor(name, list(shape), dt, kind="ExternalInput").ap()
        self.dram[name] = t
        return t

    def dout(self, name, shape, dt=F32):
        t = self.nc.dram_tensor(name, list(shape), dt, kind="ExternalOutput").ap()
        self.dram[name] = t
        return t

    def dscr(self, name, shape, dt=F32):
        t = self.nc.dram_tensor(name, list(shape), dt, kind="Internal").ap()
        self.dram[name] = t
        return t

    def run_stage(self, fn, *a, **kw):
        outer = self.P.es
        with ExitStack() as st:
            self.P.es = st
            fn(*a, **kw)
            self.P.flush()
            self.P.emit()
        self.P.es = outer

    def evac_eng(self):
        self.evac_rr ^= 1
        return "dve" if self.evac_rr else "act"

    def copy(self, eng, out, in_, reads, writes):
        if eng == "act":
            return self.P.op("act", lambda e: e.copy(out=out, in_=in_), reads, writes)
        return self.P.op(eng, lambda e: e.tensor_copy(out=out, in_=in_), reads, writes)

    def st_init_h(self):
        c, P, d = self.c, self.P, self.dram
        P.dma(d["h"][0:c.L, :], d["ctx"][:, :], ["ctx"], ["h"])
        step = min(1024, c.S)
        for r in range(0, c.S, step):
            P.dma(d["h"][c.L + r:c.L + r + step, :], d["x"][r:r + step, :], ["x"], [("h", r)])

    def groups(self):
        c = self.c
        g = []
        nctx = c.L // 128
        for i in range(0, nctx, 4):
            g.append((1, list(range(i, min(nctx, i + 4)))))
        for i in range(nctx, c.NT, 4):
            g.append((0, list(range(i, min(c.NT, i + 4)))))
        return g

    def load_modT(self, l):
        c, P = self.c, self.P
        modT = P.sb("modT", [128, 2, 6, c.KC], F32)
        for cls in range(2):
            for j in range(6):
                P.dma(modT[:, cls, j, :], self.dram["mod"][l, cls, j, :].rearrange("(k p) -> p k", p=128),
                      ["mod"], ["modT"], allow_slow_non_contiguous=True)
        return modT

    def st_norm(self, l, which, final=False):
        c, P, d = self.c, self.P, self.dram
        ident = P.sb("ident", [128, 128], F32)
        P.dma(ident[:], d["ident"][:, :], ["identd"], ["ident"])
        modT = self.load_modT(l)
        onep = P.sb("onep", [128, 2, c.KC], F32)
        js, jb = (1, 0) if which == 0 else (4, 3)
        P.op("dve", lambda e: e.tensor_scalar(out=onep[:], in0=modT[:, :, js, :], scalar1=1.0, scalar2=None,
                                              op0=ALU.add), ["modT"], ["onep"])
        hs = [P.sb(f"hs{i}", [128, c.D], F32) for i in range(4)]
        junk = P.sb("junk", [128, c.D], BF16)
        ss = P.sb("ss", [128, 4], F32)
        rs = P.sb("rs", [128, 4], F32)
        aTg = [P.sb(f"aTg{i}", [128, c.KC, 512], BF16) for i in range(1)]
        pt = [P.ps(f"pt{i}", [128, 512]) for i in range(4)]
        pti = 0
        for gi, (cls, tiles) in enumerate(self.groups()):
            n = len(tiles)
            for j, ti in enumerate(tiles):
                P.dma(hs[j][:], d["h"][ti * 128:(ti + 1) * 128, :], [("h", ti)], [("hs", j)])
                P.op("dve", lambda e, j=j: e.memset(ss[:, j:j + 1], 0.0), [], [("ss", j)])
                P.op("act", lambda e, j=j: e.activation(out=junk[:], in_=hs[j][:], func=AF.Square,
                                                         accum_out=ss[:, j:j + 1]),
                     [("hs", j), ("ss", j)], ["junk", ("ss", j)])
                P.op("dve", lambda e, j=j: e.tensor_scalar(out=rs[:, j:j + 1], in0=ss[:, j:j + 1],
                                                           scalar1=1.0 / c.D, scalar2=1e-6,
                                                           op0=ALU.mult, op1=ALU.add),
                     [("ss", j)], [("rs", j)])
                P.op("act", lambda e, j=j: e.activation(out=rs[:, j:j + 1], in_=rs[:, j:j + 1], func=AF.Sqrt),
                     [("rs", j)], [("rs", j)])
                P.op("dve", lambda e, j=j: e.reciprocal(out=rs[:, j:j + 1], in_=rs[:, j:j + 1]),
                     [("rs", j)], [("rs", j)])
                P.op("act", lambda e, j=j: e.activation(out=hs[j][:], in_=hs[j][:], func=AF.Copy,
                                                         scale=rs[:, j:j + 1]),
                     [("hs", j), ("rs", j)], [("hs", j)])
            ab = aTg[0]
            for kc in range(c.KC):
                pb = pt[pti % 4]
                pbn = ("pt", pti % 4)
                pti += 1
                for j in range(n):
                    P.tr(pb[:, j * 128:(j + 1) * 128], hs[j][:, kc * 128:(kc + 1) * 128], ident[:],
                         [("hs", j), "ident"], [pbn])
                eng = self.evac_eng()
                if eng == "dve":
                    P.op("dve", lambda e, pb=pb, kc=kc, cls=cls, n=n: e.tensor_scalar(
                        out=ab[:, kc, 0:n * 128], in0=pb[:, 0:n * 128], scalar1=onep[:, cls, kc:kc + 1],
                        scalar2=modT[:, cls, jb, kc:kc + 1], op0=ALU.mult, op1=ALU.add),
                        [pbn, "onep", "modT"], [("aTg", kc)])
                else:
                    P.op("act", lambda e, pb=pb, kc=kc, cls=cls, n=n: e.activation(
                        out=ab[:, kc, 0:n * 128], in_=pb[:, 0:n * 128], func=AF.Identity,
                        scale=onep[:, cls, kc:kc + 1], bias=modT[:, cls, jb, kc:kc + 1]),
                        [pbn, "onep", "modT"], [("aTg", kc)])
            t0 = tiles[0] * 128
            P.dma(d["aT"][:, :, t0:t0 + n * 128].rearrange("k p t -> p k t"), ab[:, :, 0:n * 128],
                  [("aTg", kc) for kc in range(c.KC)], [("aT", ti) for ti in tiles])

    def dense_tok(self, srcT, kcn, W, col0, ncols, sink, wname, src_key, cw_max=512, tile_major=False):
        c, P = self.c, self.P
        wsb = [P.sb(f"wsb{i}", [128, kcn, cw_max], BF16) for i in range(2 if kcn <= 40 else 1)]
        TG = 4 if kcn <= 40 else 1
        asb = [P.sb(f"asb{i}", [128, kcn, 128 * TG], BF16) for i in range(2)]
        pd = [P.ps(f"pd{i}", [128, 512]) for i in range(3)]
        it = 0
        ig = 0
        for ci, c0 in enumerate(range(col0, col0 + ncols, cw_max)):
            cw = min(cw_max, col0 + ncols - c0)
            wb = wsb[ci % len(wsb)]
            wk = ("wsb", ci % len(wsb))
            kstep = 8
            for k0 in range(0, kcn, kstep):
                k1 = min(kcn, k0 + kstep)
                P.dma(wb[:, k0:k1, 0:cw],
                      W[k0 * 128:k1 * 128, c0:c0 + cw].rearrange("(k p) n -> p k n", p=128),
                      [wname], [wk + (k0,)], eng="pool")
            wkeys = [wk + (k0,) for k0 in range(0, kcn, kstep)]
            for tg in range(0, c.NT, TG):
                tn = min(TG, c.NT - tg)
                ab = asb[ig % 2]
                ak = ("asb", ig % 2)
                ig += 1
                if tile_major:
                    assert TG == 1
                    P.dma(ab[:, :, 0:128], srcT[tg], [(src_key, tg)], [ak])
                else:
                    P.dma(ab[:, :, 0:tn * 128], srcT[:, :, tg * 128:(tg + tn) * 128].rearrange("k p t -> p k t"),
                          [(src_key, tg + j) for j in range(tn)], [ak])
                for j in range(tn):
                    ti = tg + j
                    pb = pd[it % 3]
                    pk = ("pd", it % 3)
                    it += 1
                    for kc in range(kcn):
                        P.mm(pb[:, 0:cw], ab[:, kc, j * 128:(j + 1) * 128], wb[:, kc, 0:cw], kc == 0, kc == kcn - 1,
                             [ak, wk + ((kc // kstep) * kstep,)], [pk])
                    sink(ti, c0, cw, pb, pk)

    def st_inproj(self, l):
        c, P, d = self.c, self.P, self.dram
        W = d["w_in_even"] if l % 2 == 0 else d["w_in_odd"]
        ncols = c.COLS_EVEN if l % 2 == 0 else c.COLS_ODD
        ot = [P.sb(f"ot{i}", [128, 512], F32) for i in range(3)]
        cnt = [0]

        def sink(ti, c0, cw, pb, pk):
            i = cnt[0] % 3
            cnt[0] += 1
            self.copy(self.evac_eng(), ot[i][:, 0:cw], pb[:, 0:cw], [pk], [("ot", i)])
            P.dma(d["p"][ti * 128:(ti + 1) * 128, c0:c0 + cw], ot[i][:, 0:cw], [("ot", i)], [("p", ti, c0)])

        self.dense_tok(d["aT"], c.KC, W, 0, ncols, sink, "w_in", "aT")

    def make_resid_sink(self, l, jgate):
        c, P, d = self.c, self.P, self.dram
        mb = [P.sb(f"mb{cls}", [128, c.D], F32) for cls in range(2)]
        for cls in range(2):
            P.dma(mb[cls][:], d["mod"][l, cls, jgate, :].partition_broadcast(128), ["mod"], [("mb", cls)])
        hb = [P.sb(f"hb{i}", [128, 512], F32) for i in range(3)]
        tb = [P.sb(f"tb{i}", [128, 512], F32) for i in range(3)]
        cnt = [0]
        nctx = c.L // 128

        def sink(ti, c0, cw, pb, pk):
            i = cnt[0] % 3
            cnt[0] += 1
            cls = 1 if ti < nctx else 0
            rows = slice(ti * 128, (ti + 1) * 128)
            P.dma(hb[i][:, 0:cw], d["h"][rows, c0:c0 + cw], [("h", ti, c0)], [("hb", i)])
            P.op("dve", lambda e: e.tensor_tensor(out=tb[i][:, 0:cw], in0=pb[:, 0:cw], in1=mb[cls][:, c0:c0 + cw],
                                                  op=ALU.mult), [pk, ("mb", cls)], [("tb", i)])
            P.op("pool", lambda e: e.tensor_tensor(out=hb[i][:, 0:cw], in0=hb[i][:, 0:cw], in1=tb[i][:, 0:cw],
                                                   op=ALU.add), [("tb", i), ("hb", i)], [("hb", i)])
            P.dma(d["h"][rows, c0:c0 + cw], hb[i][:, 0:cw], [("hb", i)], [("h", ti, c0)])

        return sink

    def st_mixout(self, l):
        c, d = self.c, self.dram
        sink = self.make_resid_sink(l, 2)
        self.dense_tok(d["yT"], c.KC, d["w_mix_out"][l], 0, c.D, sink, "w_mo", "yT")

    def st_ffn_out(self, l):
        c, d = self.c, self.dram
        sink = self.make_resid_sink(l, 5)
        self.dense_tok(d["hidT"], c.FC, d["w_ffn_out"][l], 0, c.D, sink, "w_fo", "hidT", cw_max=512, tile_major=True)

    def st_ffn_in(self, l):
        c, P, d = self.c, self.P, self.dram
        W = d["w_ffn_in"][l]
        convT = P.sb("convT", [128, 3, c.FC], F32)
        cbT = P.sb("cbT", [128, c.FC], F32)
        for k in range(3):
            P.dma(convT[:, k, :], d["ffn_conv_w"][l, k, :].rearrange("(j p) -> p j", p=128), ["cw"], ["convT"],
                  allow_slow_non_contiguous=True)
        P.dma(cbT[:], d["ffn_conv_b"][l, :].rearrange("(j p) -> p j", p=128), ["cb"], ["cbT"],
              allow_slow_non_contiguous=True)
        segs = [(0, c.L), (c.L, c.S)]
        g = [P.sb(f"g{s}", [128, n + 2], F32) for s, (t0, n) in enumerate(segs)]
        u = [P.sb(f"u{s}", [128, n], F32) for s, (t0, n) in enumerate(segs)]
        t1 = P.sb("t1", [128, c.S], F32)
        hid = P.sb("hid", [128, c.NTOK], BF16)
        for s, (t0, n) in enumerate(segs):
            P.op("pool", lambda e, s=s: e.memset(g[s][:], 0.0), [], [("g", s)])
        wg = [P.sb(f"wg{i}", [128, c.KC, 128], BF16) for i in range(2)]
        wu = [P.sb(f"wu{i}", [128, c.KC, 128], BF16) for i in range(2)]
        fsb = [P.sb(f"fsb{i}", [128, c.KC, 512], BF16) for i in range(2)]
        pg = [P.ps(f"pg{i}", [128, 512]) for i in range(2)]
        pu = [P.ps(f"pu{i}", [128, 512]) for i in range(2)]
        it = 0
        for j in range(c.FC):
            wgb, wub = wg[j % 2], wu[j % 2]
            P.dma(wgb[:], W[:, j * 128:(j + 1) * 128].rearrange("(k p) n -> p k n", p=128), ["wfi"], [("wg", j % 2)],
                  eng="pool")
            P.dma(wub[:], W[:, c.DFF + j * 128:c.DFF + (j + 1) * 128].rearrange("(k p) n -> p k n", p=128),
                  ["wfi"], [("wu", j % 2)], eng="pool")
            for s, (t0, n) in enumerate(segs):
                for q0 in range(0, n, 512):
                    qn = min(512, n - q0)
                    fb = fsb[it % 2]
                    fk = ("fsb", it % 2)
                    pgb, pub = pg[it % 2], pu[it % 2]
                    pgk, puk = ("pg", it % 2), ("pu", it % 2)
                    it += 1
                    tiles = [(t0 + q0) // 128 + i for i in range(qn // 128)]
                    P.dma(fb[:, :, 0:qn], d["aT"][:, :, t0 + q0:t0 + q0 + qn].rearrange("k p t -> p k t"),
                          [("aT", ti) for ti in tiles], [fk])
                    for kc in range(c.KC):
                        P.mm(pgb[:, 0:qn], wgb[:, kc, :], fb[:, kc, 0:qn], kc == 0, kc == c.KC - 1,
                             [fk, ("wg", j % 2)], [pgk])
                    for kc in range(c.KC):
                        P.mm(pub[:, 0:qn], wub[:, kc, :], fb[:, kc, 0:qn], kc == 0, kc == c.KC - 1,
                             [fk, ("wu", j % 2)], [puk])
                    self.copy("act", g[s][:, 1 + q0:1 + q0 + qn], pgb[:, 0:qn], [pgk], [("g", s)])
                    self.copy("dve", u[s][:, q0:q0 + qn], pub[:, 0:qn], [puk], [("u", s)])
                tt = t1[:, 0:n]
                P.op("dve", lambda e, s=s, n=n, tt=tt, j=j: e.tensor_scalar(
                    out=tt, in0=g[s][:, 1:n + 1], scalar1=convT[:, 1, j:j + 1], scalar2=cbT[:, j:j + 1],
                    op0=ALU.mult, op1=ALU.add), [("g", s), "convT", "cbT"], ["t1"])
                P.op("dve", lambda e, s=s, n=n, tt=tt, j=j: e.scalar_tensor_tensor(
                    out=tt, in0=g[s][:, 0:n], scalar=convT[:, 0, j:j + 1], in1=tt, op0=ALU.mult, op1=ALU.add),
                    [("g", s), "convT", "t1"], ["t1"])
                P.op("dve", lambda e, s=s, n=n, tt=tt, j=j: e.scalar_tensor_tensor(
                    out=tt, in0=g[s][:, 2:n + 2], scalar=convT[:, 2, j:j + 1], in1=tt, op0=ALU.mult, op1=ALU.add),
                    [("g", s), "convT", "t1"], ["t1"])
                P.op("act", lambda e, tt=tt: e.activation(out=tt, in_=tt, func=AF.Silu), ["t1"], ["t1"])
                P.op("pool", lambda e, s=s, n=n, tt=tt, t0=t0: e.tensor_tensor(
                    out=hid[:, t0:t0 + n], in0=tt, in1=u[s][:, 0:n], op=ALU.mult),
                    ["t1", ("u", s)], ["hid"])
            P.dma(d["hidT"][:, :, j, :].rearrange("t p k -> p t k"), hid[:].rearrange("p (t k) -> p t k", k=128),
                  ["hid"], [("hidT", ti) for ti in range(c.NT)])

    def st_final(self):
        c, P, d = self.c, self.P, self.dram
        fnb = P.sb("fnb", [128, c.D], F32)
        P.dma(fnb[:], d["final_norm"][:].partition_broadcast(128), ["fn"], ["fnb"])
        hs = [P.sb(f"hs{i}", [128, c.D], F32) for i in range(2)]
        junk = P.sb("junk", [128, c.D], BF16)
        ss = P.sb("ss", [128, 2], F32)
        rs = P.sb("rs", [128, 2], F32)
        nctx = c.L // 128
        for i, ti in enumerate(range(nctx, c.NT)):
            j = i % 2
            P.dma(hs[j][:], d["h"][ti * 128:(ti + 1) * 128, :], [("h", ti)], [("hs", j)])
            P.op("dve", lambda e, j=j: e.memset(ss[:, j:j + 1], 0.0), [], [("ss", j)])
            P.op("act", lambda e, j=j: e.activation(out=junk[:], in_=hs[j][:], func=AF.Square,
                                                     accum_out=ss[:, j:j + 1]),
                 [("hs", j), ("ss", j)], ["junk", ("ss", j)])
            P.op("dve", lambda e, j=j: e.tensor_scalar(out=rs[:, j:j + 1], in0=ss[:, j:j + 1],
                                                       scalar1=1.0 / c.D, scalar2=1e-6,
                                                       op0=ALU.mult, op1=ALU.add), [("ss", j)], [("rs", j)])
            P.op("act", lambda e, j=j: e.activation(out=rs[:, j:j + 1], in_=rs[:, j:j + 1], func=AF.Sqrt),
                 [("rs", j)], [("rs", j)])
            P.op("dve", lambda e, j=j: e.reciprocal(out=rs[:, j:j + 1], in_=rs[:, j:j + 1]),
                 [("rs", j)], [("rs", j)])
            P.op("dve", lambda e, j=j: e.scalar_tensor_tensor(out=hs[j][:], in0=hs[j][:], scalar=rs[:, j:j + 1],
                                                              in1=fnb[:], op0=ALU.mult, op1=ALU.mult),
                 [("hs", j), ("rs", j), "fnb"], [("hs", j)])
            P.dma(d["out"][(ti - nctx) * 128:(ti - nctx + 1) * 128, :], hs[j][:], [("hs", j)], [("out", ti)])

    def prep_featmajor(self, tiles, c0, ncols, dstT, dkey, rope=None, hd=64, func=None, identb=None):
        c, P, d = self.c, self.P, self.dram
        nch = ncols // 128
        xt = [P.sb(f"xt{i}", [128, ncols], F32) for i in range(2)]
        xb = [P.sb(f"xb{i}", [128, ncols], BF16) for i in range(2)]
        half = ncols // 2
        if rope is not None:
            ct = [P.sb(f"ct{i}", [128, half], F32) for i in range(2)]
            st = [P.sb(f"st{i}", [128, half], F32) for i in range(2)]
            ta = P.sb("ropeA", [128, half], F32)
            tb = P.sb("ropeB", [128, half], F32)
        pt = [P.ps(f"ptb{i}", [128, 512], BF16) for i in range(2)]
        ev = [P.sb(f"ev{i}", [128, 512], BF16) for i in range(2)]
        n4 = 0
        for i, ti in enumerate(tiles):
            j = i % 2
            rows = slice(ti * 128, (ti + 1) * 128)
            P.dma(xt[j][:], d["p"][rows, c0:c0 + ncols], [("p", ti)], [("xt", j)])
            roped = rope is not None and rope[2](ti) is not None
            if roped:
                r0 = rope[2](ti)
                P.dma(ct[j][:], d[rope[0]][r0:r0 + 128, 0:half], ["ropetab"], [("ct", j)])
                P.dma(st[j][:], d[rope[1]][r0:r0 + 128, 0:half], ["ropetab"], [("st", j)])
                xv = xt[j][:].rearrange("p (g two e) -> p g two e", two=2, e=hd // 2)
                ov = xb[j][:].rearrange("p (g two e) -> p g two e", two=2, e=hd // 2)
                cv = ct[j][:].rearrange("p (g e) -> p g e", e=hd // 2)
                sv = st[j][:].rearrange("p (g e) -> p g e", e=hd // 2)
                tav = ta[:].rearrange("p (g e) -> p g e", e=hd // 2)
                tbv = tb[:].rearrange("p (g e) -> p g e", e=hd // 2)
                rk = [("xt", j), ("ct", j), ("st", j)]
                P.op("dve", lambda e, xv=xv, cv=cv: e.tensor_tensor(out=tav, in0=xv[:, :, 0, :], in1=cv, op=ALU.mult),
                     rk, ["ropeA"])
                P.op("pool", lambda e, xv=xv, sv=sv: e.tensor_tensor(out=tbv, in0=xv[:, :, 1, :], in1=sv, op=ALU.mult),
                     rk, ["ropeB"])
                P.op("dve", lambda e, ov=ov: e.tensor_tensor(out=ov[:, :, 0, :], in0=tav, in1=tbv, op=ALU.subtract),
                     ["ropeA", "ropeB"], [("xb", j)])
                P.op("dve", lambda e, xv=xv, cv=cv: e.tensor_tensor(out=tav, in0=xv[:, :, 1, :], in1=cv, op=ALU.mult),
                     rk + [("xb", j)], ["ropeA"])
                P.op("pool", lambda e, xv=xv, sv=sv: e.tensor_tensor(out=tbv, in0=xv[:, :, 0, :], in1=sv, op=ALU.mult),
                     rk + [("xb", j)], ["ropeB"])
                P.op("dve", lambda e, ov=ov: e.tensor_tensor(out=ov[:, :, 1, :], in0=tav, in1=tbv, op=ALU.add),
                     ["ropeA", "ropeB"], [("xb", j)])
            elif func is not None:
                P.op("act", lambda e, j=j: e.activation(out=xb[j][:], in_=xt[j][:], func=func), [("xt", j)], [("xb", j)])
            else:
                self.copy("act", xb[j][:], xt[j][:], [("xt", j)], [("xb", j)])
            for q0 in range(0, nch, 4):
                qn = min(4, nch - q0)
                pb, pk = pt[n4 % 2], ("ptb", n4 % 2)
                eb, ek = ev[n4 % 2], ("ev", n4 % 2)
                n4 += 1
                for q in range(qn):
                    P.tr(pb[:, q * 128:(q + 1) * 128], xb[j][:, (q0 + q) * 128:(q0 + q + 1) * 128], identb[:],
                         [("xb", j), "identb"], [pk])
                self.copy(self.evac_eng(), eb[:, 0:qn * 128], pb[:, 0:qn * 128], [pk], [ek])
                P.dma(d[dstT][q0:q0 + qn, :, ti * 128:(ti + 1) * 128].rearrange("k p t -> p k t"),
                      eb[:, 0:qn * 128].rearrange("p (k t) -> p k t", t=128), [ek], [(dkey, ti)])

    def prep_tokmajor(self, tiles, c0, ncols, dst, dkey):
        c, P, d = self.c, self.P, self.dram
        xt = [P.sb(f"vt{i}", [128, ncols], F32) for i in range(2)]
        xb = [P.sb(f"vb{i}", [128, ncols], BF16) for i in range(2)]
        for i, ti in enumerate(tiles):
            j = i % 2
            rows = slice(ti * 128, (ti + 1) * 128)
            P.dma(xt[j][:], d["p"][rows, c0:c0 + ncols], [("p", ti)], [("vt", j)])
            self.copy("pool", xb[j][:], xt[j][:], [("vt", j)], [("vb", j)])
            P.dma(d[dst][rows, :], xb[j][:], [("vb", j)], [(dkey, ti)])

    def load_identb(self):
        P = self.P
        identb = P.sb("identb", [128, 128], BF16)
        P.dma(identb[:], self.dram["identb"][:, :], ["identbd"], ["identb"])
        return identb

    def st_diff_prep(self):
        c = self.c
        identb = self.load_identb()
        nctx = c.L // 128
        cb = c.COLS_A
        rope = ("ropecos", "ropesin", lambda ti: None if ti < nctx else (ti - nctx) * 128)
        self.prep_featmajor(range(c.NT), cb, 2 * c.HALF, "dqkT", "dqkT", rope=rope, hd=64, identb=identb)

    def st_diff_prepv(self):
        c = self.c
        self.prep_tokmajor(range(c.NT), c.COLS_A + 2 * c.HALF, c.HALF, "dV", "dV")

    def st_diff_core(self, l, e):
        c, P, d = self.c, self.P, self.dram
        H = c.B_HEADS
        scale = 64 ** -0.5
        lam_init = 0.8 - 0.6 * math.exp(-0.3 * l)
        lv = P.sb("lv", [128, 4, 64], F32)
        P.dma(lv[:].rearrange("p a b -> p (a b)"), d["diff_lambda"][e].rearrange("a b -> (a b)").partition_broadcast(128),
              ["dl"], ["lv"])
        lt = P.sb("lt", [128, 2, 64], F32)
        ls = P.sb("ls", [128, 2], F32)
        nlam = P.sb("nlam", [128, 1], F32)
        P.op("dve", lambda e_: e_.tensor_tensor(out=lt[:, 0, :], in0=lv[:, 0, :], in1=lv[:, 1, :], op=ALU.mult), ["lv"], ["lt0"])
        P.op("dve", lambda e_: e_.tensor_tensor(out=lt[:, 1, :], in0=lv[:, 2, :], in1=lv[:, 3, :], op=ALU.mult), ["lv"], ["lt1"])
        P.op("dve", lambda e_: e_.tensor_reduce(out=ls[:], in_=lt[:], axis=AX.X, op=ALU.add), ["lt0", "lt1"], ["ls"])
        P.op("act", lambda e_: e_.activation(out=ls[:], in_=ls[:], func=AF.Exp), ["ls"], ["ls"])
        P.op("dve", lambda e_: e_.tensor_tensor(out=nlam[:], in0=ls[:, 1:2], in1=ls[:, 0:1], op=ALU.subtract), ["ls"], ["nlam"])
        P.op("dve", lambda e_: e_.tensor_scalar(out=nlam[:], in0=nlam[:], scalar1=-lam_init, scalar2=None, op0=ALU.add),
             ["nlam"], ["nlam"])
        sub = P.sb("subln", [128, 1], F32)
        P.dma(sub[:], d["diff_subln"][e].rearrange("(p o) -> p o", o=1), ["dsl"], ["subln"])
        P.op("dve", lambda e_: e_.tensor_scalar(out=sub[:], in0=sub[:], scalar1=1.0 - lam_init, scalar2=None, op0=ALU.mult),
             ["subln"], ["subln"])
        onesb = P.sb("onesb", [128, 128], BF16)
        onesf = P.sb("onesf", [128, 128], F32)
        P.op("pool", lambda e_: e_.memset(onesb[:], 1.0), [], ["onesb"])
        P.op("pool", lambda e_: e_.memset(onesf[:], 1.0 / 128.0), [], ["onesf"])
        qs = [P.sb(f"qs{i}", [64, c.NTOK], BF16) for i in range(2)]
        ks = [P.sb(f"ks{i}", [64, c.NTOK], BF16) for i in range(2)]
        vs = P.sb("vs", [128, c.NT, 128], BF16)
        pT = [P.sb(f"pT{i}", [128, 512], BF16) for i in range(4)]
        ps_s = [P.ps(f"pss{i}", [128, 512]) for i in range(2)]
        ps_o = [P.ps(f"pso{i}", [128, 512]) for i in range(2)]
        ps_z = [P.ps(f"psz{i}", [128, 512]) for i in range(2)]
        ps_n = P.ps("psn", [128, 512])
        r = [P.sb(f"r{i}", [128, 512], F32) for i in range(2)]
        A = [P.sb(f"A{i}", [128, 512], F32) for i in range(2)]
        Y = P.sb("Y", [128, 512], F32)
        Y2 = P.sb("Y2", [128, 512], F32)
        yo = P.sb("yo", [128, 512], BF16)
        nctx = c.L // 128
        QH = c.HALF // 128
        it = 0
        for h in range(H):
            for sm in range(2):
                col = h * 128 + sm * 64
                cc, p0 = col // 128, col % 128
                P.dma(qs[sm][:], d["dqkT"][cc, p0:p0 + 64, :], [("dqkT", ti) for ti in range(c.NT)], [("qs", sm)])
                P.dma(ks[sm][:], d["dqkT"][QH + cc, p0:p0 + 64, :], [("dqkT", ti) for ti in range(c.NT)], [("ks", sm)])
            P.dma(vs[:], d["dV"][:, h * 128:(h + 1) * 128].rearrange("(t p) v -> p t v", p=128),
                  [("dV", ti) for ti in range(c.NT)], ["vs"])
            qchunks = [(0, c.L, list(range(nctx)))]
            allk = list(range(nctx, c.NT)) + list(range(nctx))
            for q0 in range(c.L, c.NTOK, 512):
                qchunks.append((q0, min(512, c.NTOK - q0), allk))
            for (q0, qn, ktiles) in qchunks:
                for sm in range(2):
                    for ki, kt in enumerate(ktiles):
                        sb_, sk = ps_s[it % 2], ("pss", it % 2)
                        pb, pk = pT[it % 4], ("pT", it % 4)
                        it += 1
                        P.mm(sb_[:, 0:qn], ks[sm][:, kt * 128:(kt + 1) * 128], qs[sm][:, q0:q0 + qn], True, True,
                             [("ks", sm), ("qs", sm)], [sk])
                        P.op("act", lambda e_, pb=pb, sb_=sb_, qn=qn: e_.activation(out=pb[:, 0:qn], in_=sb_[:, 0:qn],
                                                                                 func=AF.Exp, scale=scale),
                             [sk], [pk])
                        first, last = ki == 0, ki == len(ktiles) - 1
                        P.mm(ps_o[sm][:, 0:qn], vs[:, kt, :], pb[:, 0:qn], first, last, ["vs", pk], [("pso", sm)])
                        P.mm(ps_z[sm][:, 0:qn], onesb[:], pb[:, 0:qn], first, last, ["onesb", pk], [("psz", sm)])
                    P.op("dve", lambda e_, sm=sm, qn=qn: e_.reciprocal(out=r[sm][:, 0:qn], in_=ps_z[sm][:, 0:qn]),
                         [("psz", sm)], [("r", sm)])
                    P.op("dve", lambda e_, sm=sm, qn=qn: e_.tensor_tensor(out=A[sm][:, 0:qn], in0=ps_o[sm][:, 0:qn],
                                                                       in1=r[sm][:, 0:qn], op=ALU.mult),
                         [("pso", sm), ("r", sm)], [("A", sm)])
                P.op("dve", lambda e_, qn=qn: e_.scalar_tensor_tensor(out=Y[:, 0:qn], in0=A[1][:, 0:qn], scalar=nlam[:, 0:1],
                                                                    in1=A[0][:, 0:qn], op0=ALU.mult, op1=ALU.add),
                     [("A", 0), ("A", 1), "nlam"], ["Y"])
                P.op("pool", lambda e_, qn=qn: e_.tensor_tensor(out=Y2[:, 0:qn], in0=Y[:, 0:qn], in1=Y[:, 0:qn], op=ALU.mult),
                     ["Y"], ["Y2"])
                P.mm(ps_n[:, 0:qn], onesf[:], Y2[:, 0:qn], True, True, ["onesf", "Y2"], ["psn"])
                P.op("dve", lambda e_, qn=qn: e_.tensor_scalar(out=Y2[:, 0:qn], in0=ps_n[:, 0:qn], scalar1=1e-5, scalar2=None,
                                                             op0=ALU.add), ["psn"], ["Y2"])
                P.op("act", lambda e_, qn=qn: e_.activation(out=Y2[:, 0:qn], in_=Y2[:, 0:qn], func=AF.Sqrt), ["Y2"], ["Y2"])
                P.op("dve", lambda e_, qn=qn: e_.reciprocal(out=Y2[:, 0:qn], in_=Y2[:, 0:qn]), ["Y2"], ["Y2"])
                P.op("dve", lambda e_, qn=qn: e_.scalar_tensor_tensor(out=yo[:, 0:qn], in0=Y[:, 0:qn], scalar=sub[:, 0:1],
                                                                    in1=Y2[:, 0:qn], op0=ALU.mult, op1=ALU.mult),
                     ["Y", "Y2", "subln"], ["yo"])
                P.dma(d["yT"][c.HALF // 128 + h, :, q0:q0 + qn], yo[:, 0:qn], ["yo"], [("yT", h, q0)])

    def st_swa_prep(self):
        c = self.c
        identb = self.load_identb()
        nctx = c.L // 128
        rope = ("ropecos", "ropesin", lambda ti: None if ti < nctx else (ti - nctx) * 128)
        self.prep_featmajor(range(c.NT), 0, c.HALF + c.C_KV * 64, "sqkT", "sqkT", rope=rope, hd=64, identb=identb)

    def st_swa_prepv(self):
        c = self.c
        self.prep_tokmajor(range(c.NT), c.HALF + c.C_KV * 64, c.C_KV * 64, "sV", "sV")

    def st_swa_core(self, o):
        c, P, d = self.c, self.P, self.dram
        scale = 64 ** -0.5
        nctx = c.L // 128
        nb = c.S // 128
        esink = P.sb("esink", [64, c.C_HEADS], F32)
        P.dma(esink[:], d["swa_sink"][o].partition_broadcast(64), ["sink"], ["esink"])
        P.op("act", lambda e_: e_.activation(out=esink[:], in_=esink[:], func=AF.Exp), ["esink"], ["esink"])
        maskW = P.sb("maskW", [128, 6, 512], BF16)
        P.dma(maskW[:], d["maskW"].rearrange("m p f -> p m f"), ["maskWd"], ["maskW"])
        onesb = P.sb("onesb", [128, 64], BF16)
        P.op("pool", lambda e_: e_.memset(onesb[:], 1.0), [], ["onesb"])
        qs = [P.sb(f"qs{i}", [64, c.S], BF16) for i in range(2)]
        ks = P.sb("ks", [64, c.NTOK], BF16)
        vs = P.sb("vs", [128, c.NT, 64], BF16)
        pT = [P.sb(f"pT{i}", [128, 512], BF16) for i in range(4)]
        ps_s = [P.ps(f"pss{i}", [128, 512]) for i in range(2)]
        ps_o = [P.ps(f"pso{i}", [64, 512]) for i in range(2)]
        ps_z = [P.ps(f"psz{i}", [64, 512]) for i in range(2)]
        r = [P.sb(f"r{i}", [64, 512], F32) for i in range(2)]
        yo = [P.sb(f"yo{i}", [64, 512], BF16) for i in range(2)]
        allt = [("sqkT", ti) for ti in range(c.NT)]
        it = 0
        ic = 0
        for hq in range(c.C_HEADS):
            g = hq // 8
            if hq % 8 == 0:
                colk = c.HALF + g * 64
                P.dma(ks[:], d["sqkT"][colk // 128, colk % 128:colk % 128 + 64, :], allt, ["ks"])
                P.dma(vs[:], d["sV"][:, g * 64:(g + 1) * 64].rearrange("(t p) v -> p t v", p=128),
                      [("sV", ti) for ti in range(c.NT)], ["vs"])
            qb = qs[hq % 2]
            qk = ("qs", hq % 2)
            colq = hq * 64
            P.dma(qb[:], d["sqkT"][colq // 128, colq % 128:colq % 128 + 64, c.L:], allt, [qk])
            for t0 in range(0, c.S, 512):
                qn = min(512, c.S - t0)
                qb0 = t0 // 128
                kts = []
                for m in range(6):
                    kb = qb0 + m - 1
                    if 0 <= kb < nb and kb * 128 - 128 <= t0 + qn - 1:
                        kts.append((nctx + kb, m))
                kts += [(kt, None) for kt in range(nctx)]
                po, pok = ps_o[ic % 2], ("pso", ic % 2)
                pz, pzk = ps_z[ic % 2], ("psz", ic % 2)
                rb, rk = r[ic % 2], ("r", ic % 2)
                yb, yk = yo[ic % 2], ("yo", ic % 2)
                ic += 1
                for ki, (kt, m) in enumerate(kts):
                    sb_, sk = ps_s[it % 2], ("pss", it % 2)
                    pb, pk = pT[it % 4], ("pT", it % 4)
                    it += 1
                    P.mm(sb_[:, 0:qn], ks[:, kt * 128:(kt + 1) * 128], qb[:, t0:t0 + qn], True, True, ["ks", qk], [sk])
                    P.op("act", lambda e_, pb=pb, sb_=sb_, qn=qn: e_.activation(out=pb[:, 0:qn], in_=sb_[:, 0:qn],
                                                                             func=AF.Exp, scale=scale), [sk], [pk])
                    if m is not None:
                        P.op("dve", lambda e_, pb=pb, m=m, qn=qn: e_.tensor_tensor(out=pb[:, 0:qn], in0=pb[:, 0:qn],
                                                                                in1=maskW[:, m, 0:qn], op=ALU.mult),
                             [pk, "maskW"], [pk])
                    first, last = ki == 0, ki == len(kts) - 1
                    P.mm(po[:, 0:qn], vs[:, kt, :], pb[:, 0:qn], first, last, ["vs", pk], [pok])
                    P.mm(pz[:, 0:qn], onesb[:], pb[:, 0:qn], first, last, ["onesb", pk], [pzk])
                P.op("dve", lambda e_, rb=rb, pz=pz, qn=qn, hq=hq: e_.tensor_scalar(
                    out=rb[:, 0:qn], in0=pz[:, 0:qn], scalar1=esink[:, hq:hq + 1], scalar2=None, op0=ALU.add),
                    [pzk, "esink"], [rk])
                P.op("dve", lambda e_, rb=rb, qn=qn: e_.reciprocal(out=rb[:, 0:qn], in_=rb[:, 0:qn]), [rk], [rk])
                P.op("dve", lambda e_, rb=rb, po=po, yb=yb, qn=qn: e_.tensor_tensor(
                    out=yb[:, 0:qn], in0=po[:, 0:qn], in1=rb[:, 0:qn], op=ALU.mult), [pok, rk], [yk])
                P.dma(d["yT"][colq // 128, colq % 128:colq % 128 + 64, c.L + t0:c.L + t0 + qn], yb[:, 0:qn],
                      [yk], [("yT", hq, t0)])

    def st_ret_prep(self):
        c = self.c
        identb = self.load_identb()
        rope = ("rrcos", "rrsin", lambda ti: ti * 128)
        self.prep_featmajor(range(c.NT), c.COLS_C, 2 * c.D_HEADS * 128, "rqkT", "rqkT", rope=rope, hd=128,
                            identb=identb)

    def st_ret_prepv(self):
        c = self.c
        self.prep_tokmajor(range(c.NT), c.COLS_C + 2 * c.D_HEADS * 128, c.HALF, "rV", "rV")

    def st_ret_prepg(self):
        c = self.c
        identb = self.load_identb()
        nctx = c.L // 128
        self.prep_featmajor(range(nctx, c.NT), c.COLS_C + 2 * c.D_HEADS * 128 + c.HALF, c.HALF, "rgT", "rgT",
                            func=AF.Silu, identb=identb)

    def st_ret_core(self, o):
        c, P, d = self.c, self.P, self.dram
        nctx = c.L // 128
        H = c.D_HEADS
        lnscale = math.log(128 ** -0.5)
        lg = P.sb("lg", [128, 2, H], F32)
        nlg = P.sb("nlg", [128, 2, H], F32)
        P.dma(lg[:].rearrange("p a b -> p (a b)"), d["ret_decay"][o].rearrange("a b -> (a b)").partition_broadcast(128),
              ["rd"], ["lg"])
        P.op("act", lambda e_: e_.activation(out=lg[:], in_=lg[:], func=AF.Exp, scale=-math.log(2.0)), ["lg"], ["lg"])
        P.op("dve", lambda e_: e_.tensor_scalar(out=lg[:], in0=lg[:], scalar1=-1.0, scalar2=1.0, op0=ALU.mult, op1=ALU.add),
             ["lg"], ["lg"])
        P.op("act", lambda e_: e_.activation(out=lg[:], in_=lg[:], func=AF.Ln), ["lg"], ["lg"])
        P.op("dve", lambda e_: e_.tensor_scalar(out=nlg[:], in0=lg[:], scalar1=-1.0, scalar2=None, op0=ALU.mult),
             ["lg"], ["nlg"])
        T0 = P.sb("T0", [128, 512], F32)
        P.dma(T0[:], d["retT0"][:, :], ["retT0"], ["T0"])
        Mge = P.sb("Mge", [128, 4, 512], F32)
        Mle = P.sb("Mle", [128, 4, 512], F32)
        P.dma(Mge[:], d["retMge"].rearrange("m p f -> p m f"), ["retM"], ["Mge"])
        P.dma(Mle[:], d["retMle"].rearrange("m p f -> p m f"), ["retM"], ["Mle"])
        onesf = P.sb("onesf", [128, 128], F32)
        P.op("pool", lambda e_: e_.memset(onesf[:], 1.0 / 256.0), [], ["onesf"])
        qs = P.sb("qs", [128, c.S], BF16)
        ks = P.sb("ks", [128, c.NTOK], BF16)
        vs = P.sb("vs", [128, c.NT, 256], BF16)
        gs = [P.sb(f"gs{i}", [128, 512], BF16) for i in range(2)]
        Wt = [P.sb(f"Wt{i}", [128, 512], F32) for i in range(3)]
        Wu = [P.sb(f"Wu{i}", [128, 512], F32) for i in range(2)]
        bc = [P.sb(f"bc{i}", [128, 2], F32) for i in range(4)]
        pT = [P.sb(f"pT{i}", [128, 512], BF16) for i in range(3)]
        ps_s = [P.ps(f"pss{i}", [128, 512]) for i in range(2)]
        ps_o = [P.ps(f"pso{i}", [128, 512]) for i in range(2)]
        ps_n = P.ps("psn", [128, 512])
        Ys = [P.sb(f"Ys{i}", [128, 512], F32) for i in range(2)]
        Y2 = [P.sb(f"Y2{i}", [128, 512], F32) for i in range(2)]
        rst = P.sb("rst", [128, 512], F32)
        yo = [P.sb(f"yo{i}", [128, 512], BF16) for i in range(2)]
        allt = [("rqkT", ti) for ti in range(c.NT)]
        it = 0
        for hd in range(H):
            P.dma(qs[:], d["rqkT"][hd, :, c.L:], allt, ["qs"])
            P.dma(ks[:], d["rqkT"][H + hd, :, :], allt, ["ks"])
            P.dma(vs[:], d["rV"][:, hd * 256:(hd + 1) * 256].rearrange("(t p) v -> p t v", p=128),
                  [("rV", ti) for ti in range(c.NT)], ["vs"])
            lgf, lgb = lg[:, 0, hd:hd + 1], lg[:, 1, hd:hd + 1]
            nlgb = nlg[:, 1, hd:hd + 1]
            for t0 in range(0, c.S, 512):
                qn = min(512, c.S - t0)
                for kt in range(c.NT):
                    wb, wk = Wt[it % 3], ("Wt", it % 3)
                    b_, bk = bc[it % 4], ("bc", it % 4)
                    sb_, sk = ps_s[it % 2], ("pss", it % 2)
                    pb, pk = pT[it % 3], ("pT", it % 3)
                    it += 1

                    def bias(col, lgap, off, b_=b_, bk=bk):
                        P.op("dve", lambda e_: e_.tensor_scalar(out=b_[:, col:col + 1], in0=lgap, scalar1=float(off),
                                                                scalar2=lnscale, op0=ALU.mult, op1=ALU.add),
                             ["lg", "nlg"], [bk + (col,)])

                    def expw(out, scale_ap, col, b_=b_, bk=bk, qn=qn):
                        P.op("act", lambda e_: e_.activation(out=out[:, 0:qn], in_=T0[:, 0:qn], func=AF.Exp,
                                                             scale=scale_ap, bias=b_[:, col:col + 1]),
                             ["T0", "lg", "nlg", bk + (col,)], [])

                    if kt < nctx:
                        j0 = kt * 128
                        bias(0, lgf, c.L + t0 - j0)
                        bias(1, lgb, c.S - t0 + j0)
                        u0, u0k = Wu[0], ("Wu", 0)
                        P.op("act", lambda e_, wb=wb, b_=b_, qn=qn, lgf=lgf, nlgb=nlgb: e_.activation(
                            out=wb[:, 0:qn], in_=T0[:, 0:qn], func=AF.Exp, scale=lgf, bias=b_[:, 0:1]),
                            ["T0", "lg", bk + (0,)], [wk])
                        P.op("act", lambda e_, u0=u0, b_=b_, qn=qn, lgf=lgf, nlgb=nlgb: e_.activation(
                            out=u0[:, 0:qn], in_=T0[:, 0:qn], func=AF.Exp, scale=nlgb, bias=b_[:, 1:2]),
                            ["T0", "nlg", bk + (1,)], [u0k])
                        P.op("pool", lambda e_, wb=wb, u0=u0, qn=qn: e_.tensor_tensor(
                            out=wb[:, 0:qn], in0=wb[:, 0:qn], in1=u0[:, 0:qn], op=ALU.add), [wk, u0k], [wk])
                    else:
                        k0 = (kt - nctx) * 128
                        off = t0 - k0
                        if off >= 128:
                            bias(0, lgf, off)
                            P.op("act", lambda e_, wb=wb, b_=b_, qn=qn, lgf=lgf, nlgb=nlgb: e_.activation(
                                out=wb[:, 0:qn], in_=T0[:, 0:qn], func=AF.Exp, scale=lgf, bias=b_[:, 0:1]),
                                ["T0", "lg", bk + (0,)], [wk])
                        elif off <= -512:
                            bias(0, nlgb, off)
                            P.op("act", lambda e_, wb=wb, b_=b_, qn=qn, lgf=lgf, nlgb=nlgb: e_.activation(
                                out=wb[:, 0:qn], in_=T0[:, 0:qn], func=AF.Exp, scale=nlgb, bias=b_[:, 0:1]),
                                ["T0", "nlg", bk + (0,)], [wk])
                        else:
                            di = (-off) // 128
                            bias(0, lgf, off)
                            bias(1, nlgb, off)
                            u0, u0k = Wu[0], ("Wu", 0)
                            u1, u1k = Wu[1], ("Wu", 1)
                            P.op("act", lambda e_, u0=u0, b_=b_, qn=qn, lgf=lgf, nlgb=nlgb: e_.activation(
                                out=u0[:, 0:qn], in_=T0[:, 0:qn], func=AF.Exp, scale=lgf, bias=b_[:, 0:1]),
                                ["T0", "lg", bk + (0,)], [u0k])
                            P.op("act", lambda e_, u1=u1, b_=b_, qn=qn, lgf=lgf, nlgb=nlgb: e_.activation(
                                out=u1[:, 0:qn], in_=T0[:, 0:qn], func=AF.Exp, scale=nlgb, bias=b_[:, 1:2]),
                                ["T0", "nlg", bk + (1,)], [u1k])
                            P.op("dve", lambda e_, u0=u0, di=di, qn=qn: e_.tensor_tensor(
                                out=u0[:, 0:qn], in0=u0[:, 0:qn], in1=Mge[:, di, 0:qn], op=ALU.mult), [u0k, "Mge"], [u0k])
                            P.op("pool", lambda e_, u1=u1, di=di, qn=qn: e_.tensor_tensor(
                                out=u1[:, 0:qn], in0=u1[:, 0:qn], in1=Mle[:, di, 0:qn], op=ALU.mult), [u1k, "Mle"], [u1k])
                            P.op("dve", lambda e_, wb=wb, u0=u0, u1=u1, qn=qn: e_.tensor_tensor(
                                out=wb[:, 0:qn], in0=u0[:, 0:qn], in1=u1[:, 0:qn], op=ALU.add), [u0k, u1k], [wk])
                    if "dbgW" in d and hd == 0 and t0 == 0:
                        P.dma(d["dbgW"][kt, :, 0:qn], wb[:, 0:qn], [wk], [("dbgW", kt)])
                    P.mm(sb_[:, 0:qn], ks[:, kt * 128:(kt + 1) * 128], qs[:, t0:t0 + qn], True, True, ["ks", "qs"], [sk])
                    P.op("dve", lambda e_, pb=pb, sb_=sb_, wb=wb, qn=qn: e_.tensor_tensor(
                        out=pb[:, 0:qn], in0=sb_[:, 0:qn], in1=wb[:, 0:qn], op=ALU.mult), [sk, wk], [pk])
                    first, last = kt == 0, kt == c.NT - 1
                    P.mm(ps_o[0][:, 0:qn], vs[:, kt, 0:128], pb[:, 0:qn], first, last, ["vs", pk], [("pso", 0)])
                    P.mm(ps_o[1][:, 0:qn], vs[:, kt, 128:256], pb[:, 0:qn], first, last, ["vs", pk], [("pso", 1)])
                for hf in range(2):
                    self.copy("act", Ys[hf][:, 0:qn], ps_o[hf][:, 0:qn], [("pso", hf)], [("Ys", hf)])
                    P.op("pool", lambda e_, hf=hf, qn=qn: e_.tensor_tensor(
                        out=Y2[hf][:, 0:qn], in0=Ys[hf][:, 0:qn], in1=Ys[hf][:, 0:qn], op=ALU.mult),
                        [("Ys", hf)], [("Y2", hf)])
                    P.mm(ps_n[:, 0:qn], onesf[:], Y2[hf][:, 0:qn], hf == 0, hf == 1, ["onesf", ("Y2", hf)], ["psn"])
                P.op("dve", lambda e_, qn=qn: e_.tensor_scalar(out=rst[:, 0:qn], in0=ps_n[:, 0:qn], scalar1=1e-6,
                                                             scalar2=None, op0=ALU.add), ["psn"], ["rst"])
                P.op("act", lambda e_, qn=qn: e_.activation(out=rst[:, 0:qn], in_=rst[:, 0:qn], func=AF.Sqrt),
                     ["rst"], ["rst"])
                P.op("dve", lambda e_, qn=qn: e_.reciprocal(out=rst[:, 0:qn], in_=rst[:, 0:qn]), ["rst"], ["rst"])
                for hf in range(2):
                    ch = hd * 2 + hf
                    P.dma(gs[hf][:, 0:qn], d["rgT"][ch, :, c.L + t0:c.L + t0 + qn],
                          [("rgT", ti) for ti in range(c.NT)], [("gs", hf)])
                    P.op("dve", lambda e_, hf=hf, qn=qn: e_.tensor_tensor(
                        out=Ys[hf][:, 0:qn], in0=Ys[hf][:, 0:qn], in1=rst[:, 0:qn], op=ALU.mult),
                        [("Ys", hf), "rst"], [("Ys", hf)])
                    P.op("dve", lambda e_, hf=hf, qn=qn: e_.tensor_tensor(
                        out=yo[hf][:, 0:qn], in0=Ys[hf][:, 0:qn], in1=gs[hf][:, 0:qn], op=ALU.mult),
                        [("Ys", hf), ("gs", hf)], [("yo", hf)])
                    P.dma(d["yT"][c.HALF // 128 + ch, :, c.L + t0:c.L + t0 + qn], yo[hf][:, 0:qn],
                          [("yo", hf)], [("yT", "r", ch, t0)])

    def st_zero_y1(self):
        c, P, d = self.c, self.P, self.dram
        z = P.sb("z", [128, c.NTOK], BF16)
        P.op("pool", lambda e_: e_.memset(z[:], 0.0), [], ["z"])
        for ch in range(c.HALF // 128):
            P.dma(d["yT"][ch, :, :], z[:], ["z"], [("yT", "z", ch)])


    def seg_of(self, ti):
        nctx = self.c.L // 128
        return (0, nctx) if ti < nctx else (nctx, self.c.NT)

    def st_rwkv_shift(self, e):
        c, P, d = self.c, self.P, self.dram
        CA = c.COLS_A
        mu = P.sb("mu", [128, CA], F32)
        P.dma(mu[:], d["rwkv_mu"][e].partition_broadcast(128), ["mud"], ["mu"])
        xt = P.sb("xt", [128, CA], F32)
        xp = P.sb("xp", [128, CA], F32)
        xn = P.sb("xn", [128, CA], F32)
        for ti in range(c.NT):
            s0, s1 = self.seg_of(ti)
            r0 = ti * 128
            P.dma(xt[:], d["p"][r0:r0 + 128, 0:CA], ["p"], ["xt"])
            if ti == s0:
                P.op("pool", lambda e_: e_.memset(xp[:], 0.0), [], ["xp"])
                P.dma(xp[1:128, :], d["p"][r0:r0 + 127, 0:CA], ["p"], ["xp"])
            else:
                P.dma(xp[:], d["p"][r0 - 1:r0 + 127, 0:CA], ["p"], ["xp"])
            if ti == s1 - 1:
                P.op("pool", lambda e_: e_.memset(xn[:], 0.0), [], ["xn"])
                P.dma(xn[0:127, :], d["p"][r0 + 1:r0 + 128, 0:CA], ["p"], ["xn"])
            else:
                P.dma(xn[:], d["p"][r0 + 1:r0 + 129, 0:CA], ["p"], ["xn"])
            P.op("pool", lambda e_: e_.tensor_tensor(out=xp[:], in0=xp[:], in1=xn[:], op=ALU.add), ["xp", "xn"], ["xp"])
            P.op("dve", lambda e_: e_.scalar_tensor_tensor(out=xp[:], in0=xp[:], scalar=0.5, in1=xt[:], op0=ALU.mult,
                                                          op1=ALU.subtract), ["xp", "xt"], ["xp"])
            P.op("pool", lambda e_: e_.tensor_tensor(out=xp[:], in0=xp[:], in1=mu[:], op=ALU.mult), ["xp", "mu"], ["xp"])
            P.op("dve", lambda e_: e_.tensor_tensor(out=xt[:], in0=xt[:], in1=xp[:], op=ALU.add), ["xp", "xt"], ["xt"])
            P.dma(d["pa"][r0:r0 + 128, :], xt[:], ["xt"], [("pa", ti)])

    def st_rwkv_prep(self, e):
        c, P, d = self.c, self.P, self.dram
        CA, H2, A = c.COLS_A, c.HALF, c.A_HEADS
        J = A // 2
        lo = 3 * H2
        ident = P.sb("ident", [128, 128], F32)
        P.dma(ident[:], d["ident"][:, :], ["identd"], ["ident"])
        w2s = P.sb("w2s", [97, 2, H2], F32)
        a2s = P.sb("a2s", [97, 2, H2], F32)
        g2s = P.sb("g2s", [128, 2, H2], F32)
        for dd in range(2):
            P.dma(w2s[0:96, dd, :], d["rwkv_w2"][e, dd], ["w2d"], ["w2s"])
            P.dma(w2s[96:97, dd, :], d["rwkv_w0"][e, dd:dd + 1, :], ["w0d"], ["w2s"])
            P.dma(a2s[0:96, dd, :], d["rwkv_a2"][e, dd], ["a2d"], ["a2s"])
            P.dma(a2s[96:97, dd, :], d["rwkv_a0"][e, dd:dd + 1, :], ["a0d"], ["a2s"])
            P.dma(g2s[:, dd, :], d["rwkv_g2"][e, dd * 128:(dd + 1) * 128, :], ["g2d"], ["g2s"])
        kkb = P.sb("kkb", [128, H2], F32)
        kab = P.sb("kab", [128, H2], F32)
        rkb = P.sb("rkb", [128, H2], F32)
        P.dma(kkb[:], d["rwkv_k_k"][e].partition_broadcast(128), ["kkd"], ["kkb"])
        P.dma(kab[:], d["rwkv_k_a"][e].partition_broadcast(128), ["kad"], ["kab"])
        P.dma(rkb[:], d["rwkv_r_k"][e].rearrange("h k -> (h k)").partition_broadcast(128), ["rkd"], ["rkb"])
        X = P.sb("X", [128, CA], F32)
        T = [P.sb(f"T{i}", [128, H2], F32) if i != 5 else None for i in range(8)]
        lz = P.sb("lz", [128, 640], F32)
        zT = P.sb("zT", [128, 6, 128], F32)
        P.op("pool", lambda e_: e_.memset(zT[96:97, 0:4, :], 1.0), [], ["zT"])
        vTt = P.sb("vTt", [128, J, 128], F32)
        ssq = P.sb("ssq", [128, A], F32)
        bon = P.sb("bon", [128, A], F32)
        pzA = P.ps("pzA", [128, 512])
        pzB = P.ps("pzB", [128, 256])
        pw = [P.ps(f"pw{i}", [128, 512]) for i in range(3)]
        pv = [P.ps(f"pv{i}", [128, 512]) for i in range(2)]
        npw = 0
        v3 = lambda t: t[:].rearrange("p (h k) -> p h k", k=64)
        chunked = getattr(self, "_chunked", False)

        def wstore(dd, arr, r0, ap, rkeys, wkey):
            if chunked:
                P.dma(d[f"Wt{dd}"][r0:r0 + 128, arr, :], ap, rkeys, [wkey])
                return
            for hh in range(2):
                P.dma(d[f"Wd{dd}"][r0:r0 + 128, hh, arr, :, :],
                      ap.rearrange("p (j hh k) -> p hh j k", hh=2, k=64)[:, hh], rkeys, [wkey + (hh,)])
        for ti in range(c.NT):
            r0 = ti * 128
            P.dma(X[:], d["pa"][r0:r0 + 128, :], [("pa", ti)], ["X"])
            rr, kk_, vv = X[:, 0:H2], X[:, H2:2 * H2], X[:, 2 * H2:3 * H2]
            P.op("act", lambda e_: e_.activation(out=lz[:, 0:192], in_=X[:, lo:lo + 192], func=AF.Tanh), ["X"], ["lz"])
            P.op("act", lambda e_: e_.copy(out=lz[:, 192:384], in_=X[:, lo + 192:lo + 384]), ["X"], ["lz"])
            P.op("act", lambda e_: e_.activation(out=lz[:, 384:640], in_=X[:, lo + 384:lo + 640], func=AF.Sigmoid),
                 ["X"], ["lz"])
            for i in range(4):
                P.tr(pzA[0:96, i * 128:(i + 1) * 128], lz[:, i * 96:(i + 1) * 96], ident[:], ["lz", "ident"], ["pzA"])
            for i in range(2):
                P.tr(pzB[:, i * 128:(i + 1) * 128], lz[:, 384 + i * 128:384 + (i + 1) * 128], ident[:], ["lz", "ident"], ["pzB"])
            P.op("dve", lambda e_: e_.tensor_copy(out=zT[0:96, 0:4, :], in_=pzA[0:96, :].rearrange("p (a t) -> p a t", t=128)),
                 ["pzA"], ["zT"])
            P.op("dve", lambda e_: e_.tensor_copy(out=zT[:, 4:6, :], in_=pzB[:, :].rearrange("p (a t) -> p a t", t=128)),
                 ["pzB"], ["zT"])
            for cc in range(0, H2, 512):
                pb, pk = pw[npw % 3], ("pw", npw % 3)
                npw += 1
                P.mm(pb[:, :], zT[:, 4, :], g2s[:, 0, cc:cc + 512], True, False, ["zT", "g2s"], [pk])
                P.mm(pb[:, :], zT[:, 5, :], g2s[:, 1, cc:cc + 512], False, True, ["zT", "g2s"], [pk])
                self.copy("act", T[6][:, cc:cc + 512], pb[:, :], [pk], ["T6"])
            P.dma(d["gate"][r0:r0 + 128, :], T[6][:], ["T6"], [("gate", ti)])
            P.op("pool", lambda e_: e_.tensor_tensor(out=T[3][:], in0=kk_, in1=kkb[:], op=ALU.mult), ["X", "kkb"], ["T3"])
            P.op("dve", lambda e_: e_.tensor_tensor(out=T[7][:], in0=T[3][:], in1=T[3][:], op=ALU.mult), ["T3"], ["T7"])
            P.op("dve", lambda e_: e_.tensor_reduce(out=ssq[:], in_=v3(T[7]), axis=AX.X, op=ALU.add), ["T7"], ["ssq"])
            P.op("dve", lambda e_: e_.tensor_scalar(out=ssq[:], in0=ssq[:], scalar1=1e-12, scalar2=None, op0=ALU.add),
                 ["ssq"], ["ssq"])
            P.op("act", lambda e_: e_.activation(out=ssq[:], in_=ssq[:], func=AF.Sqrt), ["ssq"], ["ssq"])
            P.op("dve", lambda e_: e_.reciprocal(out=ssq[:], in_=ssq[:]), ["ssq"], ["ssq"])
            P.op("dve", lambda e_: e_.tensor_tensor(out=v3(T[2]), in0=v3(T[3]),
                                                    in1=ssq[:].unsqueeze(2).broadcast_to([128, A, 64]), op=ALU.mult),
                 ["T3", "ssq"], ["T2"])
            for dd in range(2):
                wstore(dd, 1, r0, T[2][:], ["T2"], ("Wd", dd, ti, 1))
                wstore(dd, 4, r0, rr, ["X"], ("Wd", dd, ti, 4))
            for dd in range(2):
                for cc in range(0, H2, 512):
                    pb, pk = pw[npw % 3], ("pw", npw % 3)
                    npw += 1
                    P.mm(pb[:, :], zT[0:97, dd, :], w2s[0:97, dd, cc:cc + 512], True, True, ["zT", "w2s"], [pk])
                    P.op("act", lambda e_, pb=pb, cc=cc: e_.activation(out=T[0][:, cc:cc + 512], in_=pb[:, :],
                                                                     func=AF.Sigmoid), [pk], ["T0"])
                P.op("act", lambda e_: e_.activation(out=T[0][:], in_=T[0][:],
                                                     func=(AF.Copy if chunked else AF.Exp), scale=-math.exp(-0.5)),
                     ["T0"], ["T0"])
                wstore(dd, 0, r0, T[0][:], ["T0"], ("Wd", dd, ti, 0))
                for cc in range(0, H2, 512):
                    pb, pk = pw[npw % 3], ("pw", npw % 3)
                    npw += 1
                    P.mm(pb[:, :], zT[0:97, 2 + dd, :], a2s[0:97, dd, cc:cc + 512], True, True, ["zT", "a2s"], [pk])
                    P.op("act", lambda e_, pb=pb, cc=cc: e_.activation(out=T[1][:, cc:cc + 512], in_=pb[:, :],
                                                                     func=AF.Sigmoid), [pk], ["T1"])
                P.op("pool", lambda e_: e_.tensor_tensor(out=T[3][:], in0=T[2][:], in1=T[1][:], op=ALU.mult),
                     ["T2", "T1"], ["T3"])
                wstore(dd, 2, r0, T[3][:], ["T3"], ("Wd", dd, ti, 2))
                kdst, kdk = (T[4], "T4") if dd == 0 else (T[6], "T6")
                P.op("dve", lambda e_: e_.scalar_tensor_tensor(out=T[7][:], in0=T[1][:], scalar=-1.0, in1=kab[:],
                                                              op0=ALU.add, op1=ALU.mult), ["T1", "kab"], ["T7"])
                P.op("dve", lambda e_, kdst=kdst: e_.scalar_tensor_tensor(out=kdst[:], in0=T[7][:], scalar=1.0, in1=kk_,
                                                                         op0=ALU.add, op1=ALU.mult),
                     ["T7", "X"], [kdk])
                wstore(dd, 3, r0, kdst[:], [kdk], ("Wd", dd, ti, 3))
            P.op("pool", lambda e_: e_.tensor_tensor(out=T[7][:], in0=T[4][:], in1=T[6][:], op=ALU.add), ["T4", "T6"], ["T7"])
            P.op("dve", lambda e_: e_.tensor_tensor(out=T[7][:], in0=T[7][:], in1=rr, op=ALU.mult), ["T7", "X"], ["T7"])
            P.op("pool", lambda e_: e_.tensor_tensor(out=T[7][:], in0=T[7][:], in1=rkb[:], op=ALU.mult), ["T7", "rkb"], ["T7"])
            P.op("dve", lambda e_: e_.tensor_reduce(out=bon[:], in_=v3(T[7]), axis=AX.X, op=ALU.add), ["T7"], ["bon"])
            P.dma(d["bon"][r0:r0 + 128, :], bon[:], ["bon"], [("bond", ti)])
            P.dma(d["vtok"][r0:r0 + 128, :], vv, ["X"], [("vtok", ti)])
            for j0 in range(0, 0 if chunked else J, 4):
                jn = min(4, J - j0)
                pb, pk = pv[(j0 // 4) % 2], ("pv", (j0 // 4) % 2)
                for jj in range(jn):
                    P.tr(pb[:, jj * 128:(jj + 1) * 128], X[:, 2 * H2 + (j0 + jj) * 128:2 * H2 + (j0 + jj + 1) * 128], ident[:],
                         ["X", "ident"], [pk])
                P.op("dve", lambda e_, pb=pb, j0=j0, jn=jn: e_.tensor_copy(
                    out=vTt[:, j0:j0 + jn, :], in_=pb[:, 0:jn * 128].rearrange("p (a t) -> p a t", t=128)), [pk], ["vTt"])
            if not chunked:
                P.dma(d["vTs"][:, :, r0:r0 + 128], vTt[:], ["vTt"], [("vTs", ti)])

    def st_rwkv_scan(self, dd):
        c, P, d = self.c, self.P, self.dram
        A = c.A_HEADS
        J = A // 2
        nctx = c.L // 128
        S = P.sb("S", [128, J, 64], F32)
        P.op("pool", lambda e_: e_.memset(S[:], 0.0), [], ["S"])
        NB = 3
        bcb = [P.sb(f"bcb{i}", [128, 5, J, 64], F32) for i in range(NB)]
        vT = [P.sb(f"vT{i}", [128, J, 128], F32) for i in range(2)]
        yT = [P.sb(f"yTt{i}", [128, J, 128], F32) for i in range(2)]
        tA = [P.sb(f"tA{i}", [128, J, 64], F32) for i in range(2)]
        t2 = [P.sb(f"t2{i}", [128, J, 64], F32) for i in range(2)]
        t3 = [P.sb(f"t3{i}", [128, J, 64], F32) for i in range(2)]
        sa = [P.sb(f"sa{i}", [128, J], F32) for i in range(2)]
        Wd = d[f"Wd{dd}"]
        if dd == 0:
            tiles = list(range(c.NT))
        else:
            tiles = list(range(nctx - 1, -1, -1)) + list(range(c.NT - 1, nctx - 1, -1))
        step = 0
        for xi, ti in enumerate(tiles):
            vb, vk = vT[xi % 2], ("vT", xi % 2)
            yb, yk = yT[xi % 2], ("yT", xi % 2)
            r0 = ti * 128
            P.dma(vb[:], d["vTs"][:, :, r0:r0 + 128], ["vTs"], [vk])
            order = range(128) if dd == 0 else range(127, -1, -1)
            for tt in order:
                u = r0 + tt
                bb, bk = bcb[step % NB], ("bcb", step % NB)
                i2 = step % 2
                step += 1
                for hh in range(2):
                    P.dma(bb[hh * 64:(hh + 1) * 64].rearrange("p a j k -> p (a j k)"),
                          Wd[u, hh].rearrange("a j k -> (a j k)").partition_broadcast(64), ["Wd"], [bk + (hh,)])
                bks = [bk + (0,), bk + (1,)]
                P.op("dve", lambda e_, bb=bb, i2=i2: e_.tensor_tensor(out=tA[i2][:], in0=S[:], in1=bb[:, 1], op=ALU.mult),
                     ["S"] + bks, [("tA", i2)])
                P.op("dve", lambda e_, i2=i2: e_.tensor_reduce(out=sa[i2][:], in_=tA[i2][:], axis=AX.X, op=ALU.add),
                     [("tA", i2)], [("sa", i2)])
                P.op("pool", lambda e_, bb=bb, vb=vb, tt=tt, i2=i2: e_.tensor_tensor(
                    out=t3[i2][:], in0=bb[:, 3], in1=vb[:, :, tt].unsqueeze(2).broadcast_to([128, J, 64]), op=ALU.mult),
                    bks + [vk], [("t3", i2)])
                P.op("pool", lambda e_, bb=bb: e_.tensor_tensor(out=S[:], in0=S[:], in1=bb[:, 0], op=ALU.mult),
                     ["S"] + bks, ["S"])
                P.op("dve", lambda e_, bb=bb, i2=i2: e_.tensor_tensor(
                    out=t2[i2][:], in0=bb[:, 2], in1=sa[i2][:].unsqueeze(2).broadcast_to([128, J, 64]), op=ALU.mult),
                    bks + [("sa", i2)], [("t2", i2)])
                P.op("dve", lambda e_, i2=i2: e_.tensor_tensor(out=t3[i2][:], in0=t3[i2][:], in1=t2[i2][:], op=ALU.subtract),
                     [("t3", i2), ("t2", i2)], [("t3", i2)])
                P.op("pool", lambda e_, i2=i2: e_.tensor_tensor(out=S[:], in0=S[:], in1=t3[i2][:], op=ALU.add),
                     ["S", ("t3", i2)], ["S"])
                P.op("dve", lambda e_, bb=bb, i2=i2: e_.tensor_tensor(out=tA[i2][:], in0=S[:], in1=bb[:, 4], op=ALU.mult),
                     ["S"] + bks, [("tA", i2)])
                P.op("dve", lambda e_, yb=yb, tt=tt, i2=i2: e_.tensor_reduce(out=yb[:, :, tt], in_=tA[i2][:], axis=AX.X,
                                                                            op=ALU.add), [("tA", i2)], [yk])
            P.dma(d[f"ysc{dd}"][:, :, r0:r0 + 128], yb[:], [yk], [("ysc", ti)])

    def st_rwkv_prep2(self, e):
        self._chunked = True
        self.st_rwkv_prep(e)

    def st_rwkv_chunk(self, dd):
        c, P, d = self.c, self.P, self.dram
        H2, A = c.HALF, c.A_HEADS
        C = 64
        G = min(16, A)
        GW = G * 64
        NG = A // G
        nch_ctx = c.L // C
        nch = c.NTOK // C
        ident = P.sb("ident", [128, 128], F32)
        P.dma(ident[:], d["ident"][:, :], ["identd"], ["ident"])
        tri = P.sb("tri", [64, 64], F32)
        P.dma(tri[:], d["ctri"][dd], ["ctri"], ["tri"])
        onec = P.sb("onec", [64, 1], F32)
        P.op("pool", lambda e_: e_.memset(onec[:], 1.0), [], ["onec"])
        onesr = P.sb("onesr", [1, 64], F32)
        P.op("pool", lambda e_: e_.memset(onesr[:], 1.0), [], ["onesr"])
        mP = P.sb("mP", [64, 128], F32)
        mL = P.sb("mL", [64, 64], F32)
        P.dma(mP[:], d["cmaskP"][dd], ["cmp"], ["mP"])
        P.dma(mL[:], d["cmaskL"][dd], ["cml"], ["mL"])
        Tst = P.sb("Tst", [64, A, 64], F32)
        P.op("pool", lambda e_: e_.memset(Tst[:], 0.0), [], [("T", g) for g in range(NG)])
        X5 = P.sb("X5", [64, 5, GW], F32)
        V = P.sb("V", [64, GW], F32)
        Lc = P.sb("Lc", [64, GW], F32)
        LtR = P.sb("LtR", [1, GW], F32)
        E = [P.sb(f"E{i}", [64, GW], F32) for i in range(8)]
        ARt = P.sb("ARt", [64, G, 128], F32)
        BtT = P.sb("BtT", [64, G, 64], F32)
        KtT = P.sb("KtT", [64, G, 64], F32)
        GamT = P.sb("GamT", [64, G], F32)
        Pb = P.sb("Pb", [64, G, 128], F32)
        Pk = P.sb("Pk", [64, G, 128], F32)
        Lm = [P.sb(f"Lm{i}", [64, G, 64], F32) for i in range(2)]
        Nm = [P.sb(f"Nm{i}", [64, G, 64], F32) for i in range(2)]
        MT = [P.sb(f"MT{i}", [64, G, 64], F32) for i in range(2)]
        Zs = P.sb("Zs", [64, G, 64], F32)
        Us = P.sb("Us", [64, G, 64], F32)
        Yt = P.sb("Yt", [64, GW], F32)
        ps = [P.ps(f"pc{i}", [64, 512]) for i in range(8)]
        psi = [0]

        def bank():
            i = psi[0] % 8
            psi[0] += 1
            return ps[i], ("pc", i)

        def hv(t):
            return t[:].rearrange("p (h k) -> p h k", k=64)

        Wt = d[f"Wt{dd}"]
        if dd == 0:
            chunks = list(range(nch))
        else:
            chunks = list(range(nch_ctx - 1, -1, -1)) + list(range(nch - 1, nch_ctx - 1, -1))
        HB = 8
        for ch in chunks:
            u0 = ch * C
            for g in range(NG):
                c0 = g * GW
                P.dma(X5[:], Wt[u0:u0 + C, :, c0:c0 + GW], ["Wt"], ["X5"])
                P.dma(V[:], d["vtok"][u0:u0 + C, c0:c0 + GW], ["vtok"], ["V"])
                lw, kk_, b_, kd_, r_ = (X5[:, i, :] for i in range(5))
                for cc in range(0, GW, 512):
                    pb, pk = bank()
                    P.mm(pb[:, :], tri[:], X5[:, 0, cc:cc + 512], True, True, ["tri", "X5"], [pk])
                    self.copy("act", Lc[:, cc:cc + 512], pb[:, :], [pk], ["Lc"])
                    pb, pk = bank()
                    P.mm(pb[0:1, :], onec[:], X5[:, 0, cc:cc + 512], True, True, ["onec", "X5"], [pk])
                    self.copy("dve", LtR[:, cc:cc + 512], pb[0:1, :], [pk], ["LtR"])
                P.op("act", lambda e_: e_.activation(out=E[0][:], in_=Lc[:], func=AF.Exp), ["Lc"], ["E0"])
                P.op("act", lambda e_: e_.activation(out=E[1][:], in_=Lc[:], func=AF.Exp, scale=-1.0), ["Lc"], ["E1"])
                P.op("dve", lambda e_, lw=lw: e_.tensor_tensor(out=E[2][:], in0=Lc[:], in1=lw, op=ALU.subtract),
                     ["Lc", "X5"], ["E2"])
                P.op("act", lambda e_: e_.activation(out=E[2][:], in_=E[2][:], func=AF.Exp), ["E2"], ["E2"])
                P.op("act", lambda e_: e_.activation(out=LtR[:], in_=LtR[:], func=AF.Exp), ["LtR"], ["LtR"])
                for cc in range(0, GW, 512):
                    pb, pk = bank()
                    P.mm(pb[:, :], onesr[:], LtR[0:1, cc:cc + 512], True, True, ["onesr", "LtR"], [pk])
                    P.op("dve", lambda e_, pb=pb, cc=cc: e_.tensor_tensor(out=E[3][:, cc:cc + 512], in0=pb[:, :],
                                                                       in1=E[1][:, cc:cc + 512], op=ALU.mult),
                         [pk, "E1"], ["E3"])
                P.op("dve", lambda e_, kk_=kk_: e_.scalar_tensor_tensor(out=E[2][:], in0=kk_, scalar=-1.0, in1=E[2][:],
                                                                      op0=ALU.mult, op1=ALU.mult), ["X5", "E2"], ["E2"])
                P.op("pool", lambda e_, b_=b_: e_.tensor_tensor(out=E[4][:], in0=b_, in1=E[1][:], op=ALU.mult), ["X5", "E1"], ["E4"])
                P.op("dve", lambda e_, kd_=kd_: e_.tensor_tensor(out=E[5][:], in0=kd_, in1=E[1][:], op=ALU.mult), ["X5", "E1"], ["E5"])
                P.op("pool", lambda e_, r_=r_: e_.tensor_tensor(out=E[0][:], in0=r_, in1=E[0][:], op=ALU.mult), ["X5", "E0"], ["E0"])
                P.op("dve", lambda e_, b_=b_: e_.tensor_tensor(out=E[6][:], in0=b_, in1=E[3][:], op=ALU.mult), ["X5", "E3"], ["E6"])
                P.op("pool", lambda e_, kd_=kd_: e_.tensor_tensor(out=E[7][:], in0=kd_, in1=E[3][:], op=ALU.mult), ["X5", "E3"], ["E7"])
                for src_t, skey, dst, dkey, doff in ((E[2], "E2", ARt, "ARt", 0), (E[0], "E0", ARt, "ARt", 64),
                                                     (E[4], "E4", BtT, "BtT", 0), (E[5], "E5", KtT, "KtT", 0)):
                    for h0 in range(0, G, HB):
                        pb, pk = bank()
                        for hh in range(HB):
                            P.tr(pb[:, hh * 64:(hh + 1) * 64], src_t[:, (h0 + hh) * 64:(h0 + hh + 1) * 64], ident[0:64, 0:64],
                                 [skey, "ident"], [pk])
                        dv = dst[:, h0:h0 + HB, doff:doff + 64]
                        self.copy(self.evac_eng(), dv, pb[:, :].rearrange("p (h t) -> p h t", t=64), [pk], [dkey])
                pb, pk = bank()
                for h in range(G):
                    P.tr(pb[:, h:h + 1], LtR[0:1, h * 64:(h + 1) * 64], ident[0:1, 0:1], ["LtR", "ident"], [pk])
                self.copy("act", GamT[:], pb[:, 0:G], [pk], ["GamT"])
                for lhs, lkey, dst, dkey in ((BtT, "BtT", Pb, "Pb"), (KtT, "KtT", Pk, "Pk")):
                    for h0 in range(0, G, 4):
                        pb, pk = bank()
                        for hh in range(4):
                            P.mm(pb[:, hh * 128:(hh + 1) * 128], lhs[:, h0 + hh, :], ARt[:, h0 + hh, :], True, True,
                                 [lkey, "ARt"], [pk])
                        P.op("dve", lambda e_, pb=pb, dst=dst, h0=h0: e_.tensor_tensor(
                            out=dst[:, h0:h0 + 4, :], in0=pb[:, :].rearrange("p (h t) -> p h t", t=128),
                            in1=mP[:].unsqueeze(1).broadcast_to([64, 4, 128]), op=ALU.mult), [pk, "mP"], [dkey])
                for h0 in range(0, G, HB):
                    pb, pk = bank()
                    for hh in range(HB):
                        P.mm(pb[:, hh * 64:(hh + 1) * 64], ARt[:, h0 + hh, 0:64], BtT[:, h0 + hh, :], True, True,
                             ["ARt", "BtT"], [pk])
                    P.op("dve", lambda e_, pb=pb, h0=h0: e_.tensor_tensor(
                        out=Lm[0][:, h0:h0 + HB, :], in0=pb[:, :].rearrange("p (h t) -> p h t", t=64),
                        in1=mL[:].unsqueeze(1).broadcast_to([64, HB, 64]), op=ALU.mult), [pk, "mL"], [("Lm", 0)])
                P.op("pool", lambda e_: e_.tensor_copy(out=Nm[0][:], in_=Pb[:, :, 0:64]), ["Pb"], [("Nm", 0)])
                P.op("dve", lambda e_: e_.tensor_tensor(out=MT[0][:], in0=Nm[0][:],
                                                        in1=ident[0:64, 0:64].unsqueeze(1).broadcast_to([64, G, 64]),
                                                        op=ALU.add), [("Nm", 0), "ident"], [("MT", 0)])
                cur = 0
                for lvl in range(1, 6):
                    nxt = cur ^ 1
                    for lhsA, lkA, rhsA, rkA, dstA, dkA in ((Lm[cur], ("Lm", cur), Nm[cur], ("Nm", cur), Nm[nxt], ("Nm", nxt)),
                                                            (Nm[cur], ("Nm", cur), Lm[cur], ("Lm", cur), Lm[nxt], ("Lm", nxt))):
                        for h0 in range(0, G, HB):
                            pb, pk = bank()
                            for hh in range(HB):
                                P.mm(pb[:, hh * 64:(hh + 1) * 64], lhsA[:, h0 + hh, :], rhsA[:, h0 + hh, :], True, True,
                                     [lkA, rkA], [pk])
                            self.copy(self.evac_eng(), dstA[:, h0:h0 + HB, :], pb[:, :].rearrange("p (h t) -> p h t", t=64),
                                      [pk], [dkA])
                    for h0 in range(0, G, HB):
                        pb, pk = bank()
                        for hh in range(HB):
                            P.mm(pb[:, hh * 64:(hh + 1) * 64], Lm[nxt][:, h0 + hh, :], MT[cur][:, h0 + hh, :], True, True,
                                 [("Lm", nxt), ("MT", cur)], [pk])
                        P.op("dve", lambda e_, pb=pb, h0=h0, cur=cur, nxt=nxt: e_.tensor_tensor(
                            out=MT[nxt][:, h0:h0 + HB, :], in0=pb[:, :].rearrange("p (h t) -> p h t", t=64),
                            in1=MT[cur][:, h0:h0 + HB, :], op=ALU.add), [pk, ("MT", cur)], [("MT", nxt)])
                    cur = nxt
                MTf, MTk = MT[cur], ("MT", cur)
                Tg = Tst[:, g * G:(g + 1) * G, :]
                tk = ("T", g)
                Vh = hv(V)
                for h0 in range(0, G, HB):
                    pb, pk = bank()
                    for hh in range(HB):
                        h = h0 + hh
                        P.mm(pb[:, hh * 64:(hh + 1) * 64], ARt[:, h, 0:64], Tg[:, h, :], True, False, ["ARt", tk], [pk])
                        P.mm(pb[:, hh * 64:(hh + 1) * 64], Pk[:, h, 0:64], Vh[:, h, :], False, True, ["Pk", "V"], [pk])
                    self.copy(self.evac_eng(), Zs[:, h0:h0 + HB, :], pb[:, :].rearrange("p (h t) -> p h t", t=64), [pk], ["Zs"])
                for h0 in range(0, G, HB):
                    pb, pk = bank()
                    for hh in range(HB):
                        h = h0 + hh
                        P.mm(pb[:, hh * 64:(hh + 1) * 64], MTf[:, h, :], Zs[:, h, :], True, True, [MTk, "Zs"], [pk])
                    self.copy(self.evac_eng(), Us[:, h0:h0 + HB, :], pb[:, :].rearrange("p (h t) -> p h t", t=64), [pk], ["Us"])
                for h0 in range(0, G, HB):
                    pb, pk = bank()
                    for hh in range(HB):
                        h = h0 + hh
                        P.mm(pb[:, hh * 64:(hh + 1) * 64], ARt[:, h, 64:128], Tg[:, h, :], True, False, ["ARt", tk], [pk])
                        P.mm(pb[:, hh * 64:(hh + 1) * 64], Pb[:, h, 64:128], Us[:, h, :], False, False, ["Pb", "Us"], [pk])
                        P.mm(pb[:, hh * 64:(hh + 1) * 64], Pk[:, h, 64:128], Vh[:, h, :], False, True, ["Pk", "V"], [pk])
                    self.copy(self.evac_eng(), Yt[:, h0 * 64:(h0 + HB) * 64], pb[:, :], [pk], ["Yt"])
                P.dma(d[f"ytk{dd}"][u0:u0 + C, c0:c0 + GW], Yt[:], ["Yt"], [("ytk", ch, g)])
                Bh, Kh = hv(E[6]), hv(E[7])
                P.op("pool", lambda e_, Tg=Tg: e_.tensor_tensor(out=Tg, in0=Tg, in1=GamT[:].unsqueeze(2).broadcast_to([64, G, 64]),
                                                               op=ALU.mult), [tk, "GamT"], [tk])
                for h0 in range(0, G, HB):
                    pb, pk = bank()
                    for hh in range(HB):
                        h = h0 + hh
                        P.mm(pb[:, hh * 64:(hh + 1) * 64], Bh[:, h, :], Us[:, h, :], True, False, ["E6", "Us"], [pk])
                        P.mm(pb[:, hh * 64:(hh + 1) * 64], Kh[:, h, :], Vh[:, h, :], False, True, ["E7", "V"], [pk])
                    P.op("dve", lambda e_, pb=pb, h0=h0, Tg=Tg: e_.tensor_tensor(
                        out=Tg[:, h0:h0 + HB, :], in0=Tg[:, h0:h0 + HB, :], in1=pb[:, :].rearrange("p (h t) -> p h t", t=64),
                        op=ALU.add), [pk, tk], [tk])

    def st_rwkv_post(self, e):
        c, P, d = self.c, self.P, self.dram
        H2, A = c.HALF, c.A_HEADS
        J = A // 2
        ident = P.sb("ident", [128, 128], F32)
        P.dma(ident[:], d["ident"][:, :], ["identd"], ["ident"])
        identb = self.load_identb()
        lwb = P.sb("lwb", [128, H2], F32)
        lbb = P.sb("lbb", [128, H2], F32)
        P.dma(lwb[:], d["rwkv_ln_w"][e].partition_broadcast(128), ["lwd"], ["lwb"])
        P.dma(lbb[:], d["rwkv_ln_b"][e].partition_broadcast(128), ["lbd"], ["lbb"])
        ya = P.sb("ya", [128, J, 128], F32)
        yb_ = P.sb("yb", [128, J, 128], F32)
        Y = P.sb("Y", [128, H2], F32)
        Q = P.sb("Q", [128, H2], F32)
        G = P.sb("G", [128, H2], F32)
        V = P.sb("V", [128, H2], F32)
        Yb = P.sb("Yb", [128, H2], BF16)
        st = P.sb("st", [128, A], F32)
        bon = P.sb("bon", [128, A], F32)
        ev = [P.sb(f"ev{i}", [128, 512], BF16) for i in range(2)]
        pp = [P.ps(f"pp{i}", [128, 512]) for i in range(2)]
        pq = [P.ps(f"pq{i}", [128, 512], BF16) for i in range(2)]
        v3 = lambda t: t[:].rearrange("p (h k) -> p h k", k=64)
        bc = lambda s: s[:].unsqueeze(2).broadcast_to([128, A, 64])
        n4 = 0
        for ti in range(c.NT):
            r0 = ti * 128
            chunked = getattr(self, "_chunked", False)
            if chunked:
                P.dma(Y[:], d["ytk0"][r0:r0 + 128, :], ["ytk0"], ["Y"])
                P.dma(Q[:], d["ytk1"][r0:r0 + 128, :], ["ytk1"], ["Q"])
                P.op("pool", lambda e_: e_.tensor_tensor(out=Y[:], in0=Y[:], in1=Q[:], op=ALU.add), ["Y", "Q"], ["Y"])
            else:
                P.dma(ya[:], d["ysc0"][:, :, r0:r0 + 128], ["ysc0"], ["ya"])
                P.dma(yb_[:], d["ysc1"][:, :, r0:r0 + 128], ["ysc1"], ["yb"])
            P.dma(G[:], d["gate"][r0:r0 + 128, :], ["gate"], ["G"])
            P.dma(V[:], d["vtok"][r0:r0 + 128, :], ["vtok"], ["V"])
            P.dma(bon[:], d["bon"][r0:r0 + 128, :], ["bond"], ["bon"])
            if not chunked:
                P.op("pool", lambda e_: e_.tensor_tensor(out=ya[:], in0=ya[:], in1=yb_[:], op=ALU.add), ["ya", "yb"], ["ya"])
            for j0 in range(0, 0 if chunked else J, 4):
                jn = min(4, J - j0)
                pb, pk = pp[(j0 // 4) % 2], ("pp", (j0 // 4) % 2)
                for jj in range(jn):
                    P.tr(pb[:, jj * 128:(jj + 1) * 128], ya[:, j0 + jj, :], ident[:], ["ya", "ident"], [pk])
                self.copy("act", Y[:, j0 * 128:(j0 + jn) * 128], pb[:, 0:jn * 128], [pk], ["Y"])
            P.op("dve", lambda e_: e_.tensor_reduce(out=st[:], in_=v3(Y), axis=AX.X, op=ALU.add), ["Y"], ["st"])
            P.op("dve", lambda e_: e_.tensor_scalar(out=st[:], in0=st[:], scalar1=1.0 / 64.0, scalar2=None, op0=ALU.mult),
                 ["st"], ["st"])
            P.op("dve", lambda e_: e_.tensor_tensor(out=v3(Y), in0=v3(Y), in1=bc(st), op=ALU.subtract), ["Y", "st"], ["Y"])
            P.op("pool", lambda e_: e_.tensor_tensor(out=Q[:], in0=Y[:], in1=Y[:], op=ALU.mult), ["Y"], ["Q"])
            P.op("dve", lambda e_: e_.tensor_reduce(out=st[:], in_=v3(Q), axis=AX.X, op=ALU.add), ["Q", "Y"], ["st"])
            P.op("dve", lambda e_: e_.tensor_scalar(out=st[:], in0=st[:], scalar1=1.0 / 64.0, scalar2=64e-5,
                                                    op0=ALU.mult, op1=ALU.add), ["st"], ["st"])
            P.op("act", lambda e_: e_.activation(out=st[:], in_=st[:], func=AF.Sqrt), ["st"], ["st"])
            P.op("dve", lambda e_: e_.reciprocal(out=st[:], in_=st[:]), ["st"], ["st"])
            P.op("dve", lambda e_: e_.tensor_tensor(out=v3(Y), in0=v3(Y), in1=bc(st), op=ALU.mult), ["Y", "st"], ["Y"])
            P.op("pool", lambda e_: e_.tensor_tensor(out=Y[:], in0=Y[:], in1=lwb[:], op=ALU.mult), ["Y", "lwb"], ["Y"])
            P.op("dve", lambda e_: e_.tensor_tensor(out=Y[:], in0=Y[:], in1=lbb[:], op=ALU.add), ["Y", "lbb"], ["Y"])
            P.op("pool", lambda e_: e_.tensor_tensor(out=v3(Q), in0=v3(V), in1=bc(bon), op=ALU.mult), ["V", "bon", "Q"], ["Q"])
            P.op("dve", lambda e_: e_.tensor_tensor(out=Y[:], in0=Y[:], in1=Q[:], op=ALU.add), ["Y", "Q"], ["Y"])
            P.op("dve", lambda e_: e_.tensor_tensor(out=Yb[:], in0=Y[:], in1=G[:], op=ALU.mult), ["Y", "G"], ["Yb"])
            for q0 in range(0, H2 // 128, 4):
                qn = min(4, H2 // 128 - q0)
                pb, pk = pq[n4 % 2], ("pq", n4 % 2)
                eb, ek = ev[n4 % 2], ("ev", n4 % 2)
                n4 += 1
                for q in range(qn):
                    P.tr(pb[:, q * 128:(q + 1) * 128], Yb[:, (q0 + q) * 128:(q0 + q + 1) * 128], identb[:],
                         ["Yb", "identb"], [pk])
                self.copy(self.evac_eng(), eb[:, 0:qn * 128], pb[:, 0:qn * 128], [pk], [ek])
                P.dma(d["yT"][q0:q0 + qn, :, r0:r0 + 128].rearrange("k p t -> p k t"),
                      eb[:, 0:qn * 128].rearrange("p (k t) -> p k t", t=128), [ek], [("yT", "a", ti, q0)])

    def st_mod(self):
        c, P, d = self.c, self.P, self.dram
        NR = 2
        ncols = 6 * c.D
        ident = P.sb("ident", [128, 128], F32)
        P.dma(ident[:], d["ident"][:, :], ["identd"], ["ident"])
        cv = P.sb("cv", [NR, c.D], F32)
        P.dma(cv[:], d["cvec"][:, :], ["cvec"], ["cv"])
        P.op("act", lambda e: e.activation(out=cv[:], in_=cv[:], func=AF.Silu), ["cv"], ["cv"])
        cT = P.sb("cT", [128, c.KC, NR], F32)
        pt = P.ps("pt", [128, 512])
        for kc in range(c.KC):
            P.tr(pt[:, 0:NR], cv[:, kc * 128:(kc + 1) * 128], ident[0:NR, 0:NR], ["cv", "ident"], ["pt"])
            self.copy("dve", cT[:, kc, :], pt[:, 0:NR], ["pt"], ["cT"])
        ws = [P.sb(f"ws{i}", [128, c.KC, 512], F32) for i in range(2)]
        po = [P.ps(f"po{i}", [NR, 512]) for i in range(2)]
        bb = [P.sb(f"bb{i}", [NR, 512], F32) for i in range(2)]
        ob = [P.sb(f"ob{i}", [NR, 512], F32) for i in range(2)]
        it = 0
        for l in range(c.DEPTH):
            mo = d["mod"][l].rearrange("r j d -> r (j d)")
            for c0 in range(0, ncols, 512):
                cw = min(512, ncols - c0)
                i2 = it % 2
                w, wk = ws[i2], ("ws", i2)
                p_, pk = po[i2], ("po", i2)
                o_, ok = ob[i2], ("ob", i2)
                b_, bk = bb[i2], ("bb", i2)
                it += 1
                P.dma(b_[:, 0:cw], d["b_mod"][l, c0:c0 + cw].partition_broadcast(NR), ["bm"], [bk])
                for k0 in range(0, c.KC, 8):
                    k1 = min(c.KC, k0 + 8)
                    P.dma(w[:, k0:k1, 0:cw], d["w_mod"][l, k0 * 128:k1 * 128, c0:c0 + cw].rearrange(
                        "(k p) n -> p k n", p=128), ["wm"], [wk + (k0,)])
                for kc in range(c.KC):
                    P.mm(p_[:, 0:cw], cT[:, kc, :], w[:, kc, 0:cw], kc == 0, kc == c.KC - 1,
                         ["cT", wk + ((kc // 8) * 8,)], [pk])
                P.op("dve", lambda e, o_=o_, p_=p_, b_=b_, cw=cw: e.tensor_tensor(
                    out=o_[:, 0:cw], in0=p_[:, 0:cw], in1=b_[:, 0:cw], op=ALU.add), [pk, bk], [ok])
                P.dma(mo[:, c0:c0 + cw], o_[:, 0:cw], [ok], [("mod", l, c0)])

def _axial(n, dim):
    rows = n // 64
    row = np.repeat(np.arange(rows, dtype=np.float32), 64)
    col = np.tile(np.arange(64, dtype=np.float32), rows)
    quarter = dim // 4
    inv = (10000.0 ** (-np.arange(quarter, dtype=np.float32) / quarter)).astype(np.float32)
    ang = np.concatenate([row[:, None] * inv, col[:, None] * inv], -1)
    return np.cos(ang).astype(np.float32), np.sin(ang).astype(np.float32)


def _seqrope(n, dim):
    inv = (10000.0 ** (-np.linspace(0.0, 1.0, dim // 2, dtype=np.float32))).astype(np.float32)
    ang = np.arange(n, dtype=np.float32)[:, None] * inv
    return np.cos(ang).astype(np.float32), np.sin(ang).astype(np.float32)


def _consts(cfg):
    import ml_dtypes
    cs, sn = _axial(cfg.S, 64)
    rc, rs = _seqrope(cfg.NTOK, 128)
    p = np.arange(128)[:, None]
    f = np.arange(512)[None, :]
    maskW = np.stack([(np.abs((m - 1) * 128 + p - f) <= 128) for m in range(6)]).astype(ml_dtypes.bfloat16)
    T0 = (f - p).astype(np.float32)
    Mge = np.stack([((-128 * di + f - p) >= 0) for di in range(4)]).astype(np.float32)
    Mle = np.stack([((-128 * di + f - p) <= 0) for di in range(4)]).astype(np.float32)
    s_ = np.arange(64)[:, None]
    t_ = np.arange(64)[None, :]
    ctri = np.stack([s_ <= t_, s_ >= t_]).astype(np.float32)
    cmaskP = np.stack([np.concatenate([s_ < t_, s_ <= t_], 1), np.concatenate([s_ > t_, s_ >= t_], 1)]).astype(np.float32)
    cmaskL = np.stack([t_ < s_, t_ > s_]).astype(np.float32)
    return dict(ropecos=np.tile(cs, (1, cfg.HALF // 32)), ropesin=np.tile(sn, (1, cfg.HALF // 32)),
                rrcos=np.tile(rc, (1, 2 * cfg.D_HEADS)), rrsin=np.tile(rs, (1, 2 * cfg.D_HEADS)),
                maskW=maskW, retT0=T0, retMge=Mge, retMle=Mle, ctri=ctri, cmaskP=cmaskP, cmaskL=cmaskL,
                ident=np.eye(128, dtype=np.float32), identb=np.eye(128).astype(ml_dtypes.bfloat16))


def build_mod(cfg, ncols):
    b = Builder(cfg)
    P, c = b.P, cfg
    NR = cfg.B + 1
    b.din("cvec", [NR, c.D]); b.din("wm", [c.DEPTH, c.D, ncols]); b.din("bm", [c.DEPTH, ncols]); b.din("ident", [128, 128])
    b.dout("mo", [c.DEPTH, NR, ncols])
    d = b.dram

    def stage():
        ident = P.sb("ident", [128, 128], F32)
        P.dma(ident[:], d["ident"][:, :], ["identd"], ["ident"])
        cv = P.sb("cv", [NR, c.D], F32)
        P.dma(cv[:], d["cvec"][:, :], ["cvec"], ["cv"])
        P.op("act", lambda e: e.activation(out=cv[:], in_=cv[:], func=AF.Silu), ["cv"], ["cv"])
        cT = P.sb("cT", [128, c.KC, NR], F32)
        pt = P.ps("pt", [128, 512])
        for kc in range(c.KC):
            P.tr(pt[:, 0:NR], cv[:, kc * 128:(kc + 1) * 128], ident[0:NR, 0:NR], ["cv", "ident"], ["pt"])
            b.copy("dve", cT[:, kc, :], pt[:, 0:NR], ["pt"], ["cT"])
        ws = [P.sb(f"ws{i}", [128, c.KC, 512], F32) for i in range(2)]
        po = [P.ps(f"po{i}", [NR, 512]) for i in range(2)]
        bb = P.sb("bb", [NR, c.DEPTH, ncols], F32)
        for l in range(c.DEPTH):
            P.dma(bb[:, l, :], d["bm"][l, :].partition_broadcast(NR), ["bm"], ["bb"])
        ob = [P.sb(f"ob{i}", [NR, 512], F32) for i in range(2)]
        it = 0
        for l in range(c.DEPTH):
            for c0 in range(0, ncols, 512):
                cw = min(512, ncols - c0)
                w, wk = ws[it % 2], ("ws", it % 2)
                p_, pk = po[it % 2], ("po", it % 2)
                o_, ok = ob[it % 2], ("ob", it % 2)
                it += 1
                for k0 in range(0, c.KC, 8):
                    P.dma(w[:, k0:k0 + 8, 0:cw], d["wm"][l, k0 * 128:(k0 + 8) * 128, c0:c0 + cw].rearrange(
                        "(k p) n -> p k n", p=128), ["wm"], [wk + (k0,)])
                for kc in range(c.KC):
                    P.mm(p_[:, 0:cw], cT[:, kc, :], w[:, kc, 0:cw], kc == 0, kc == c.KC - 1,
                         ["cT", wk + ((kc // 8) * 8,)], [pk])
                P.op("dve", lambda e, o_=o_, p_=p_, l=l, c0=c0, cw=cw: e.tensor_tensor(
                    out=o_[:, 0:cw], in0=p_[:, 0:cw], in1=bb[:, l, c0:c0 + cw], op=ALU.add), [pk, "bb"], [ok])
                P.dma(d["mo"][l, :, c0:c0 + cw], o_[:, 0:cw], [ok], [("mo", l, c0)])

    b.run_stage(stage)
    return b


def rwkv_decl(b, dbg=False):
    c = b.c
    scr = b.dout if dbg else b.dscr
    H2, A = c.HALF, c.A_HEADS
    b.din("rwkv_mu", [1, c.COLS_A]); b.din("rwkv_w0", [1, 2, H2]); b.din("rwkv_w2", [1, 2, 96, H2])
    b.din("rwkv_a0", [1, 2, H2]); b.din("rwkv_a2", [1, 2, 96, H2]); b.din("rwkv_g2", [1, 256, H2])
    b.din("rwkv_k_k", [1, H2]); b.din("rwkv_k_a", [1, H2]); b.din("rwkv_r_k", [1, A, 64])
    b.din("rwkv_ln_w", [1, H2]); b.din("rwkv_ln_b", [1, H2])
    scr("pa", [c.NTOK, c.COLS_A]); scr("Wd0", [c.NTOK, 2, 5, A // 2, 64]); scr("Wd1", [c.NTOK, 2, 5, A // 2, 64])
    scr("vTs", [128, A // 2, c.NTOK]); scr("ysc0", [128, A // 2, c.NTOK]); scr("ysc1", [128, A // 2, c.NTOK])
    scr("gate", [c.NTOK, H2]); scr("vtok", [c.NTOK, H2]); scr("bon", [c.NTOK, A])
    scr("Wt0", [c.NTOK, 5, H2]); scr("Wt1", [c.NTOK, 5, H2]); scr("ytk0", [c.NTOK, H2]); scr("ytk1", [c.NTOK, H2])
    b.din("ctri", [2, 64, 64]); b.din("cmaskP", [2, 64, 128]); b.din("cmaskL", [2, 64, 64])


RWKV_KEYS = ("rwkv_mu", "rwkv_w0", "rwkv_w2", "rwkv_a0", "rwkv_a2", "rwkv_g2", "rwkv_k_k", "rwkv_k_a", "rwkv_r_k",
             "rwkv_ln_w", "rwkv_ln_b")


def build_main(cfg):
    b = Builder(cfg)
    c = cfg
    b.din("x", [c.S, c.D]); b.din("ctx", [c.L, c.D]); b.dscr("mod", [c.DEPTH, 2, 6, c.D])
    b.din("cvec", [2, c.D]); b.din("w_mod", [c.DEPTH, c.D, 6 * c.D]); b.din("b_mod", [c.DEPTH, 6 * c.D])
    b.din("ident", [128, 128]); b.din("identb", [128, 128], BF16)
    b.din("ropecos", [c.S, c.HALF]); b.din("ropesin", [c.S, c.HALF])
    b.din("rrcos", [c.NTOK, 2 * c.D_HEADS * 64]); b.din("rrsin", [c.NTOK, 2 * c.D_HEADS * 64])
    b.din("maskW", [6, 128, 512], BF16); b.din("retT0", [128, 512])
    b.din("retMge", [4, 128, 512]); b.din("retMle", [4, 128, 512])
    b.din("w_in_even", [c.D, c.COLS_EVEN]); b.din("w_in_odd", [c.D, c.COLS_ODD])
    b.din("diff_lambda", [1, 4, 64]); b.din("diff_subln", [1, 128])
    b.din("swa_sink", [1, c.C_HEADS]); b.din("ret_decay", [1, 2, c.D_HEADS])
    rwkv_decl(b)
    b.din("w_mix_out", [c.DEPTH, c.D, c.D]); b.din("w_ffn_in", [c.DEPTH, c.D, 2 * c.DFF])
    b.din("ffn_conv_w", [c.DEPTH, 3, c.DFF]); b.din("ffn_conv_b", [c.DEPTH, c.DFF])
    b.din("w_ffn_out", [c.DEPTH, c.DFF, c.D]); b.din("final_norm", [c.D])
    b.dout("out", [c.S, c.D])
    b.dscr("h", [c.NTOK, c.D]); b.dscr("aT", [c.KC, 128, c.NTOK], BF16)
    b.dscr("p", [c.NTOK, max(c.COLS_EVEN, c.COLS_ODD)])
    b.dscr("yT", [c.KC, 128, c.NTOK], BF16); b.dscr("hidT", [c.NT, 128, c.FC, 128], BF16)
    b.dscr("dqkT", [2 * c.HALF // 128, 128, c.NTOK], BF16); b.dscr("dV", [c.NTOK, c.HALF], BF16)
    b.dscr("sqkT", [(c.HALF + c.C_KV * 64) // 128, 128, c.NTOK], BF16); b.dscr("sV", [c.NTOK, c.C_KV * 64], BF16)
    b.dscr("rqkT", [2 * c.D_HEADS, 128, c.NTOK], BF16); b.dscr("rV", [c.NTOK, c.HALF], BF16)
    b.dscr("rgT", [c.HALF // 128, 128, c.NTOK], BF16)
    rs = b.run_stage
    rs(b.st_mod)
    rs(b.st_init_h)
    for l in range(c.DEPTH):
        rs(b.st_norm, l, 0)
        rs(b.st_inproj, l)
        if l % 2 == 0:
            e = l // 2
            rs(b.st_rwkv_shift, e); rs(b.st_rwkv_prep2, e)
            rs(b.st_rwkv_chunk, 0); rs(b.st_rwkv_chunk, 1)
            rs(b.st_rwkv_post, e)
            rs(b.st_diff_prep); rs(b.st_diff_prepv); rs(b.st_diff_core, l, l // 2)
        else:
            rs(b.st_swa_prep); rs(b.st_swa_prepv); rs(b.st_swa_core, l // 2)
            rs(b.st_ret_prep); rs(b.st_ret_prepv); rs(b.st_ret_prepg); rs(b.st_ret_core, l // 2)
        rs(b.st_mixout, l)
        rs(b.st_norm, l, 1)
        rs(b.st_ffn_in, l)
        rs(b.st_ffn_out, l)
    rs(b.st_final)
    return b


def kernel(x, c, ctx, c_ctx, w_mod, b_mod, w_in_even, rwkv_mu, rwkv_w0, rwkv_w2, rwkv_a0, rwkv_a2,
           rwkv_g2, rwkv_k_k, rwkv_k_a, rwkv_r_k, rwkv_ln_w, rwkv_ln_b, diff_lambda, diff_subln,
           w_in_odd, swa_sink, ret_decay, w_mix_out, w_ffn_in, ffn_conv_w, ffn_conv_b, w_ffn_out,
           final_norm):
    f32 = lambda a: np.ascontiguousarray(np.asarray(a, dtype=np.float32))
    cfg = Cfg(D=x.shape[2], S=x.shape[1], L=ctx.shape[1], B=x.shape[0], DEPTH=w_mod.shape[0])
    hc = _consts(cfg)
    main = build_main(cfg)
    x = np.asarray(x)
    ctx = np.asarray(ctx)
    c = f32(c)
    c_ctx = f32(c_ctx)
    shared = dict(w_mod=f32(w_mod), b_mod=f32(b_mod), w_in_even=f32(w_in_even)[0], w_in_odd=f32(w_in_odd)[0], diff_lambda=f32(diff_lambda),
                  diff_subln=f32(diff_subln), swa_sink=f32(swa_sink), ret_decay=f32(ret_decay),
                  w_mix_out=f32(w_mix_out), w_ffn_in=f32(w_ffn_in), ffn_conv_w=f32(ffn_conv_w),
                  ffn_conv_b=f32(ffn_conv_b), w_ffn_out=f32(w_ffn_out), final_norm=f32(final_norm),
                  rwkv_mu=f32(rwkv_mu), rwkv_w0=f32(rwkv_w0), rwkv_w2=f32(rwkv_w2), rwkv_a0=f32(rwkv_a0),
                  rwkv_a2=f32(rwkv_a2), rwkv_g2=f32(rwkv_g2), rwkv_k_k=f32(rwkv_k_k), rwkv_k_a=f32(rwkv_k_a),
                  rwkv_r_k=f32(rwkv_r_k), rwkv_ln_w=f32(rwkv_ln_w), rwkv_ln_b=f32(rwkv_ln_b))
    for k in ("ident", "identb", "ropecos", "ropesin", "rrcos", "rrsin", "maskW", "retT0", "retMge", "retMle",
              "ctri", "cmaskP", "cmaskL"):
        shared[k] = hc[k]
    in2 = []
    for bi in range(cfg.B):
        dd = dict(shared)
        dd.update(x=f32(x[bi]), ctx=f32(ctx[bi]), cvec=np.ascontiguousarray(np.stack([c[bi], c_ctx], 0)))
        in2.append(dd)
    r2 = run_bass_kernel_spmd(main.nc, in2, core_ids=list(range(cfg.B)))
    return np.stack([r2.results[bi]["out"] for bi in range(cfg.B)], axis=0).astype(np.float32)
```
